# Optimizing a Trainium2 kernel written in Bass

```python
import jax, jax.numpy as jnp
from jax import lax
import numpy as np

D_MODEL = 1024
BATCH = 4
SEQ = 4096
DEPTH = 2
DEC_BATCH = 8
DEC_SEQ = 64
PAST_LEN = 2048

CHUNK = 64
N_META = 16
PADF = CHUNK - N_META
A_HEADS = 8
A_KV_HEADS = 2
HEAD_DIM = 64
WINDOW = 128
WIN_CHUNKS = WINDOW // CHUNK
B_WIDTH = 512
B_CONV = 3
C_WIDTH = 512
C_CONV = 31
D_HEADS = 4
D_KEY = 128
D_VAL = 128
D_WIDTH = D_HEADS * D_VAL
FFN_HIDDEN = -(-8 * D_MODEL // (3 * 256)) * 256
ALPHA = (2 * DEPTH) ** 0.25
BETA = (8 * DEPTH) ** -0.25
LN_EPS = 1e-5
RMS_EPS = 1e-6
NEG = -1e30

AB_SPLITS = (A_HEADS * HEAD_DIM, A_KV_HEADS * HEAD_DIM, A_KV_HEADS * HEAD_DIM, B_WIDTH, B_WIDTH, B_WIDTH)
AB_IN = sum(AB_SPLITS)
AB_OUT = A_HEADS * HEAD_DIM + B_WIDTH
CD_SPLITS = (C_WIDTH, C_WIDTH, D_HEADS * D_KEY, D_HEADS * D_KEY, D_WIDTH, D_WIDTH)
CD_IN = sum(CD_SPLITS)
CD_OUT = C_WIDTH + D_WIDTH

kernel_name = 'hybrid_stream_encoder_step'


def split_cols(y, sizes):
    idx = [int(s) for s in np.cumsum(sizes)[:-1]]
    return jnp.split(y, idx, axis=-1)


def layer_norm(x, g, b):
    xf = x.astype(jnp.float32)
    mu = xf.mean(-1, keepdims=True)
    var = jnp.mean(jnp.square(xf - mu), -1, keepdims=True)
    return ((xf - mu) * lax.rsqrt(var + LN_EPS) * g + b).astype(x.dtype)


def ffn(h, w_gu, w_down):
    gate, up = jnp.split(h @ w_gu, 2, axis=-1)
    return (jax.nn.silu(gate) * up) @ w_down


def residual_block(h, mix, g1, b1, g2, b2, w_gu, w_down):
    h = layer_norm(ALPHA * h + mix, g1, b1)
    return layer_norm(ALPHA * h + ffn(h, w_gu, w_down), g2, b2)


def causal_dwconv(u, hist, w):
    width, ch = w.shape
    full = jnp.concatenate([hist.astype(u.dtype), u], axis=1)
    y = lax.conv_general_dilated(full, w.astype(u.dtype)[:, None, :], (1,), 'VALID',
                                 dimension_numbers=('NWC', 'WIO', 'NWC'), feature_group_count=ch)
    return y, full[:, full.shape[1] - (width - 1):]


def alibi_slopes(n):
    return jnp.asarray(np.exp2(-8.0 * np.arange(1, n + 1, dtype=np.float32) / n), jnp.float32)


def sink_attention(q, k, v, qpos, kpos, kvalid, sinks):
    bn, nblk, nq, nh, hd = q.shape
    nkv = k.shape[3]
    grp = nh // nkv
    qg = q.reshape(bn, nblk, nq, nkv, grp, hd)
    s = jnp.einsum('bnqkgd,bnskd->bnkgqs', qg, k, preferred_element_type=jnp.float32) * (hd ** -0.5)
    dist = jnp.abs(qpos[:, :, None] - kpos[:, None, :]).astype(jnp.float32)
    slopes = alibi_slopes(nh).reshape(nkv, grp)
    s = s - slopes[None, None, :, :, None, None] * dist[None, :, None, None]
    s = jnp.where(kvalid[None, :, None, None, None, :], s, NEG)
    sink = sinks.astype(jnp.float32).reshape(nkv, grp)[None, None, :, :, None, None]
    m = jnp.maximum(s.max(-1, keepdims=True), sink)
    p = jnp.exp(s - m)
    p = (p / (p.sum(-1, keepdims=True) + jnp.exp(sink - m))).astype(v.dtype)
    o = jnp.einsum('bnkgqs,bnskd->bnqkgd', p, v)
    return o.reshape(bn, nblk, nq, nh * hd)


def window_attention_prompt(q, k, v, sinks):
    bn, length = q.shape[:2]
    lp = length + PADF
    nb = lp // CHUNK
    front = lambda a, n: jnp.pad(a, ((0, 0), (n, 0), (0, 0), (0, 0)))
    qb = front(q, PADF).reshape(bn, nb, CHUNK, A_HEADS, HEAD_DIM)
    kp = front(k, PADF + WIN_CHUNKS * CHUNK).reshape(bn, nb + WIN_CHUNKS, CHUNK, A_KV_HEADS, HEAD_DIM)
    vp = front(v, PADF + WIN_CHUNKS * CHUNK).reshape(bn, nb + WIN_CHUNKS, CHUNK, A_KV_HEADS, HEAD_DIM)
    kband = jnp.concatenate([kp[:, j:j + nb] for j in range(WIN_CHUNKS + 1)], axis=2)
    vband = jnp.concatenate([vp[:, j:j + nb] for j in range(WIN_CHUNKS + 1)], axis=2)
    blk = jnp.arange(nb)[:, None]
    qpos = blk * CHUNK + jnp.arange(CHUNK)[None] - PADF
    kpos = (blk - WIN_CHUNKS) * CHUNK + jnp.arange((WIN_CHUNKS + 1) * CHUNK)[None] - PADF
    o = sink_attention(qb, kband, vband, qpos, kpos, kpos >= 0, sinks)
    return o.reshape(bn, lp, A_HEADS * HEAD_DIM)[:, PADF:]


def window_attention_sample(q, k_new, v_new, cache_k, cache_v, sinks):
    bn, t = q.shape[:2]
    w = cache_k.shape[1]
    kall = jnp.concatenate([cache_k.astype(k_new.dtype), k_new], axis=1)
    vall = jnp.concatenate([cache_v.astype(v_new.dtype), v_new], axis=1)
    qpos = (w + jnp.arange(t))[None]
    kpos = jnp.arange(w + t)[None]
    o = sink_attention(q[:, None], kall[:, None], vall[:, None], qpos, kpos, jnp.ones(kpos.shape, bool), sinks)
    return o.reshape(bn, t, A_HEADS * HEAD_DIM), kall[:, -w:], vall[:, -w:]


def ab_project(x, w_in, b_in):
    q, k, v, bg, cg, hb = split_cols(x @ w_in + b_in, AB_SPLITS)
    bn, t = x.shape[:2]
    q = q.reshape(bn, t, A_HEADS, HEAD_DIM)
    k = k.reshape(bn, t, A_KV_HEADS, HEAD_DIM)
    v = v.reshape(bn, t, A_KV_HEADS, HEAD_DIM)
    return q, k, v, bg, cg * hb


def mixer_ab_prompt(x, w_in, b_in, sinks, conv_w, w_o, win):
    q, k, v, bg, u = ab_project(x, w_in, b_in)
    attn = window_attention_prompt(q, k, v, sinks)
    cb, conv_state = causal_dwconv(u, jnp.zeros((x.shape[0], B_CONV - 1, B_WIDTH), x.dtype), conv_w)
    out = jnp.concatenate([attn, bg * cb], axis=-1) @ w_o
    return out, k[:, -win:], v[:, -win:], conv_state


def mixer_ab_sample(x, cache_k, cache_v, conv_hist, w_in, b_in, sinks, conv_w, w_o):
    q, k, v, bg, u = ab_project(x, w_in, b_in)
    attn, new_k, new_v = window_attention_sample(q, k, v, cache_k, cache_v, sinks)
    cb, conv_state = causal_dwconv(u, conv_hist, conv_w)
    out = jnp.concatenate([attn, bg * cb], axis=-1) @ w_o
    return out, new_k, new_v, conv_state


def hgrn_lower_bound(lower_bounds, layer):
    p = jax.nn.softmax(lower_bounds.astype(jnp.float32), axis=0)
    return (jnp.cumsum(p, axis=0) - p[0])[layer]


def cd_project(x, w_in, b_in, lb):
    a, gc, q, f, i, g = split_cols(x @ w_in + b_in, CD_SPLITS)
    bn, t = x.shape[:2]
    lbk = lb.reshape(D_HEADS, D_KEY)
    forget = lbk + (1.0 - lbk) * jax.nn.sigmoid(f.reshape(bn, t, D_HEADS, D_KEY).astype(jnp.float32))
    q = q.reshape(bn, t, D_HEADS, D_KEY).astype(jnp.float32)
    v = i.reshape(bn, t, D_HEADS, D_VAL).astype(jnp.float32)
    return a * jax.nn.sigmoid(gc), q, 1.0 - forget, v, jnp.log(forget), g


def conformer_conv(u, hist, conv_w, conv_b, ln_g, ln_b):
    c, new_hist = causal_dwconv(u, hist, conv_w)
    return jax.nn.silu(layer_norm(c + conv_b, ln_g, ln_b)), new_hist


def hgrn_block(S, qb, kb, vb, lfb):
    t = qb.shape[1]
    cum = jnp.cumsum(lfb, axis=1)
    o_inter = jnp.einsum('bthk,bhkv->bthv', qb * jnp.exp(cum), S)
    tri = jnp.tril(jnp.ones((t, t), bool))
    diff = cum[:, :, None] - cum[:, None, :]
    decay = jnp.exp(jnp.where(tri[None, :, :, None, None], diff, -jnp.inf))
    att = jnp.einsum('bthk,bshk,btshk->bhts', qb, kb, decay)
    o_intra = jnp.einsum('bhts,bshv->bthv', att, vb)
    tot = cum[:, -1]
    k_dec = kb * jnp.exp(tot[:, None] - cum)
    S_new = jnp.exp(tot)[..., None] * S + jnp.einsum('bshk,bshv->bhkv', k_dec, vb)
    return S_new, o_inter + o_intra


def hgrn_prompt(q, k, v, lf):
    bn = q.shape[0]
    pad = ((0, 0), (PADF, 0), (0, 0), (0, 0))
    q, k, v, lf = [jnp.pad(a, pad) for a in (q, k, v, lf)]
    nb = q.shape[1] // CHUNK
    blocks = lambda a: a.reshape(bn, nb, CHUNK, *a.shape[2:]).swapaxes(0, 1)
    S0 = jnp.zeros((bn, D_HEADS, D_KEY, D_VAL), jnp.float32)
    S, o = lax.scan(lambda s, xs: hgrn_block(s, *xs), S0, (blocks(q), blocks(k), blocks(v), blocks(lf)))
    o = o.swapaxes(0, 1).reshape(bn, nb * CHUNK, D_HEADS, D_VAL)[:, PADF:]
    return S, o


def hgrn_readout(o, g, norm_g):
    bn, t = o.shape[:2]
    o = o * lax.rsqrt(jnp.mean(o * o, -1, keepdims=True) + RMS_EPS)
    return (o.reshape(bn, t, D_WIDTH) * norm_g * jax.nn.silu(g.astype(jnp.float32))).astype(g.dtype)


def mixer_cd_prompt(x, w_in, b_in, conv_w, conv_b, ln_g, ln_b, lb, norm_g, w_o):
    u, q, k, v, lf, g = cd_project(x, w_in, b_in, lb)
    yc, conv_state = conformer_conv(u, jnp.zeros((x.shape[0], C_CONV - 1, C_WIDTH), x.dtype), conv_w, conv_b, ln_g, ln_b)
    S, o = hgrn_prompt(q, k, v, lf)
    out = jnp.concatenate([yc, hgrn_readout(o, g, norm_g)], axis=-1) @ w_o
    return out, conv_state, S.astype(x.dtype)


def mixer_cd_sample(x, conv_hist, S, w_in, b_in, conv_w, conv_b, ln_g, ln_b, lb, norm_g, w_o):
    u, q, k, v, lf, g = cd_project(x, w_in, b_in, lb)
    yc, conv_state = conformer_conv(u, conv_hist, conv_w, conv_b, ln_g, ln_b)
    S_new, o = hgrn_block(S.astype(jnp.float32), q, k, v, lf)
    out = jnp.concatenate([yc, hgrn_readout(o, g, norm_g)], axis=-1) @ w_o
    return out, conv_state, S_new.astype(x.dtype)


def setup_inputs(seed: int = 0) -> dict:
    key = jax.random.key(seed)
    ks = jax.random.split(key, 28)
    nrm = lambda k, shape, scale: scale * jax.random.normal(k, shape, jnp.float32)
    win = min(WINDOW, PAST_LEN)
    return {
        'x_prompt': nrm(ks[0], (BATCH, SEQ, D_MODEL), 1.0),
        'x_sample': nrm(ks[1], (DEC_BATCH, DEC_SEQ, D_MODEL), 1.0),
        'cache_k_a': nrm(ks[2], (DEC_BATCH, win, A_KV_HEADS, HEAD_DIM), 1.0),
        'cache_v_a': nrm(ks[3], (DEC_BATCH, win, A_KV_HEADS, HEAD_DIM), 1.0),
        'state_conv_b': nrm(ks[4], (DEC_BATCH, B_CONV - 1, B_WIDTH), 1.0),
        'state_conv_c': nrm(ks[5], (DEC_BATCH, C_CONV - 1, C_WIDTH), 0.5),
        'state_hgrn': nrm(ks[6], (DEC_BATCH, D_HEADS, D_KEY, D_VAL), 0.5),
        'meta_tokens': nrm(ks[7], (N_META, D_MODEL), 1.0),
        'ab_w_in': nrm(ks[8], (D_MODEL, AB_IN), D_MODEL ** -0.5),
        'ab_b_in': nrm(ks[9], (AB_IN,), 0.01),
        'a_sinks': nrm(ks[10], (A_HEADS,), 0.5),
        'b_conv_w': nrm(ks[11], (B_CONV, B_WIDTH), B_CONV ** -0.5),
        'ab_w_o': nrm(ks[12], (AB_OUT, D_MODEL), BETA * AB_OUT ** -0.5),
        'cd_w_in': nrm(ks[13], (D_MODEL, CD_IN), D_MODEL ** -0.5),
        'cd_b_in': nrm(ks[14], (CD_IN,), 0.01),
        'c_conv_w': nrm(ks[15], (C_CONV, C_WIDTH), C_CONV ** -0.5),
        'c_conv_b': nrm(ks[16], (C_WIDTH,), 0.01),
        'c_ln_g': 1.0 + nrm(ks[17], (C_WIDTH,), 0.01),
        'c_ln_b': nrm(ks[18], (C_WIDTH,), 0.01),
        'd_lower_bounds': nrm(ks[19], (DEPTH, D_HEADS * D_KEY), 0.1),
        'd_norm_g': 1.0 + nrm(ks[20], (D_WIDTH,), 0.01),
        'cd_w_o': nrm(ks[21], (CD_OUT, D_MODEL), BETA * CD_OUT ** -0.5),
        'ln1_g': 1.0 + nrm(ks[22], (DEPTH, D_MODEL), 0.01),
        'ln1_b': nrm(ks[23], (DEPTH, D_MODEL), 0.01),
        'ln2_g': 1.0 + nrm(ks[24], (DEPTH, D_MODEL), 0.01),
        'ln2_b': nrm(ks[25], (DEPTH, D_MODEL), 0.01),
        'ffn_w_gu': nrm(ks[26], (DEPTH, D_MODEL, 2 * FFN_HIDDEN), D_MODEL ** -0.5),
        'ffn_w_down': nrm(ks[27], (DEPTH, FFN_HIDDEN, D_MODEL), BETA * FFN_HIDDEN ** -0.5),
    }


def reference(x_prompt, x_sample, cache_k_a, cache_v_a, state_conv_b, state_conv_c, state_hgrn,
              meta_tokens, ab_w_in, ab_b_in, a_sinks, b_conv_w, ab_w_o, cd_w_in, cd_b_in,
              c_conv_w, c_conv_b, c_ln_g, c_ln_b, d_lower_bounds, d_norm_g, cd_w_o,
              ln1_g, ln1_b, ln2_g, ln2_b, ffn_w_gu, ffn_w_down):
    win = cache_k_a.shape[1]
    bp = x_prompt.shape[0]
    meta = jnp.broadcast_to(meta_tokens.astype(x_prompt.dtype)[None], (bp, N_META, D_MODEL))
    hp = jnp.concatenate([meta, x_prompt], axis=1)
    hs = x_sample
    for l in range(DEPTH):
        if l % 2 == 0:
            mp, k_a_p, v_a_p, conv_b_p = mixer_ab_prompt(hp, ab_w_in, ab_b_in, a_sinks, b_conv_w, ab_w_o, win)
            ms, k_a_s, v_a_s, conv_b_s = mixer_ab_sample(hs, cache_k_a, cache_v_a, state_conv_b,
                                                         ab_w_in, ab_b_in, a_sinks, b_conv_w, ab_w_o)
        else:
            lb = hgrn_lower_bound(d_lower_bounds, l)
            mp, conv_c_p, hgrn_p = mixer_cd_prompt(hp, cd_w_in, cd_b_in, c_conv_w, c_conv_b, c_ln_g, c_ln_b,
                                                   lb, d_norm_g, cd_w_o)
            ms, conv_c_s, hgrn_s = mixer_cd_sample(hs, state_conv_c, state_hgrn, cd_w_in, cd_b_in, c_conv_w,
                                                   c_conv_b, c_ln_g, c_ln_b, lb, d_norm_g, cd_w_o)
        hp = residual_block(hp, mp, ln1_g[l], ln1_b[l], ln2_g[l], ln2_b[l], ffn_w_gu[l], ffn_w_down[l])
        hs = residual_block(hs, ms, ln1_g[l], ln1_b[l], ln2_g[l], ln2_b[l], ffn_w_gu[l], ffn_w_down[l])
    y_prompt = hp[:, N_META:]
    return (y_prompt, hs, k_a_p, v_a_p, conv_b_p, conv_c_p, hgrn_p, k_a_s, v_a_s, conv_b_s, conv_c_s, hgrn_s)
```

```python
import numpy as np
import concourse.bass as bass
import concourse.mybir as mybir

F32 = mybir.dt.float32
BF16 = mybir.dt.bfloat16
AF = mybir.ActivationFunctionType
ALU = mybir.AluOpType
AX = mybir.AxisListType

_DS = {F32: 4, BF16: 2}


def _rng(ap):
    t = ap.tensor
    ds = _DS.get(ap.dtype, 4)
    a = ap.ap
    off = int(ap.offset)
    sp = str(ap.space)
    if "PSUM" in sp.upper():
        return (t.name, 0, 1 << 30, 0, 128)
    if "DRAM" in sp.upper() or "HBM" in sp.upper():
        span = 1
        for st, n in a:
            span += abs(st) * (n - 1)
        return (t.name, off * ds, (off + span) * ds, 0, 1)
    pst, pn = a[0]
    if pst == 0:
        pst = 1 << 40
    if pn > 1 or True:
        tp = 1
        for s in list(t.shape)[1:]:
            tp *= s
        tds = _DS.get(t.dtype, 4)
        tp = tp * tds // ds
    plo = off // tp
    fo = off % tp
    span = 1
    for st, n in a[1:]:
        span += abs(st) * (n - 1)
    return (t.name, fo * ds, (fo + span) * ds, plo, plo + pn)


class Sched:
    ENG = ["pe", "act", "dve", "pool", "sp"]
    R = 12
    RQ = {'sp': 12, 'pool': 2, 'act': 4}

    def __init__(self):
        self.ops = {e: [] for e in self.ENG}
        self.count = {e: 0 for e in self.ENG}
        self.seen = {e: {} for e in self.ENG}
        self.recs = {}
        self.dma_idx = {"sp": 0, "pool": 0, "act": 0}
        self.all_dma = {}

    def _deps(self, ins, outs, eng=None):
        deps = {}

        def add(tk):
            if tk is None:
                return
            s, v = tk
            if deps.get(s, 0) < v:
                deps[s] = v

        for ap in ins:
            n, lo, hi, pl, ph = _rng(ap)
            for r in self.recs.get(n, ()):
                if r[0] < hi and lo < r[1] and r[2] < ph and pl < r[3]:
                    add(r[4])
                    if hi == 1 << 30:
                        for s, v in r[5].items():
                            if s != eng:
                                add((s, v))
        for ap in outs:
            n, lo, hi, pl, ph = _rng(ap)
            for r in self.recs.get(n, ()):
                if r[0] < hi and lo < r[1] and r[2] < ph and pl < r[3]:
                    add(r[4])
                    for s, v in r[5].items():
                        add((s, v))
        return deps

    def _split(self, n, lo, hi, pl, ph):
        lst = self.recs.get(n)
        if not lst:
            return
        out = []
        for r in lst:
            if r[0] < hi and lo < r[1] and pl <= r[2] and r[3] <= ph and (r[0] < lo or hi < r[1]):
                if r[0] < lo:
                    out.append([r[0], lo, r[2], r[3], r[4], dict(r[5])])
                out.append([max(r[0], lo), min(r[1], hi), r[2], r[3], r[4], dict(r[5])])
                if hi < r[1]:
                    out.append([hi, r[1], r[2], r[3], r[4], dict(r[5])])
            else:
                out.append(r)
        self.recs[n] = out

    def _update(self, ins, outs, tk):
        for ap in ins:
            n, lo, hi, pl, ph = _rng(ap)
            self._split(n, lo, hi, pl, ph)
            hit = False
            for r in self.recs.get(n, ()):
                if r[0] < hi and lo < r[1] and r[2] < ph and pl < r[3]:
                    if r[5].get(tk[0], 0) < tk[1]:
                        r[5][tk[0]] = tk[1]
                    hit = True
            if not hit:
                self.recs.setdefault(n, []).append([lo, hi, pl, ph, None, {tk[0]: tk[1]}])
        for ap in outs:
            n, lo, hi, pl, ph = _rng(ap)
            self._split(n, lo, hi, pl, ph)
            lst = self.recs.setdefault(n, [])
            lst[:] = [r for r in lst if not (lo <= r[0] and r[1] <= hi and pl <= r[2] and r[3] <= ph)]
            lst.append([lo, hi, pl, ph, tk, {}])

    def add(self, eng, fn, ins=(), outs=(), dma=False, signal=True):
        deps = self._deps(ins, outs, eng)
        if dma:
            i = self.dma_idx[eng]
            self.dma_idx[eng] = i + 1
            R = self.RQ[eng]
            k = i % R
            sem = f"{eng}_d{k}"
            val = 16 * (i // R + 1)
            if val > 16:
                if deps.get(sem, 0) < val - 16:
                    deps[sem] = val - 16
            tk = (sem, val)
            inc = (sem, 16)
            self.all_dma[sem] = val
        elif not signal:
            tk = (eng, self.count[eng] + 1)
            inc = None
        else:
            self.count[eng] += 1
            tk = (eng, self.count[eng])
            inc = (eng, 1)
        waits = []
        seen = self.seen[eng]
        for s, v in deps.items():
            if s == eng and eng == "pe":
                continue
            if seen.get(s, 0) >= v:
                continue
            seen[s] = v
            waits.append((s, v))
        self.ops[eng].append((waits, fn, inc))
        self._update(ins, outs, tk)
        return tk

    def finish(self):
        waits = []
        for s, v in self.all_dma.items():
            waits.append((s, v))
        for e in ["pe", "act", "dve", "pool"]:
            if self.count[e]:
                waits.append((e, self.count[e]))
        self.ops["sp"].append((waits, None, None))

    def sem_names(self):
        names = ["pe", "act", "dve", "pool"]
        for q in ("sp", "pool", "act"):
            n = min(self.dma_idx[q], self.RQ[q])
            names += [f"{q}_d{k}" for k in range(n)]
        return names

    def emit(self, nc):
        import contextlib
        names = self.sem_names()
        with contextlib.ExitStack() as st:
            sems = {n: st.enter_context(nc.semaphore(n)) for n in names}
            block = st.enter_context(nc.Block())

            def run(e, key):
                for waits, fn, inc in self.ops[key]:
                    for s, v in waits:
                        e.wait_ge(sems[s], v)
                    if fn is None:
                        continue
                    ins = fn(e)
                    if inc is not None:
                        ins.then_inc(sems[inc[0]], inc[1])

            @block.tensor
            def _(e):
                run(e, "pe")

            @block.scalar
            def _(e):
                run(e, "act")

            @block.vector
            def _(e):
                run(e, "dve")

            @block.gpsimd
            def _(e):
                run(e, "pool")

            @block.sync
            def _(e):
                run(e, "sp")

import contextlib

D = 1024
NH = 8
ALPHA = 4 ** 0.25
LN_EPS = 1e-5
RMS_EPS = 1e-6
NEG = -1e30
SLOT = 4096
NSLOT = 3

PO = {}
_o = 0
for _n, _w in [("b0", 18), ("wB", 12), ("ln", 64), ("b1", 20), ("wC", 124), ("ccb", 4), ("clng", 4),
               ("clnb", 4), ("lbin", 8), ("normg", 4), ("sink", 8)]:
    PO[_n] = _o
    _o += _w
NPAR = _o

WSPEC = {
    "w0in": (5, 4096), "w0o": (2, 4096), "gu0": (11, 4096), "dn0": (8, 2816),
    "w1in": (6, 4096), "w1o": (2, 4096), "gu1": (11, 4096), "dn1": (8, 2816),
}


class _Stop(Exception):
    pass


def build_program(nc, n_main_tiles=8, debug=False, stop_at=99):
    S = Sched()
    dr = {}

    def din(name, shape):
        dr[name] = nc.dram_tensor(name, shape, F32, kind="ExternalInput").ap()
        return dr[name]

    def dout(name, shape):
        dr[name] = nc.dram_tensor(name, shape, F32, kind="ExternalOutput").ap()
        return dr[name]

    xp = din("xp", [4096, D]); xs = din("xs", [64, D]); meta = din("meta", [16, D])
    ckd = din("ckd", [128, 256]); ck = din("ck", [128, 128]); cv = din("cv", [128, 128])
    scb = din("scb", [128, 8]); scc = din("scc", [128, 120]); shg = din("shg", [128, 512])
    par_d = din("par", [128, NPAR]); bkv_d = din("bkv", [128, 256]); bi_d = din("bi", [128, 512])
    ident_d = din("ident", [128, 128]); biasT_d = din("biasT", [3 * 128, 2048])
    matt_d = din("matt", [128, 256]); cmask_d = din("cmask", [128, 512])
    wd = {}
    ws = {}
    for n, (g, c) in WSPEC.items():
        wd[n] = din(n, [g * 128, c])
        ws[n] = nc.dram_tensor(n + "_s", [g * 128, c], BF16).ap()
    yp = dout("yp", [4096, D]); ys = dout("ys", [64, D])
    kp = dout("kp", [128, 128]); vp = dout("vp", [128, 128]); cbp = dout("cbp", [2, 512]); ccp = dout("ccp", [30, 512])
    hgp = dout("hgp", [4, 128, 128])
    kso = dout("ks", [128, 128]); vso = dout("vs", [128, 128]); cbs = dout("cbs", [2, 512]); ccs = dout("ccs", [30, 512])
    hgs = dout("hgs", [4, 128, 128])

    st = contextlib.ExitStack()
    cur = [0]

    def alloc(nbytes):
        o = cur[0]
        cur[0] += (nbytes + 63) // 64 * 64
        return o

    GB = 4 * 544 * 4
    offs = {}
    offs["U"] = alloc(5 * GB)
    for n, nb_ in [("h32", 16384), ("hbf", 8192), ("r32", 16384), ("tbf", 8192), ("mixbf", 8192), ("ST", 6144),
                   ("qT", 4096), ("kdT", 2 * 640 * 2), ("vtok", 5 * 128 * 2), ("kebf", 4096), ("kdtok", 4096),
                   ("vtok1", 4096), ("atbf", 2048), ("ucbf", 4 * 544 * 2), ("PA", 10240), ("S32", 2048),
                   ("biasg", 4096), ("biasx", 4096), ("biasx2", 4096), ("ident32", 512), ("identbf", 256), ("ones", 768),
                   ("matt", 1024), ("cmask", 2048), ("par", NPAR * 4), ("dg", 8 * 256), ("stg", 2048),
                   ("kvout", 1024), ("bkv", 1024), ("bi", 2048), ("small", 1024), ("uh0", 32), ("uh1", 480),
                   ("lbw", 256), ("ring", NSLOT * SLOT * 2)]:
        offs[n] = alloc(nb_)
    total = cur[0]
    assert total <= 212000, total
    print('SBUF total', total)
    arena = st.enter_context(nc.sbuf_tensor("arena", [128, total // 4], F32))

    def V(off, shape, dt=F32):
        n = 1
        for s_ in shape:
            n *= s_
        nb_ = n * (4 if dt == F32 else 2)
        a = arena[:, off // 4: off // 4 + nb_ // 4]
        if dt != F32:
            a = a.bitcast(dt)
        if len(shape) == 2:
            return a.rearrange("p (a b) -> p a b", a=shape[0])
        if len(shape) == 3:
            return a.rearrange("p (a b c) -> p a b c", a=shape[0], b=shape[1])
        return a

    G = [V(offs["U"] + i * GB, [4, 544]) for i in range(5)]
    actb = V(offs["U"], [22, 512], BF16)
    h32 = V(offs["h32"], [8, 512]); hbf = V(offs["hbf"], [8, 512], BF16)
    r32 = V(offs["r32"], [8, 512]); xstage = V(offs["r32"], [4, 1024])
    xin = V(offs["U"], [4, 1024])
    tbf = V(offs["tbf"], [8, 512], BF16); mixbf = V(offs["mixbf"], [8, 512], BF16)
    stt_ = [V(offs["ST"] + i * 2048, [512]) for i in range(3)]
    qT = V(offs["qT"], [4, 512], BF16)
    kdT = V(offs["kdT"], [2, 640], BF16); vtok = V(offs["vtok"], [5, 128], BF16)
    kebf = V(offs["kebf"], [4, 512], BF16); kdtok = V(offs["kdtok"], [4, 512], BF16)
    vtok1 = V(offs["vtok1"], [4, 512], BF16); atbf = V(offs["atbf"], [4, 4, 64], BF16)
    ucbf = V(offs["ucbf"], [4, 544], BF16)
    P32 = V(offs["PA"], [4, 256]); Pn32 = V(offs["PA"] + 4096, [4, 256]); PTb = V(offs["PA"] + 8192, [8, 128], BF16)
    Sbf = V(offs["PA"], [9, 4, 128], BF16)
    S32 = V(offs["S32"], [4, 128])
    biasg = V(offs["biasg"], [8, 256], BF16); biasx = V(offs["biasx"], [8, 256], BF16); biasx2 = V(offs["biasx2"], [8, 256], BF16)
    ident32 = V(offs["ident32"], [128]); identbf = V(offs["identbf"], [128], BF16)
    ones = V(offs["ones"], [3, 128], BF16)
    matt = V(offs["matt"], [4, 64]); cmask = V(offs["cmask"], [512])
    par = V(offs["par"], [NPAR])
    dg = V(offs["dg"], [8, 128], BF16)
    stg = V(offs["stg"], [512]); kvout = V(offs["kvout"], [256]); bkv = V(offs["bkv"], [256]); bi = V(offs["bi"], [512])
    small = V(offs["small"], [256])
    uh0 = V(offs["uh0"], [4, 2]); uh1 = V(offs["uh1"], [4, 30]); lbw = V(offs["lbw"], [64])
    ring = [V(offs["ring"] + i * SLOT * 2, [SLOT], BF16) for i in range(NSLOT)]
    PS = [st.enter_context(nc.psum_tensor(f"ps{i}", [128, 512], F32)) for i in range(8)]
    bank_i = [0]

    def nb():
        b = PS[bank_i[0] % 6]
        bank_i[0] += 1
        return b

    def pc(name, i=0, n=1):
        return par[:, PO[name] + i: PO[name] + i + n]

    def aps(*xs_):
        return [x for x in xs_ if x is not None and not isinstance(x, (int, float))]

    def mm(out, lhsT, rhs, start=True, stop=True, sig=None):
        S.add("pe", lambda e, o=out, l=lhsT, r=rhs, s=start, t=stop: e.matmul(o, lhsT=l, rhs=r, start=s, stop=t),
              ins=[lhsT, rhs], outs=[out], signal=(stop if sig is None else sig))

    def tr(out, in_, idn):
        S.add("pe", lambda e, o=out, i=in_, d=idn: e.transpose(o, i, d), ins=[in_, idn], outs=[out])

    def act(out, in_, func, bias=None, scale=None, accum=None):
        kw = {}
        if bias is not None:
            kw["bias"] = bias
        if scale is not None:
            kw["scale"] = scale
        if accum is not None:
            kw["accum_out"] = accum
        S.add("act", lambda e, o=out, i=in_, f=func, k=kw: e.activation(out=o, in_=i, func=f, **k),
              ins=aps(in_, bias, scale), outs=aps(out, accum))

    def ts(out, in0, s1, s2, op0, op1=None, eng="dve"):
        if op1 is None:
            S.add(eng, lambda e, o=out, i=in0, a=s1, p=op0: e.tensor_scalar(out=o, in0=i, scalar1=a, scalar2=None, op0=p),
                  ins=aps(in0, s1), outs=[out])
        else:
            S.add(eng, lambda e, o=out, i=in0, a=s1, b=s2, p=op0, q=op1: e.tensor_scalar(out=o, in0=i, scalar1=a, scalar2=b, op0=p, op1=q),
                  ins=aps(in0, s1, s2), outs=[out])

    def tt(out, in0, in1, op, eng="dve"):
        S.add(eng, lambda e, o=out, a=in0, b=in1, p=op: e.tensor_tensor(out=o, in0=a, in1=b, op=p), ins=[in0, in1], outs=[out])

    def stt(out, in0, scalar, in1, op0, op1):
        S.add("dve", lambda e, o=out, a=in0, s=scalar, b=in1, p=op0, q=op1: e.scalar_tensor_tensor(out=o, in0=a, scalar=s, in1=b, op0=p, op1=q),
              ins=aps(in0, scalar, in1), outs=[out])

    def cp(out, in_, eng="dve"):
        if eng == "act":
            act(out, in_, AF.Identity)
        else:
            S.add(eng, lambda e, o=out, i=in_: e.tensor_copy(out=o, in_=i), ins=[in_], outs=[out])

    def ms(ap, val, eng="dve"):
        S.add(eng, lambda e, a=ap, v=val: e.memset(a, v), outs=[ap])

    def dma(eng, out, in_):
        S.add(eng, lambda e, o=out, i=in_: e.dma_start(out=o, in_=i), ins=[in_], outs=[out], dma=True)

    def red(out, in_, op):
        S.add("dve", lambda e, o=out, i=in_, p=op: e.tensor_reduce(out=o, in_=i, axis=AX.X, op=p), ins=[in_], outs=[out])

    def recip(out, in_):
        S.add("dve", lambda e, o=out, i=in_: e.reciprocal(out=o, in_=i), ins=[in_], outs=[out])

    ms(arena[:, 0:total // 8], 0.0, "dve")
    ms(arena[:, total // 8: total // 4], 0.0, "pool")
    dma("sp", par, par_d); dma("sp", ident32, ident_d); dma("sp", bkv, bkv_d); dma("sp", bi, bi_d)
    dma("sp", matt.rearrange("p a b -> p (a b)"), matt_d); dma("sp", cmask, cmask_d)
    dma("pool", biasg.rearrange("p a b -> p (a b)"), biasT_d[0:128, :])
    dma("pool", biasx.rearrange("p a b -> p (a b)"), biasT_d[128:256, :])
    dma("pool", biasx2.rearrange("p a b -> p (a b)"), biasT_d[256:384, :])
    dma("pool", vtok[:, 0, :], cv)
    cp(identbf, ident32)
    ms(ones[:, 0, :], 1.0 / 1024); ms(ones[:, 1, :], 1.0 / 512); ms(ones[:, 2, :], 1.0 / 128)
    l0 = par[:, PO["lbin"]: PO["lbin"] + 4]; l1 = par[:, PO["lbin"] + 4: PO["lbin"] + 8]
    e0 = lbw[:, 0:4]; e1 = lbw[:, 4:8]; sm = lbw[:, 8:12]; p0 = lbw[:, 12:16]; p1 = lbw[:, 16:20]
    lb = lbw[:, 20:24]; oml = lbw[:, 24:28]; mxl = lbw[:, 28:32]
    tt(mxl, l0, l1, ALU.max)
    tt(e0, l0, mxl, ALU.subtract); tt(e1, l1, mxl, ALU.subtract)
    act(e0, e0, AF.Exp); act(e1, e1, AF.Exp)
    tt(sm, e0, e1, ALU.add); recip(sm, sm)
    tt(p0, e0, sm, ALU.mult); tt(p1, e1, sm, ALU.mult)
    tt(lb, p0, p1, ALU.add); tt(lb, lb, p0, ALU.subtract)
    ts(oml, lb, -1.0, 1.0, ALU.mult, ALU.add)
    cast_seq = []
    for n in ["w0in", "w0o", "gu0", "dn0"]:
        cast_seq += [(n, gi) for gi in range(WSPEC[n][0])]
    cast_seq += [("w1in", gi) for gi in (1, 0, 2, 3, 4, 5)]
    for n in ["w1o", "gu1", "dn1"]:
        cast_seq += [(n, gi) for gi in range(WSPEC[n][0])]
    cast_pos = {c: i for i, c in enumerate(cast_seq)}
    cast_done = [0]

    def ensure_cast(upto):
        while cast_done[0] <= min(upto, len(cast_seq) - 1):
            n, gi = cast_seq[cast_done[0]]
            dma("pool", ws[n][gi * 128:(gi + 1) * 128, :], wd[n][gi * 128:(gi + 1) * 128, :])
            cast_done[0] += 1

    ensure_cast(len(cast_seq))
    ring_i = [0]

    def wl(name, gi):
        g, c = WSPEC[name]
        k = ring_i[0] % NSLOT
        ring_i[0] += 1
        dma("sp", ring[k][:, 0:c], ws[name][gi * 128:(gi + 1) * 128, :])
        return ring[k]

    def ln_block(nch, src, onesrow, gname, bname, dst32, dstbf, sqbuf, T, silu_out=None, pre=False):
        bS = PS[6]; bQ = PS[7]
        if not pre:
            h = nch // 2
            cp(tbf[:, 0:h, 0:T], src[:, 0:h, 0:T], "dve")
            cp(tbf[:, h:nch, 0:T], src[:, h:nch, 0:T], "dve")
            act(sqbuf[:, 0:h, 0:T], src[:, 0:h, 0:T], AF.Square)
            act(sqbuf[:, h:nch, 0:T], src[:, h:nch, 0:T], AF.Square)
            for k in range(nch):
                mm(bS[:, 0:T], ones[:, onesrow, :], tbf[:, k, 0:T], k == 0, k == nch - 1)
            for k in range(nch):
                mm(bQ[:, 0:T], ones[:, onesrow, :], sqbuf[:, k, 0:T], k == 0, k == nch - 1)
        mean = stt_[0][:, 0:T]; msq = stt_[1][:, 0:T]; rstd = stt_[2][:, 0:T]
        cp(mean, bS[:, 0:T], "dve")
        act(msq, bS[:, 0:T], AF.Square)
        tt(rstd, bQ[:, 0:T], msq, ALU.subtract)
        ts(rstd, rstd, 0.0, None, ALU.max)
        act(rstd, rstd, AF.Ln, bias=small[:, 200:201])
        act(rstd, rstd, AF.Exp, scale=-0.5)
        npool = 2 if nch == 8 else 0
        for k in range(nch - npool, nch):
            tt(src[:, k, 0:T], src[:, k, 0:T], mean, ALU.subtract, "pool")
            tt(src[:, k, 0:T], src[:, k, 0:T], rstd, ALU.mult, "pool")
        for k in range(nch):
            if k < nch - npool:
                tt(src[:, k, 0:T], src[:, k, 0:T], mean, ALU.subtract)
                tt(src[:, k, 0:T], src[:, k, 0:T], rstd, ALU.mult)
            if silu_out is not None:
                act(silu_out[:, k, 0:T], src[:, k, 0:T], AF.Silu, bias=bname(k), scale=gname(k))
            else:
                act(dstbf[:, k, 0:T], src[:, k, 0:T], AF.Identity, bias=bname(k), scale=gname(k))
        if silu_out is None:
            for k in range(nch):
                act(dst32[:, k, 0:T], src[:, k, 0:T], AF.Identity, bias=bname(k), scale=gname(k))

    def evac_stats(m, bank, T, pend):
        stt(r32[:, m, 0:T], h32[:, m, 0:T], ALPHA, bank[:, 0:T], ALU.mult, ALU.add)
        cp(tbf[:, m, 0:T], r32[:, m, 0:T], "dve")
        act(hbf[:, m, 0:T], r32[:, m, 0:T], AF.Square)
        if pend is not None:
            stat_mm(pend, T)
        return m

    def stat_mm(m, T):
        mm(PS[6][:, 0:T], ones[:, 0, :], tbf[:, m, 0:T], m == 0, m == 7, sig=True)
        mm(PS[7][:, 0:T], ones[:, 0, :], hbf[:, m, 0:T], m == 0, m == 7, sig=True)

    ms(small[:, 200:201], LN_EPS)
    ms(small[:, 201:202], RMS_EPS)

    def ffn_block(layer, T):
        gu = f"gu{layer}"; dn = f"dn{layer}"
        for j in range(22):
            if j % 2 == 0:
                w = wl(gu, j // 2)
                wv = w[:, 0:4096].rearrange("p (k n) -> p k n", k=8)
            off = (j % 2) * 256
            bG = nb(); bU = nb()
            for k in range(8):
                mm(bG[:, 0:T], wv[:, k, off:off + 128], hbf[:, k, 0:T], k == 0, k == 7)
            for k in range(8):
                mm(bU[:, 0:T], wv[:, k, off + 128:off + 256], hbf[:, k, 0:T], k == 0, k == 7)
            sil = stt_[j % 2][:, 0:T]
            act(sil, bG[:, 0:T], AF.Silu)
            tt(actb[:, j, 0:T], sil, bU[:, 0:T], ALU.mult)
        pend = None
        for m in range(8):
            w = wl(dn, m)
            wv = w[:, 0:2816].rearrange("p (k n) -> p k n", k=22)
            b = nb()
            for k in range(22):
                mm(b[:, 0:T], wv[:, k, :], actb[:, k, 0:T], k == 0, k == 21)
            pend = evac_stats(m, b, T, pend)
        stat_mm(pend, T)
        lo = PO["ln"] + layer * 32
        ln_block(8, r32, 0, lambda k: par[:, lo + 16 + k: lo + 17 + k], lambda k: par[:, lo + 24 + k: lo + 25 + k],
                 h32, hbf, mixbf, T, pre=True)

    def wo_block(name, layer, T):
        pend = None
        for m in range(8):
            if m % 4 == 0:
                w = wl(name, m // 4)
                wv = w[:, 0:4096].rearrange("p (k n) -> p k n", k=8)
            off = (m % 4) * 128
            b = nb()
            for k in range(8):
                mm(b[:, 0:T], wv[:, k, off:off + 128], mixbf[:, k, 0:T], k == 0, k == 7)
            pend = evac_stats(m, b, T, pend)
        stat_mm(pend, T)
        lo = PO["ln"] + layer * 32
        ln_block(8, r32, 0, lambda k: par[:, lo + k: lo + 1 + k], lambda k: par[:, lo + 8 + k: lo + 9 + k],
                 h32, hbf, mixbf, T, pre=True)

    def state_rows_out(src, c0, dst, r0, nrows):
        b = nb()
        for c in range(4):
            tr(b[0:32, c * 128:(c + 1) * 128], src[:, c, c0:c0 + 32], ident32[:, :])
        cp(stg[0:32, :], b[0:32, :], "dve")
        dma("pool", dst, stg[r0:r0 + nrows, :])

    def load_x(kind, ti):
        if kind == "sample":
            dma("pool", xin[0:64, 0, :], xs)
        elif kind == "prefix":
            ms(xin[0:64, 0, :], 0.0)
            dma("pool", xin[48:64, 0, :], meta)
        else:
            dma("pool", xin[:, :, :], xp[ti * 512:(ti + 1) * 512, :].rearrange("(s p) d -> p s d", p=128))

    def tile_pass(kind, ti, pre_loaded=False, nxt=None):
        T = 512 if kind == "main" else 64
        NS = max(1, T // 128)
        Pt = min(T, 128)
        last = (kind == "main" and ti == n_main_tiles - 1)
        if not pre_loaded:
            load_x(kind, ti)
        for c in range(8):
            b = nb()
            for s in range(NS):
                tr(b[:, s * 128:s * 128 + Pt], xin[0:Pt, s, c * 128:(c + 1) * 128], ident32[0:Pt, 0:Pt])
            cp(h32[:, c, 0:T], b[:, 0:T], "dve")
            cp(hbf[:, c, 0:T], b[:, 0:T], "act")
        stage(2)
        bg, u32, cc32 = G[0], G[1], G[2]
        if kind == "sample":
            dma("sp", uh0.rearrange("p a b -> p (a b)"), scb)
            dma("sp", r32[:, 0, 0:256], ckd)
            for g in range(2):
                b = nb()
                tr(b[:, 0:128], r32[:, 0, g * 128:(g + 1) * 128], ident32[:, :])
                cp(kdT[:, g, 0:128], b[:, 0:128], "act")
        elif kind == "prefix":
            ms(uh0, 0.0)
        cp(u32[:, :, 0:2], uh0)
        stage(2.1)
        wg = {}

        def w0(gi):
            if gi not in wg:
                wg.clear()
                wg[gi] = wl("w0in", gi)[:, 0:4096].rearrange("p (k n) -> p k n", k=8)
            return wg[gi]

        for m in range(18):
            wv = w0(m // 4); off = (m % 4) * 128
            b = nb()
            for k in range(8):
                mm(b[:, 0:T], wv[:, k, off:off + 128], hbf[:, k, 0:T], k == 0, k == 7)
            bc = pc("b0", m)
            if m < 4:
                ts(qT[:, m, 0:T], b[:, 0:T], bc, 0.125, ALU.add, ALU.mult)
            elif m < 6:
                act(kdT[:, m - 4, 128:128 + T], b[:, 0:T], AF.Identity, bias=bc)
            elif m < 10:
                act(bg[:, m - 6, 0:T], b[:, 0:T], AF.Identity, bias=bc)
            elif m < 14:
                act(u32[:, m - 10, 2:2 + T], b[:, 0:T], AF.Identity, bias=bc)
            else:
                stt(u32[:, m - 14, 2:2 + T], b[:, 0:T], bc, u32[:, m - 14, 2:2 + T], ALU.add, ALU.mult)
        stage(2.3)
        wv = w0(4)
        for s in range(NS):
            b = nb()
            for k in range(8):
                mm(b[0:Pt, 0:256], hbf[:, k, s * 128:s * 128 + Pt], wv[:, k, 256:512], k == 0, k == 7)
            tt(vtok[0:Pt, 1 + s, :], b[0:Pt, 128:256], bkv[0:Pt, 128:256], ALU.add)
            if last and s == NS - 1 or kind == "sample":
                tt(kvout[0:Pt, :], b[0:Pt, 0:256], bkv[0:Pt, :], ALU.add)
        if kind == "prefix":
            b = nb()
            for k in range(8):
                mm(b[64:128, 0:256], hbf[:, k, 0:64], wv[:, k, 256:512], k == 0, k == 7)
            tt(vtok[64:128, 0, :], b[64:128, 128:256], bkv[64:128, 128:256], ALU.add)
            ms(u32[:, :, 0:50], 0.0)
        stage(2.5)
        if kind == "sample":
            dma("sp", stg[0:64, 0:128], ck[64:128, :]); dma("sp", stg[0:64, 128:256], cv[64:128, :])
            dma("pool", kso[0:64, :], stg[0:64, 0:128]); dma("pool", vso[0:64, :], stg[0:64, 128:256])
            dma("pool", kso[64:128, :], kvout[0:64, 0:128]); dma("pool", vso[64:128, :], kvout[0:64, 128:256])
        stage(2.7)
        if last:
            dma("pool", kp, kvout[:, 0:128]); dma("pool", vp, kvout[:, 128:256])
        stage(3)
        nq = NS

        def s_phase(j, g):
            bt = biasx if kind == "prefix" else (biasx2 if (kind == "main" and ti == 0 and j == 0) else biasg)
            ui = (j * 2 + g) % 2
            banks = [PS[2 * ui], PS[2 * ui + 1]]
            for hh in range(4):
                h = 4 * g + hh; c = h // 2; hf = h % 2
                o = banks[hh // 2][0:Pt, (hh % 2) * 256:(hh % 2) * 256 + 256]
                mm(o, qT[64 * hf:64 * hf + 64, c, j * 128:j * 128 + Pt], kdT[64 * hf:64 * hf + 64, g, 128 * j:128 * j + 256], True, False)
                mm(o, identbf[:, 0:Pt], bt[:, h, :], False, True)
            return banks

        def rest_phase(j, g, banks):
            mx = small[0:Pt, 0:4]; mneg = small[0:Pt, 4:8]; ssum = small[0:Pt, 8:12]; esk = small[0:Pt, 12:16]
            for q in range(2):
                red(mx[:, 2 * q:2 * q + 2], banks[q][0:Pt, :].rearrange("p (h k) -> p h k", h=2), ALU.max)
            sk = par[0:Pt, PO["sink"] + 4 * g: PO["sink"] + 4 * g + 4]
            tt(mx, mx, sk, ALU.max)
            ts(mneg, mx, -1.0, None, ALU.mult)
            for hh in range(4):
                act(P32[0:Pt, hh, :], banks[hh // 2][0:Pt, (hh % 2) * 256:(hh % 2) * 256 + 256], AF.Exp,
                    bias=mneg[:, hh:hh + 1], accum=ssum[:, hh:hh + 1])
            tt(esk, sk, mx, ALU.subtract)
            act(esk, esk, AF.Exp)
            tt(ssum, ssum, esk, ALU.add)
            recip(ssum, ssum)
            for hh in range(4):
                ts(Pn32[0:Pt, hh, :], P32[0:Pt, hh, :], ssum[:, hh:hh + 1], None, ALU.mult)
            tb = [PS[4], PS[5]]
            for hh in range(4):
                for kb in range(2):
                    idx = hh * 2 + kb
                    tr(tb[idx // 4][:, (idx % 4) * 128:(idx % 4) * 128 + Pt], Pn32[0:Pt, hh, kb * 128:(kb + 1) * 128], ident32[0:Pt, 0:Pt])
            for q in range(2):
                src = tb[q][:, :].rearrange("p (a b) -> p a b", a=4)[:, :, 0:Pt]
                cp(PTb[:, 4 * q:4 * q + 4, 0:Pt], src, "act" if q else "dve")
            for pr in range(2):
                ob = PS[6 + pr]
                for hf in range(2):
                    hh = pr * 2 + hf
                    for kb in range(2):
                        mm(ob[64 * hf:64 * hf + 64, 0:Pt], vtok[:, j + kb, g * 64:(g + 1) * 64], PTb[:, hh * 2 + kb, 0:Pt], kb == 0, kb == 1)
                cp(mixbf[:, 2 * g + pr, j * 128:j * 128 + Pt], ob[:, 0:Pt], "act")

        units = [(j, g) for j in range(nq) for g in range(2)]
        prev = None
        for u in units:
            bk = s_phase(*u)
            if prev is not None:
                rest_phase(*prev)
            prev = (u[0], u[1], bk)
        rest_phase(*prev)
        if kind == "prefix":
            cp(kdT[:, :, 64:128], kdT[:, :, 128:192])
        elif kind == "main":
            cp(kdT[:, :, 0:128], kdT[:, :, 512:640])
            cp(vtok[:, 0, :], vtok[:, 4, :], "dve")
        stage(4)
        for c in range(4):
            ts(cc32[:, c, 0:T], u32[:, c, 0:T], pc("wB", c * 3), None, ALU.mult)
            stt(cc32[:, c, 0:T], u32[:, c, 1:1 + T], pc("wB", c * 3 + 1), cc32[:, c, 0:T], ALU.mult, ALU.add)
            stt(cc32[:, c, 0:T], u32[:, c, 2:2 + T], pc("wB", c * 3 + 2), cc32[:, c, 0:T], ALU.mult, ALU.add)
            tt(mixbf[:, 4 + c, 0:T], bg[:, c, 0:T], cc32[:, c, 0:T], ALU.mult)
        cp(uh0, u32[:, :, T:T + 2])
        if kind == "sample":
            state_rows_out(u32, T + 2 - 32, cbs, 30, 2)
        if last:
            state_rows_out(u32, T + 2 - 32, cbp, 30, 2)
        stage(5)
        wo_block("w0o", 0, T)
        stage(6)
        ffn_block(0, T)
        stage(7)
        uc32, sg32, c32, q32, gate32 = G[0], G[1], G[2], G[3], G[4]
        if kind == "sample":
            dma("sp", uh1.rearrange("p a b -> p (a b)"), scc)
            dma("sp", S32.rearrange("p a b -> p (a b)"), shg)
        elif kind == "prefix":
            ms(uh1, 0.0)
            ms(S32, 0.0)
        cp(uc32[:, :, 0:30], uh1)
        order = [4, 5, 6, 7, 0, 1, 2, 3] + list(range(8, 20))
        wg1 = {}

        def w1(gi):
            if gi not in wg1:
                wg1.clear()
                wg1[gi] = wl("w1in", gi)[:, 0:4096].rearrange("p (k n) -> p k n", k=8)
            return wg1[gi]

        for m in order:
            wv = w1(m // 4); off = (m % 4) * 128; c = m % 4
            b = nb()
            for k in range(8):
                mm(b[:, 0:T], wv[:, k, off:off + 128], hbf[:, k, 0:T], k == 0, k == 7)
            bc = pc("b1", m)
            if m < 4:
                stt(uc32[:, c, 30:30 + T], b[:, 0:T], bc, c32[:, c, 0:T], ALU.add, ALU.mult)
            elif m < 8:
                act(c32[:, c, 0:T], b[:, 0:T], AF.Sigmoid, bias=bc)
            elif m < 12:
                act(q32[:, c, 0:T], b[:, 0:T], AF.Identity, bias=bc)
            elif m < 16:
                act(sg32[:, c, 0:T], b[:, 0:T], AF.Sigmoid, bias=bc)
            else:
                act(gate32[:, c, 0:T], b[:, 0:T], AF.Silu, bias=bc)
                ts(gate32[:, c, 0:T], gate32[:, c, 0:T], pc("normg", c), None, ALU.mult)
        wv = w1(5)
        for s in range(NS):
            b = nb()
            for k in range(8):
                mm(b[0:Pt, 0:512], hbf[:, k, s * 128:s * 128 + Pt], wv[:, k, 0:512], k == 0, k == 7)
            tt(vtok1[0:Pt, s, :], b[0:Pt, 0:512], bi[0:Pt, :], ALU.add)
        if kind == "prefix":
            ms(uc32[:, :, 0:78], 0.0)
        stage(8)
        cp(ucbf[:, :, 0:30 + T], uc32[:, :, 0:30 + T], "act")
        cp(uh1, uc32[:, :, T:T + 30])
        if kind == "sample":
            state_rows_out(uc32, T + 30 - 32, ccs, 2, 30)
        if last:
            state_rows_out(uc32, T + 30 - 32, ccp, 2, 30)
        di = [0]
        for c in range(4):
            b = nb()
            for j in range(31):
                d = dg[:, di[0] % 8, :]
                di[0] += 1
                if j % 2:
                    act(d, identbf, AF.Identity, scale=pc("wC", c * 31 + j))
                else:
                    ts(d, identbf, pc("wC", c * 31 + j), None, ALU.mult)
                mm(b[:, 0:T], d, ucbf[:, c, j:j + T], j == 0, j == 30, sig=True)
            act(c32[:, c, 0:T], b[:, 0:T], AF.Identity, bias=pc("ccb", c))
        ln_block(4, c32, 1, lambda k: pc("clng", k), lambda k: pc("clnb", k), None, None, mixbf[:, 4:8, :], T,
                 silu_out=mixbf)
        stage(9)
        lf32 = G[0]; cum32 = G[2]
        for c in range(4):
            ts(sg32[:, c, 0:T], sg32[:, c, 0:T], oml[:, c:c + 1], lb[:, c:c + 1], ALU.mult, ALU.add)
        act(lf32[:, :, 0:T], sg32[:, :, 0:T], AF.Ln)
        ts(sg32[:, :, 0:T], sg32[:, :, 0:T], -1.0, 1.0, ALU.mult, ALU.add)
        for c in range(4):
            S.add("dve", lambda e, o=cum32[:, c, 0:T], d0=cmask[:, 0:T], d1=lf32[:, c, 0:T]:
                  e.tensor_tensor_scan(out=o, data0=d0, data1=d1, initial=0.0, op0=ALU.mult, op1=ALU.add),
                  ins=[cmask[:, 0:T], lf32[:, c, 0:T]], outs=[cum32[:, c, 0:T]])
        NCH = T // 64
        etot = small[:, 32:32 + 4 * 8].rearrange("p (a b) -> p a b", a=4)
        act(etot[:, :, 0:NCH], cum32[:, :, 63:T:64], AF.Exp)
        act(lf32[:, :, 0:T], cum32[:, :, 0:T], AF.Exp)
        tt(qT[:, :, 0:T], q32[:, :, 0:T], lf32[:, :, 0:T], ALU.mult)
        act(lf32[:, :, 0:T], cum32[:, :, 0:T], AF.Exp, scale=-1.0)
        tt(sg32[:, :, 0:T], sg32[:, :, 0:T], lf32[:, :, 0:T], ALU.mult)
        kd32 = G[0]
        for c in range(4):
            tt(kd32[:, c, 0:T].rearrange("p (a b) -> p a b", b=64), sg32[:, c, 0:T].rearrange("p (a b) -> p a b", b=64),
               etot[:, c, 0:NCH].unsqueeze(2).to_broadcast([128, NCH, 64]), ALU.mult)
        cp(kebf[:, :, 0:T], sg32[:, :, 0:T], "act")
        if kind == "prefix":
            ms(kebf[:, :, 0:48], 0.0)
            ms(kd32[:, :, 0:48], 0.0)
        for s in range(NS):
            b = nb()
            for c in range(4):
                tr(b[0:Pt, c * 128:(c + 1) * 128], kd32[:, c, s * 128:s * 128 + Pt], ident32[:, :])
            cp(kdtok[0:Pt, s, :], b[0:Pt, :], "act")
        for c in range(4):
            b = nb()
            for ch in range(NCH):
                po = (ch % 2) * 64; s = ch // 2
                mm(b[po:po + 64, s * 64:(s + 1) * 64], kebf[:, c, ch * 64:(ch + 1) * 64], qT[:, c, ch * 64:(ch + 1) * 64])
            if NCH == 1:
                tt(atbf[0:64, c, 0, :], b[0:64, 0:64], matt[0:64, 0, :], ALU.mult)
            else:
                tt(atbf[:, c, :, :], b[:, 0:256].rearrange("p (a b) -> p a b", a=4), matt[:, :, :], ALU.mult)
        cp(Sbf[:, 0, :, :], S32, "act")
        for ch in range(NCH):
            po = (ch % 2) * 64; s = ch // 2
            b = nb()
            for c in range(4):
                mm(b[:, c * 128:(c + 1) * 128], kdtok[po:po + 64, s, c * 128:(c + 1) * 128], vtok1[po:po + 64, s, c * 128:(c + 1) * 128])
            for c in range(4):
                stt(S32[:, c, :], S32[:, c, :], etot[:, c, ch:ch + 1], b[:, c * 128:(c + 1) * 128], ALU.mult, ALU.add)
            cp(Sbf[:, ch + 1, :, :], S32, "act")
        if kind == "sample":
            dma("pool", hgs.rearrange("h k v -> k h v"), S32)
        if last:
            dma("pool", hgp.rearrange("h k v -> k h v"), S32)
        o32 = G[3]
        for c in range(4):
            b = nb()
            for ch in range(NCH):
                po = (ch % 2) * 64; s = ch // 2
                o = b[:, ch * 64:(ch + 1) * 64]
                mm(o, Sbf[:, ch, c, :], qT[:, c, ch * 64:(ch + 1) * 64], True, False)
                mm(o, vtok1[po:po + 64, s, c * 128:(c + 1) * 128], atbf[po:po + 64, c, s, :], False, True)
            cp(o32[:, c, 0:T], b[:, 0:T], "dve")
        act(tbf[:, 0:4, 0:T], o32[:, :, 0:T], AF.Square)
        for c in range(4):
            b = nb()
            mm(b[:, 0:T], ones[:, 2, :], tbf[:, c, 0:T])
            rs = stt_[c % 3][:, 0:T]
            act(rs, b[:, 0:T], AF.Ln, bias=small[:, 201:202])
            act(rs, rs, AF.Exp, scale=-0.5)
            tt(o32[:, c, 0:T], o32[:, c, 0:T], rs, ALU.mult)
            tt(mixbf[:, 4 + c, 0:T], o32[:, c, 0:T], gate32[:, c, 0:T], ALU.mult)
        stage(10)
        wo_block("w1o", 1, T)
        ffn_block(1, T)
        if nxt is not None:
            load_x(*nxt)
        stage(11)
        if kind != "prefix":
            for s in range(NS):
                for q in range(2):
                    b = nb()
                    for cc in range(4):
                        c = q * 4 + cc
                        tr(b[0:Pt, cc * 128:(cc + 1) * 128], h32[:, c, s * 128:s * 128 + Pt], ident32[:, :])
                    cp(xstage[0:Pt, s, q * 512:(q + 1) * 512], b[0:Pt, :], "act" if q else "dve")
            if kind == "sample":
                dma("pool", ys, xstage[0:64, 0, :])
            else:
                dma("pool", yp[ti * 512:(ti + 1) * 512, :].rearrange("(s p) d -> p s d", p=128), xstage[:, :, :])

    def stage(n):
        if n > stop_at:
            raise _Stop()

    try:
        stage(1)
        seq = [("sample", 0), ("prefix", 0)] + [("main", i) for i in range(n_main_tiles)]
        for i, (kd_, ti_) in enumerate(seq):
            tile_pass(kd_, ti_, pre_loaded=(i > 0), nxt=(seq[i + 1] if i + 1 < len(seq) else None))
    except _Stop:
        pass
    S.finish()
    S.emit(nc)
    st.close()
    return nc

from concourse.bass_utils import run_bass_kernel_spmd

_CACHE = {}


def _tile_w(W, gn):
    K, N = W.shape
    kc = K // 128
    g = N // gn
    return np.ascontiguousarray(W.reshape(kc, 128, g, gn).transpose(2, 1, 0, 3).reshape(g * 128, kc * gn))


def _cols(v, n):
    return np.ascontiguousarray(np.asarray(v, np.float32).reshape(n, 128).T)


def _consts():
    ident = np.eye(128, dtype=np.float32)
    slopes = np.exp2(-8.0 * np.arange(1, 9, dtype=np.float32) / 8).astype(np.float32)
    r = np.arange(128)[:, None]
    c = np.arange(256)[None, :]
    dist = np.abs(128 + r - c).astype(np.float32)
    base_mask = np.zeros((128, 256), bool)
    base_mask[:64, 192:] = True
    base_mask[64:, :64] = True
    m_first = base_mask.copy(); m_first[:, :176] = True
    m_t0 = base_mask.copy(); m_t0[:, :112] = True
    tabs = []
    for msk in (base_mask, m_first, m_t0):
        t = np.zeros((128, 8, 256), np.float32)
        for h in range(8):
            t[:, h, :] = np.where(msk, np.float32(NEG), -slopes[h] * dist)
        tabs.append(t.reshape(128, 2048))
    biasT = np.concatenate(tabs, 0)
    p = np.arange(128)[:, None] % 64
    t = np.arange(64)[None, :]
    matt = np.tile((p <= t).astype(np.float32)[:, None, :], (1, 4, 1)).reshape(128, 256)
    cm = np.ones((128, 512), np.float32)
    cm[:, ::64] = 0.0
    return ident, biasT, np.ascontiguousarray(matt), cm


def kernel(x_prompt, x_sample, cache_k_a, cache_v_a, state_conv_b, state_conv_c, state_hgrn,
           meta_tokens, ab_w_in, ab_b_in, a_sinks, b_conv_w, ab_w_o, cd_w_in, cd_b_in,
           c_conv_w, c_conv_b, c_ln_g, c_ln_b, d_lower_bounds, d_norm_g, cd_w_o,
           ln1_g, ln1_b, ln2_g, ln2_b, ffn_w_gu, ffn_w_down):
    f = lambda a: np.asarray(a, np.float32)
    x_prompt, x_sample = f(x_prompt), f(x_sample)
    ab_w_in, ab_b_in = f(ab_w_in), f(ab_b_in)
    cd_w_in, cd_b_in = f(cd_w_in), f(cd_b_in)
    q0, k0, v0, bg0, cg0, hb0 = 0, 512, 640, 768, 1280, 1792
    kd_idx = np.concatenate([np.arange(k0, k0 + 64), np.arange(k0, k0 + 64), np.arange(k0 + 64, k0 + 128), np.arange(k0 + 64, k0 + 128)])
    colsA = np.concatenate([np.arange(q0, q0 + 512), kd_idx, np.arange(bg0, bg0 + 512), np.arange(hb0, hb0 + 512),
                            np.arange(cg0, cg0 + 512)])
    colsB = np.concatenate([np.arange(k0, k0 + 128), np.arange(v0, v0 + 128)])
    w0in = _tile_w(ab_w_in[:, np.concatenate([colsA, colsB])], 512)
    b0 = _cols(ab_b_in[colsA], 18)
    bkv = np.ascontiguousarray(np.tile(ab_b_in[colsB][None, :], (128, 1)))
    c1 = np.concatenate([np.arange(0, 2048), np.arange(2560, 3072), np.arange(2048, 2560)])
    w1in = _tile_w(cd_w_in[:, c1], 512)
    b1 = _cols(cd_b_in[c1[:2560]], 20)
    bi = np.ascontiguousarray(np.tile(cd_b_in[2048:2560][None, :], (128, 1)))
    gu_idx = np.stack([np.arange(2816).reshape(22, 128), 2816 + np.arange(2816).reshape(22, 128)], 1).reshape(-1)
    wts = {"w0in": w0in, "w0o": _tile_w(f(ab_w_o), 512), "w1in": w1in, "w1o": _tile_w(f(cd_w_o), 512)}
    for l in range(2):
        wts[f"gu{l}"] = _tile_w(f(ffn_w_gu)[l][:, gu_idx], 512)
        wts[f"dn{l}"] = _tile_w(f(ffn_w_down)[l], 128)
    par = np.zeros((128, NPAR), np.float32)
    par[:, PO["b0"]:PO["b0"] + 18] = b0
    par[:, PO["wB"]:PO["wB"] + 12] = f(b_conv_w).reshape(3, 4, 128).transpose(2, 1, 0).reshape(128, 12)
    for l in range(2):
        lo = PO["ln"] + l * 32
        par[:, lo:lo + 8] = _cols(f(ln1_g)[l], 8); par[:, lo + 8:lo + 16] = _cols(f(ln1_b)[l], 8)
        par[:, lo + 16:lo + 24] = _cols(f(ln2_g)[l], 8); par[:, lo + 24:lo + 32] = _cols(f(ln2_b)[l], 8)
    par[:, PO["b1"]:PO["b1"] + 20] = b1
    par[:, PO["wC"]:PO["wC"] + 124] = f(c_conv_w).reshape(31, 4, 128).transpose(2, 1, 0).reshape(128, 124)
    par[:, PO["ccb"]:PO["ccb"] + 4] = _cols(c_conv_b, 4)
    par[:, PO["clng"]:PO["clng"] + 4] = _cols(c_ln_g, 4)
    par[:, PO["clnb"]:PO["clnb"] + 4] = _cols(c_ln_b, 4)
    par[:, PO["lbin"]:PO["lbin"] + 4] = _cols(f(d_lower_bounds)[0], 4)
    par[:, PO["lbin"] + 4:PO["lbin"] + 8] = _cols(f(d_lower_bounds)[1], 4)
    par[:, PO["normg"]:PO["normg"] + 4] = _cols(d_norm_g, 4)
    par[:, PO["sink"]:PO["sink"] + 8] = np.tile(f(a_sinks)[None, :], (128, 1))
    ident, biasT, matt, cm = _consts()
    common = {"meta": f(meta_tokens), "par": par, "bkv": bkv, "bi": bi, "ident": ident, "biasT": biasT,
              "matt": matt, "cmask": cm}
    common.update(wts)
    in_maps = []
    for c in range(8):
        ckc = f(cache_k_a)[c].reshape(128, 2, 64)
        m = dict(common)
        m["xp"] = np.ascontiguousarray(x_prompt[c % 4])
        m["xs"] = np.ascontiguousarray(x_sample[c])
        m["ckd"] = np.ascontiguousarray(np.concatenate([ckc[:, 0], ckc[:, 0], ckc[:, 1], ckc[:, 1]], 1))
        m["ck"] = np.ascontiguousarray(ckc.reshape(128, 128))
        m["cv"] = np.ascontiguousarray(f(cache_v_a)[c].reshape(128, 128))
        m["scb"] = np.ascontiguousarray(f(state_conv_b)[c].reshape(2, 4, 128).transpose(2, 1, 0).reshape(128, 8))
        m["scc"] = np.ascontiguousarray(f(state_conv_c)[c].reshape(30, 4, 128).transpose(2, 1, 0).reshape(128, 120))
        m["shg"] = np.ascontiguousarray(f(state_hgrn)[c].transpose(1, 0, 2).reshape(128, 512))
        in_maps.append(m)
    if "nc" not in _CACHE:
        nc = bass.Bass("TRN2", target_bir_lowering=False)
        build_program(nc)
        _CACHE["nc"] = nc
    res = run_bass_kernel_spmd(_CACHE["nc"], in_maps, core_ids=list(range(8)))
    R = res.results
    yp = np.stack([R[b]["yp"] for b in range(4)]).astype(np.float32)
    ys = np.stack([R[c]["ys"] for c in range(8)]).astype(np.float32)
    st4 = lambda k, shp: np.stack([R[b][k] for b in range(4)]).reshape(shp).astype(np.float32)
    st8 = lambda k, shp: np.stack([R[c][k] for c in range(8)]).reshape(shp).astype(np.float32)
    return (yp, ys, st4("kp", (4, 128, 2, 64)), st4("vp", (4, 128, 2, 64)), st4("cbp", (4, 2, 512)),
            st4("ccp", (4, 30, 512)), st4("hgp", (4, 4, 128, 128)),
            st8("ks", (8, 128, 2, 64)), st8("vs", (8, 128, 2, 64)), st8("cbs", (8, 2, 512)),
            st8("ccs", (8, 30, 512)), st8("hgs", (8, 4, 128, 128)))
```

```python
import numpy as np
import concourse.bass as bass
import concourse.mybir as mybir

F32 = mybir.dt.float32
BF16 = mybir.dt.bfloat16
AF = mybir.ActivationFunctionType
ALU = mybir.AluOpType
AX = mybir.AxisListType

_DS = {F32: 4, BF16: 2}


def _rng(ap):
    t = ap.tensor
    ds = _DS.get(ap.dtype, 4)
    a = ap.ap
    off = int(ap.offset)
    sp = str(ap.space)
    if "PSUM" in sp.upper():
        return (t.name, 0, 1 << 30, 0, 128)
    if "DRAM" in sp.upper() or "HBM" in sp.upper():
        span = 1
        for st, n in a:
            span += abs(st) * (n - 1)
        return (t.name, off * ds, (off + span) * ds, 0, 1)
    pst, pn = a[0]
    if pst == 0:
        pst = 1 << 40
    if pn > 1 or True:
        tp = 1
        for s in list(t.shape)[1:]:
            tp *= s
        tds = _DS.get(t.dtype, 4)
        tp = tp * tds // ds
    plo = off // tp
    fo = off % tp
    span = 1
    for st, n in a[1:]:
        span += abs(st) * (n - 1)
    return (t.name, fo * ds, (fo + span) * ds, plo, plo + pn)


class Sched:
    ENG = ["pe", "act", "dve", "pool", "sp"]
    R = 12
    RQ = {'sp': 12, 'pool': 2, 'act': 4}

    def __init__(self):
        self.ops = {e: [] for e in self.ENG}
        self.count = {e: 0 for e in self.ENG}
        self.seen = {e: {} for e in self.ENG}
        self.recs = {}
        self.dma_idx = {"sp": 0, "pool": 0, "act": 0}
        self.all_dma = {}

    def _deps(self, ins, outs, eng=None):
        deps = {}

        def add(tk):
            if tk is None:
                return
            s, v = tk
            if deps.get(s, 0) < v:
                deps[s] = v

        for ap in ins:
            n, lo, hi, pl, ph = _rng(ap)
            for r in self.recs.get(n, ()):
                if r[0] < hi and lo < r[1] and r[2] < ph and pl < r[3]:
                    add(r[4])
                    if hi == 1 << 30:
                        for s, v in r[5].items():
                            if s != eng:
                                add((s, v))
        for ap in outs:
            n, lo, hi, pl, ph = _rng(ap)
            for r in self.recs.get(n, ()):
                if r[0] < hi and lo < r[1] and r[2] < ph and pl < r[3]:
                    add(r[4])
                    for s, v in r[5].items():
                        add((s, v))
        return deps

    def _split(self, n, lo, hi, pl, ph):
        lst = self.recs.get(n)
        if not lst:
            return
        out = []
        for r in lst:
            if r[0] < hi and lo < r[1] and pl <= r[2] and r[3] <= ph and (r[0] < lo or hi < r[1]):
                if r[0] < lo:
                    out.append([r[0], lo, r[2], r[3], r[4], dict(r[5])])
                out.append([max(r[0], lo), min(r[1], hi), r[2], r[3], r[4], dict(r[5])])
                if hi < r[1]:
                    out.append([hi, r[1], r[2], r[3], r[4], dict(r[5])])
            else:
                out.append(r)
        self.recs[n] = out

    def _update(self, ins, outs, tk):
        for ap in ins:
            n, lo, hi, pl, ph = _rng(ap)
            self._split(n, lo, hi, pl, ph)
            hit = False
            for r in self.recs.get(n, ()):
                if r[0] < hi and lo < r[1] and r[2] < ph and pl < r[3]:
                    if r[5].get(tk[0], 0) < tk[1]:
                        r[5][tk[0]] = tk[1]
                    hit = True
            if not hit:
                self.recs.setdefault(n, []).append([lo, hi, pl, ph, None, {tk[0]: tk[1]}])
        for ap in outs:
            n, lo, hi, pl, ph = _rng(ap)
            self._split(n, lo, hi, pl, ph)
            lst = self.recs.setdefault(n, [])
            lst[:] = [r for r in lst if not (lo <= r[0] and r[1] <= hi and pl <= r[2] and r[3] <= ph)]
            lst.append([lo, hi, pl, ph, tk, {}])

    def add(self, eng, fn, ins=(), outs=(), dma=False, signal=True):
        deps = self._deps(ins, outs, eng)
        if dma:
            i = self.dma_idx[eng]
            self.dma_idx[eng] = i + 1
            R = self.RQ[eng]
            k = i % R
            sem = f"{eng}_d{k}"
            val = 16 * (i // R + 1)
            if val > 16:
                if deps.get(sem, 0) < val - 16:
                    deps[sem] = val - 16
            tk = (sem, val)
            inc = (sem, 16)
            self.all_dma[sem] = val
        elif not signal:
            tk = (eng, self.count[eng] + 1)
            inc = None
        else:
            self.count[eng] += 1
            tk = (eng, self.count[eng])
            inc = (eng, 1)
        waits = []
        seen = self.seen[eng]
        for s, v in deps.items():
            if s == eng and eng == "pe":
                continue
            if seen.get(s, 0) >= v:
                continue
            seen[s] = v
            waits.append((s, v))
        self.ops[eng].append((waits, fn, inc))
        self._update(ins, outs, tk)
        return tk

    def finish(self):
        waits = []
        for s, v in self.all_dma.items():
            waits.append((s, v))
        for e in ["pe", "act", "dve", "pool"]:
            if self.count[e]:
                waits.append((e, self.count[e]))
        self.ops["sp"].append((waits, None, None))

    def sem_names(self):
        names = ["pe", "act", "dve", "pool"]
        for q in ("sp", "pool", "act"):
            n = min(self.dma_idx[q], self.RQ[q])
            names += [f"{q}_d{k}" for k in range(n)]
        return names

    def emit(self, nc):
        import contextlib
        names = self.sem_names()
        with contextlib.ExitStack() as st:
            sems = {n: st.enter_context(nc.semaphore(n)) for n in names}
            block = st.enter_context(nc.Block())

            def run(e, key):
                for waits, fn, inc in self.ops[key]:
                    for s, v in waits:
                        e.wait_ge(sems[s], v)
                    if fn is None:
                        continue
                    ins = fn(e)
                    if inc is not None:
                        ins.then_inc(sems[inc[0]], inc[1])

            @block.tensor
            def _(e):
                run(e, "pe")

            @block.scalar
            def _(e):
                run(e, "act")

            @block.vector
            def _(e):
                run(e, "dve")

            @block.gpsimd
            def _(e):
                run(e, "pool")

            @block.sync
            def _(e):
                run(e, "sp")

import contextlib

D = 1024
NH = 8
ALPHA = 4 ** 0.25
LN_EPS = 1e-5
RMS_EPS = 1e-6
NEG = -1e30
SLOT = 4096
NSLOT = 4

PO = {}
_o = 0
for _n, _w in [("b0", 18), ("wB", 12), ("ln", 64), ("b1", 20), ("wC", 124), ("ccb", 4), ("clng", 4),
               ("clnb", 4), ("lbin", 8), ("normg", 4), ("sink", 8)]:
    PO[_n] = _o
    _o += _w
NPAR = _o

WSPEC = {
    "w0in": (5, 4096), "w0o": (2, 4096), "gu0": (11, 4096), "dn0": (8, 2816),
    "w1in": (6, 4096), "w1o": (2, 4096), "gu1": (11, 4096), "dn1": (8, 2816),
}


class _Stop(Exception):
    pass


def build_program(nc, n_main_tiles=8, debug=False, stop_at=99):
    S = Sched()
    dr = {}

    def din(name, shape):
        dr[name] = nc.dram_tensor(name, shape, F32, kind="ExternalInput").ap()
        return dr[name]

    def dout(name, shape):
        dr[name] = nc.dram_tensor(name, shape, F32, kind="ExternalOutput").ap()
        return dr[name]

    xp = din("xp", [4096, D]); xs = din("xs", [64, D]); meta = din("meta", [16, D])
    ckd = din("ckd", [128, 256]); ck = din("ck", [128, 128]); cv = din("cv", [128, 128])
    scb = din("scb", [128, 8]); scc = din("scc", [128, 120]); shg = din("shg", [128, 512])
    par_d = din("par", [128, NPAR]); bkv_d = din("bkv", [128, 256]); bi_d = din("bi", [128, 512])
    ident_d = din("ident", [128, 128]); biasT_d = din("biasT", [3 * 128, 2048])
    matt_d = din("matt", [128, 256]); cmask_d = din("cmask", [128, 512])
    wd = {}
    ws = {}
    for n, (g, c) in WSPEC.items():
        wd[n] = din(n, [g * 128, c])
        ws[n] = nc.dram_tensor(n + "_s", [g * 128, c], BF16).ap()
    yp = dout("yp", [4096, D]); ys = dout("ys", [64, D])
    kp = dout("kp", [128, 128]); vp = dout("vp", [128, 128]); cbp = dout("cbp", [2, 512]); ccp = dout("ccp", [30, 512])
    hgp = dout("hgp", [4, 128, 128])
    kso = dout("ks", [128, 128]); vso = dout("vs", [128, 128]); cbs = dout("cbs", [2, 512]); ccs = dout("ccs", [30, 512])
    hgs = dout("hgs", [4, 128, 128])

    st = contextlib.ExitStack()
    cur = [0]

    def alloc(nbytes):
        o = cur[0]
        cur[0] += (nbytes + 63) // 64 * 64
        return o

    GB = 4 * 544 * 4
    offs = {}
    offs["U"] = alloc(5 * GB)
    for n, nb_ in [("h32", 16384), ("hbf", 8192), ("r32", 16384), ("tbf", 8192), ("mixbf", 8192), ("ST", 6144),
                   ("qT", 4096), ("kdT", 2 * 640 * 2), ("vtok", 5 * 128 * 2), ("kebf", 4096), ("kdtok", 4096),
                   ("vtok1", 4096), ("atbf", 2048), ("ucbf", 4 * 544 * 2), ("PA", 10240), ("S32", 2048),
                   ("biasg", 4096), ("biasx", 4096), ("biasx2", 4096), ("ident32", 512), ("identbf", 256), ("ones", 768),
                   ("matt", 1024), ("cmask", 2048), ("par", NPAR * 4), ("dg", 8 * 256), ("stg", 2048),
                   ("kvout", 1024), ("bkv", 1024), ("bi", 2048), ("small", 1024), ("uh0", 32), ("uh1", 480),
                   ("lbw", 256), ("ring", NSLOT * SLOT * 2)]:
        offs[n] = alloc(nb_)
    total = cur[0]
    assert total <= 212000, total
    print('SBUF total', total)
    arena = st.enter_context(nc.sbuf_tensor("arena", [128, total // 4], F32))

    def V(off, shape, dt=F32):
        n = 1
        for s_ in shape:
            n *= s_
        nb_ = n * (4 if dt == F32 else 2)
        a = arena[:, off // 4: off // 4 + nb_ // 4]
        if dt != F32:
            a = a.bitcast(dt)
        if len(shape) == 2:
            return a.rearrange("p (a b) -> p a b", a=shape[0])
        if len(shape) == 3:
            return a.rearrange("p (a b c) -> p a b c", a=shape[0], b=shape[1])
        return a

    G = [V(offs["U"] + i * GB, [4, 544]) for i in range(5)]
    actb = V(offs["U"], [22, 512], BF16)
    h32 = V(offs["h32"], [8, 512]); hbf = V(offs["hbf"], [8, 512], BF16)
    r32 = V(offs["r32"], [8, 512]); xstage = V(offs["r32"], [4, 1024])
    xin = V(offs["U"], [4, 1024])
    tbf = V(offs["tbf"], [8, 512], BF16); mixbf = V(offs["mixbf"], [8, 512], BF16)
    stt_ = [V(offs["ST"] + i * 2048, [512]) for i in range(3)]
    qT = V(offs["qT"], [4, 512], BF16)
    kdT = V(offs["kdT"], [2, 640], BF16); vtok = V(offs["vtok"], [5, 128], BF16)
    kebf = V(offs["kebf"], [4, 512], BF16); kdtok = V(offs["kdtok"], [4, 512], BF16)
    vtok1 = V(offs["vtok1"], [4, 512], BF16); atbf = V(offs["atbf"], [4, 4, 64], BF16)
    ucbf = V(offs["ucbf"], [4, 544], BF16)
    P32 = V(offs["PA"], [4, 256]); Pn32 = V(offs["PA"] + 4096, [4, 256]); PTb = V(offs["PA"] + 8192, [8, 128], BF16)
    Sbf = V(offs["PA"], [9, 4, 128], BF16)
    S32 = V(offs["S32"], [4, 128])
    biasg = V(offs["biasg"], [8, 256], BF16); biasx = V(offs["biasx"], [8, 256], BF16); biasx2 = V(offs["biasx2"], [8, 256], BF16)
    ident32 = V(offs["ident32"], [128]); identbf = V(offs["identbf"], [128], BF16)
    ones = V(offs["ones"], [3, 128], BF16)
    matt = V(offs["matt"], [4, 64]); cmask = V(offs["cmask"], [512])
    par = V(offs["par"], [NPAR])
    dg = V(offs["dg"], [8, 128], BF16)
    stg = V(offs["stg"], [512]); kvout = V(offs["kvout"], [256]); bkv = V(offs["bkv"], [256]); bi = V(offs["bi"], [512])
    small = V(offs["small"], [256])
    uh0 = V(offs["uh0"], [4, 2]); uh1 = V(offs["uh1"], [4, 30]); lbw = V(offs["lbw"], [64])
    ring = [V(offs["ring"] + i * SLOT * 2, [SLOT], BF16) for i in range(NSLOT)]
    PS = [st.enter_context(nc.psum_tensor(f"ps{i}", [128, 512], F32)) for i in range(8)]
    bank_i = [0]

    def nb():
        b = PS[bank_i[0] % 8]
        bank_i[0] += 1
        return b

    def pc(name, i=0, n=1):
        return par[:, PO[name] + i: PO[name] + i + n]

    def aps(*xs_):
        return [x for x in xs_ if x is not None and not isinstance(x, (int, float))]

    def mm(out, lhsT, rhs, start=True, stop=True, sig=None):
        S.add("pe", lambda e, o=out, l=lhsT, r=rhs, s=start, t=stop: e.matmul(o, lhsT=l, rhs=r, start=s, stop=t),
              ins=[lhsT, rhs], outs=[out], signal=(stop if sig is None else sig))

    def tr(out, in_, idn):
        S.add("pe", lambda e, o=out, i=in_, d=idn: e.transpose(o, i, d), ins=[in_, idn], outs=[out])

    def act(out, in_, func, bias=None, scale=None, accum=None):
        kw = {}
        if bias is not None:
            kw["bias"] = bias
        if scale is not None:
            kw["scale"] = scale
        if accum is not None:
            kw["accum_out"] = accum
        S.add("act", lambda e, o=out, i=in_, f=func, k=kw: e.activation(out=o, in_=i, func=f, **k),
              ins=aps(in_, bias, scale), outs=aps(out, accum))

    def ts(out, in0, s1, s2, op0, op1=None, eng="dve"):
        if op1 is None:
            S.add(eng, lambda e, o=out, i=in0, a=s1, p=op0: e.tensor_scalar(out=o, in0=i, scalar1=a, scalar2=None, op0=p),
                  ins=aps(in0, s1), outs=[out])
        else:
            S.add(eng, lambda e, o=out, i=in0, a=s1, b=s2, p=op0, q=op1: e.tensor_scalar(out=o, in0=i, scalar1=a, scalar2=b, op0=p, op1=q),
                  ins=aps(in0, s1, s2), outs=[out])

    def tt(out, in0, in1, op, eng="dve"):
        S.add(eng, lambda e, o=out, a=in0, b=in1, p=op: e.tensor_tensor(out=o, in0=a, in1=b, op=p), ins=[in0, in1], outs=[out])

    def stt(out, in0, scalar, in1, op0, op1):
        S.add("dve", lambda e, o=out, a=in0, s=scalar, b=in1, p=op0, q=op1: e.scalar_tensor_tensor(out=o, in0=a, scalar=s, in1=b, op0=p, op1=q),
              ins=aps(in0, scalar, in1), outs=[out])

    def cp(out, in_, eng="dve"):
        if eng == "act":
            act(out, in_, AF.Identity)
        else:
            S.add(eng, lambda e, o=out, i=in_: e.tensor_copy(out=o, in_=i), ins=[in_], outs=[out])

    def ms(ap, val, eng="dve"):
        S.add(eng, lambda e, a=ap, v=val: e.memset(a, v), outs=[ap])

    def dma(eng, out, in_):
        S.add(eng, lambda e, o=out, i=in_: e.dma_start(out=o, in_=i), ins=[in_], outs=[out], dma=True)

    def red(out, in_, op):
        S.add("dve", lambda e, o=out, i=in_, p=op: e.tensor_reduce(out=o, in_=i, axis=AX.X, op=p), ins=[in_], outs=[out])

    def recip(out, in_):
        S.add("dve", lambda e, o=out, i=in_: e.reciprocal(out=o, in_=i), ins=[in_], outs=[out])

    ms(arena[:, 0:total // 8], 0.0, "dve")
    ms(arena[:, total // 8: total // 4], 0.0, "pool")
    dma("sp", par, par_d); dma("sp", ident32, ident_d); dma("sp", bkv, bkv_d); dma("sp", bi, bi_d)
    dma("sp", matt.rearrange("p a b -> p (a b)"), matt_d); dma("sp", cmask, cmask_d)
    dma("pool", biasg.rearrange("p a b -> p (a b)"), biasT_d[0:128, :])
    dma("pool", biasx.rearrange("p a b -> p (a b)"), biasT_d[128:256, :])
    dma("pool", biasx2.rearrange("p a b -> p (a b)"), biasT_d[256:384, :])
    dma("pool", vtok[:, 0, :], cv)
    cp(identbf, ident32)
    ms(ones[:, 0, :], 1.0 / 1024); ms(ones[:, 1, :], 1.0 / 512); ms(ones[:, 2, :], 1.0 / 128)
    l0 = par[:, PO["lbin"]: PO["lbin"] + 4]; l1 = par[:, PO["lbin"] + 4: PO["lbin"] + 8]
    e0 = lbw[:, 0:4]; e1 = lbw[:, 4:8]; sm = lbw[:, 8:12]; p0 = lbw[:, 12:16]; p1 = lbw[:, 16:20]
    lb = lbw[:, 20:24]; oml = lbw[:, 24:28]; mxl = lbw[:, 28:32]
    tt(mxl, l0, l1, ALU.max)
    tt(e0, l0, mxl, ALU.subtract); tt(e1, l1, mxl, ALU.subtract)
    act(e0, e0, AF.Exp); act(e1, e1, AF.Exp)
    tt(sm, e0, e1, ALU.add); recip(sm, sm)
    tt(p0, e0, sm, ALU.mult); tt(p1, e1, sm, ALU.mult)
    tt(lb, p0, p1, ALU.add); tt(lb, lb, p0, ALU.subtract)
    ts(oml, lb, -1.0, 1.0, ALU.mult, ALU.add)
    cast_seq = []
    for n in ["w0in", "w0o", "gu0", "dn0"]:
        cast_seq += [(n, gi) for gi in range(WSPEC[n][0])]
    cast_seq += [("w1in", gi) for gi in (1, 0, 2, 3, 4, 5)]
    for n in ["w1o", "gu1", "dn1"]:
        cast_seq += [(n, gi) for gi in range(WSPEC[n][0])]
    cast_pos = {c: i for i, c in enumerate(cast_seq)}
    cast_done = [0]

    def ensure_cast(upto):
        while cast_done[0] <= min(upto, len(cast_seq) - 1):
            n, gi = cast_seq[cast_done[0]]
            dma("pool", ws[n][gi * 128:(gi + 1) * 128, :], wd[n][gi * 128:(gi + 1) * 128, :])
            cast_done[0] += 1

    ensure_cast(len(cast_seq))
    ring_i = [0]

    def wl(name, gi):
        g, c = WSPEC[name]
        k = ring_i[0] % NSLOT
        ring_i[0] += 1
        dma("sp", ring[k][:, 0:c], ws[name][gi * 128:(gi + 1) * 128, :])
        return ring[k]

    def ln_block(nch, src, onesrow, gname, bname, dst32, dstbf, sqbuf, T, silu_out=None):
        h = nch // 2
        cp(tbf[:, 0:h, 0:T], src[:, 0:h, 0:T], "dve")
        cp(tbf[:, h:nch, 0:T], src[:, h:nch, 0:T], "dve")
        act(sqbuf[:, 0:h, 0:T], src[:, 0:h, 0:T], AF.Square)
        act(sqbuf[:, h:nch, 0:T], src[:, h:nch, 0:T], AF.Square)
        bS = nb(); bQ = nb()
        for k in range(nch):
            mm(bS[:, 0:T], ones[:, onesrow, :], tbf[:, k, 0:T], k == 0, k == nch - 1)
        for k in range(nch):
            mm(bQ[:, 0:T], ones[:, onesrow, :], sqbuf[:, k, 0:T], k == 0, k == nch - 1)
        mean = stt_[0][:, 0:T]; msq = stt_[1][:, 0:T]; rstd = stt_[2][:, 0:T]
        cp(mean, bS[:, 0:T], "dve")
        act(msq, bS[:, 0:T], AF.Square)
        tt(rstd, bQ[:, 0:T], msq, ALU.subtract)
        ts(rstd, rstd, 0.0, None, ALU.max)
        act(rstd, rstd, AF.Ln, bias=small[:, 200:201])
        act(rstd, rstd, AF.Exp, scale=-0.5)
        for k in range(nch):
            tt(src[:, k, 0:T], src[:, k, 0:T], mean, ALU.subtract)
            tt(src[:, k, 0:T], src[:, k, 0:T], rstd, ALU.mult)
            if silu_out is not None:
                act(silu_out[:, k, 0:T], src[:, k, 0:T], AF.Silu, bias=bname(k), scale=gname(k))
            else:
                act(dstbf[:, k, 0:T], src[:, k, 0:T], AF.Identity, bias=bname(k), scale=gname(k))
        if silu_out is None:
            for k in range(nch):
                act(dst32[:, k, 0:T], src[:, k, 0:T], AF.Identity, bias=bname(k), scale=gname(k))

    ms(small[:, 200:201], LN_EPS)
    ms(small[:, 201:202], RMS_EPS)

    def ffn_block(layer, T):
        gu = f"gu{layer}"; dn = f"dn{layer}"
        for j in range(22):
            if j % 2 == 0:
                w = wl(gu, j // 2)
                wv = w[:, 0:4096].rearrange("p (k n) -> p k n", k=8)
            off = (j % 2) * 256
            bG = nb(); bU = nb()
            for k in range(8):
                mm(bG[:, 0:T], wv[:, k, off:off + 128], hbf[:, k, 0:T], k == 0, k == 7)
            for k in range(8):
                mm(bU[:, 0:T], wv[:, k, off + 128:off + 256], hbf[:, k, 0:T], k == 0, k == 7)
            sil = stt_[j % 2][:, 0:T]
            act(sil, bG[:, 0:T], AF.Silu)
            tt(actb[:, j, 0:T], sil, bU[:, 0:T], ALU.mult)
        for m in range(8):
            w = wl(dn, m)
            wv = w[:, 0:2816].rearrange("p (k n) -> p k n", k=22)
            b = nb()
            for k in range(22):
                mm(b[:, 0:T], wv[:, k, :], actb[:, k, 0:T], k == 0, k == 21)
            stt(r32[:, m, 0:T], h32[:, m, 0:T], ALPHA, b[:, 0:T], ALU.mult, ALU.add)
        lo = PO["ln"] + layer * 32
        ln_block(8, r32, 0, lambda k: par[:, lo + 16 + k: lo + 17 + k], lambda k: par[:, lo + 24 + k: lo + 25 + k],
                 h32, hbf, mixbf, T)

    def wo_block(name, layer, T):
        for m in range(8):
            if m % 4 == 0:
                w = wl(name, m // 4)
                wv = w[:, 0:4096].rearrange("p (k n) -> p k n", k=8)
            off = (m % 4) * 128
            b = nb()
            for k in range(8):
                mm(b[:, 0:T], wv[:, k, off:off + 128], mixbf[:, k, 0:T], k == 0, k == 7)
            stt(r32[:, m, 0:T], h32[:, m, 0:T], ALPHA, b[:, 0:T], ALU.mult, ALU.add)
        lo = PO["ln"] + layer * 32
        ln_block(8, r32, 0, lambda k: par[:, lo + k: lo + 1 + k], lambda k: par[:, lo + 8 + k: lo + 9 + k],
                 h32, hbf, mixbf, T)

    def state_rows_out(src, c0, dst, r0, nrows):
        b = nb()
        for c in range(4):
            tr(b[0:32, c * 128:(c + 1) * 128], src[:, c, c0:c0 + 32], ident32[:, :])
        cp(stg[0:32, :], b[0:32, :], "dve")
        dma("pool", dst, stg[r0:r0 + nrows, :])

    def load_x(kind, ti):
        if kind == "sample":
            dma("pool", xin[0:64, 0, :], xs)
        elif kind == "prefix":
            ms(xin[0:64, 0, :], 0.0)
            dma("pool", xin[48:64, 0, :], meta)
        else:
            dma("pool", xin[:, :, :], xp[ti * 512:(ti + 1) * 512, :].rearrange("(s p) d -> p s d", p=128))

    def tile_pass(kind, ti, pre_loaded=False, nxt=None):
        T = 512 if kind == "main" else 64
        NS = max(1, T // 128)
        Pt = min(T, 128)
        last = (kind == "main" and ti == n_main_tiles - 1)
        if not pre_loaded:
            load_x(kind, ti)
        for c in range(8):
            b = nb()
            for s in range(NS):
                tr(b[:, s * 128:s * 128 + Pt], xin[0:Pt, s, c * 128:(c + 1) * 128], ident32[0:Pt, 0:Pt])
            cp(h32[:, c, 0:T], b[:, 0:T], "dve")
            cp(hbf[:, c, 0:T], b[:, 0:T], "act")
        stage(2)
        bg, u32, cc32 = G[0], G[1], G[2]
        if kind == "sample":
            dma("sp", uh0.rearrange("p a b -> p (a b)"), scb)
            dma("sp", r32[:, 0, 0:256], ckd)
            for g in range(2):
                b = nb()
                tr(b[:, 0:128], r32[:, 0, g * 128:(g + 1) * 128], ident32[:, :])
                cp(kdT[:, g, 0:128], b[:, 0:128], "act")
        elif kind == "prefix":
            ms(uh0, 0.0)
        cp(u32[:, :, 0:2], uh0)
        stage(2.1)
        wg = {}

        def w0(gi):
            if gi not in wg:
                wg.clear()
                wg[gi] = wl("w0in", gi)[:, 0:4096].rearrange("p (k n) -> p k n", k=8)
            return wg[gi]

        for m in range(18):
            wv = w0(m // 4); off = (m % 4) * 128
            b = nb()
            for k in range(8):
                mm(b[:, 0:T], wv[:, k, off:off + 128], hbf[:, k, 0:T], k == 0, k == 7)
            bc = pc("b0", m)
            if m < 4:
                ts(qT[:, m, 0:T], b[:, 0:T], bc, 0.125, ALU.add, ALU.mult)
            elif m < 6:
                act(kdT[:, m - 4, 128:128 + T], b[:, 0:T], AF.Identity, bias=bc)
            elif m < 10:
                act(bg[:, m - 6, 0:T], b[:, 0:T], AF.Identity, bias=bc)
            elif m < 14:
                act(u32[:, m - 10, 2:2 + T], b[:, 0:T], AF.Identity, bias=bc)
            else:
                stt(u32[:, m - 14, 2:2 + T], b[:, 0:T], bc, u32[:, m - 14, 2:2 + T], ALU.add, ALU.mult)
        stage(2.3)
        wv = w0(4)
        for s in range(NS):
            b = nb()
            for k in range(8):
                mm(b[0:Pt, 0:256], hbf[:, k, s * 128:s * 128 + Pt], wv[:, k, 256:512], k == 0, k == 7)
            tt(vtok[0:Pt, 1 + s, :], b[0:Pt, 128:256], bkv[0:Pt, 128:256], ALU.add)
            if last and s == NS - 1 or kind == "sample":
                tt(kvout[0:Pt, :], b[0:Pt, 0:256], bkv[0:Pt, :], ALU.add)
        if kind == "prefix":
            b = nb()
            for k in range(8):
                mm(b[64:128, 0:256], hbf[:, k, 0:64], wv[:, k, 256:512], k == 0, k == 7)
            tt(vtok[64:128, 0, :], b[64:128, 128:256], bkv[64:128, 128:256], ALU.add)
            ms(u32[:, :, 0:50], 0.0)
        stage(2.5)
        if kind == "sample":
            dma("sp", stg[0:64, 0:128], ck[64:128, :]); dma("sp", stg[0:64, 128:256], cv[64:128, :])
            dma("pool", kso[0:64, :], stg[0:64, 0:128]); dma("pool", vso[0:64, :], stg[0:64, 128:256])
            dma("pool", kso[64:128, :], kvout[0:64, 0:128]); dma("pool", vso[64:128, :], kvout[0:64, 128:256])
        stage(2.7)
        if last:
            dma("pool", kp, kvout[:, 0:128]); dma("pool", vp, kvout[:, 128:256])
        stage(3)
        nq = NS

        def s_phase(j, g):
            bt = biasx if kind == "prefix" else (biasx2 if (kind == "main" and ti == 0 and j == 0) else biasg)
            banks = [nb(), nb()]
            for hh in range(4):
                h = 4 * g + hh; c = h // 2; hf = h % 2
                o = banks[hh // 2][0:Pt, (hh % 2) * 256:(hh % 2) * 256 + 256]
                mm(o, qT[64 * hf:64 * hf + 64, c, j * 128:j * 128 + Pt], kdT[64 * hf:64 * hf + 64, g, 128 * j:128 * j + 256], True, False)
                mm(o, identbf[:, 0:Pt], bt[:, h, :], False, True)
            return banks

        def rest_phase(j, g, banks):
            mx = small[0:Pt, 0:4]; mneg = small[0:Pt, 4:8]; ssum = small[0:Pt, 8:12]; esk = small[0:Pt, 12:16]
            for q in range(2):
                red(mx[:, 2 * q:2 * q + 2], banks[q][0:Pt, :].rearrange("p (h k) -> p h k", h=2), ALU.max)
            sk = par[0:Pt, PO["sink"] + 4 * g: PO["sink"] + 4 * g + 4]
            tt(mx, mx, sk, ALU.max)
            ts(mneg, mx, -1.0, None, ALU.mult)
            for hh in range(4):
                act(P32[0:Pt, hh, :], banks[hh // 2][0:Pt, (hh % 2) * 256:(hh % 2) * 256 + 256], AF.Exp,
                    bias=mneg[:, hh:hh + 1], accum=ssum[:, hh:hh + 1])
            tt(esk, sk, mx, ALU.subtract)
            act(esk, esk, AF.Exp)
            tt(ssum, ssum, esk, ALU.add)
            recip(ssum, ssum)
            for hh in range(4):
                ts(Pn32[0:Pt, hh, :], P32[0:Pt, hh, :], ssum[:, hh:hh + 1], None, ALU.mult)
            tb = [nb(), nb()]
            for hh in range(4):
                for kb in range(2):
                    idx = hh * 2 + kb
                    tr(tb[idx // 4][:, (idx % 4) * 128:(idx % 4) * 128 + Pt], Pn32[0:Pt, hh, kb * 128:(kb + 1) * 128], ident32[0:Pt, 0:Pt])
            for q in range(2):
                src = tb[q][:, :].rearrange("p (a b) -> p a b", a=4)[:, :, 0:Pt]
                cp(PTb[:, 4 * q:4 * q + 4, 0:Pt], src, "act" if q else "dve")
            for pr in range(2):
                ob = nb()
                for hf in range(2):
                    hh = pr * 2 + hf
                    for kb in range(2):
                        mm(ob[64 * hf:64 * hf + 64, 0:Pt], vtok[:, j + kb, g * 64:(g + 1) * 64], PTb[:, hh * 2 + kb, 0:Pt], kb == 0, kb == 1)
                cp(mixbf[:, 2 * g + pr, j * 128:j * 128 + Pt], ob[:, 0:Pt], "act")

        units = [(j, g) for j in range(nq) for g in range(2)]
        prev = None
        for u in units:
            bk = s_phase(*u)
            if prev is not None:
                rest_phase(*prev)
            prev = (u[0], u[1], bk)
        rest_phase(*prev)
        if kind == "prefix":
            cp(kdT[:, :, 64:128], kdT[:, :, 128:192])
        elif kind == "main":
            cp(kdT[:, :, 0:128], kdT[:, :, 512:640])
            cp(vtok[:, 0, :], vtok[:, 4, :], "dve")
        stage(4)
        for c in range(4):
            ts(cc32[:, c, 0:T], u32[:, c, 0:T], pc("wB", c * 3), None, ALU.mult)
            stt(cc32[:, c, 0:T], u32[:, c, 1:1 + T], pc("wB", c * 3 + 1), cc32[:, c, 0:T], ALU.mult, ALU.add)
            stt(cc32[:, c, 0:T], u32[:, c, 2:2 + T], pc("wB", c * 3 + 2), cc32[:, c, 0:T], ALU.mult, ALU.add)
            tt(mixbf[:, 4 + c, 0:T], bg[:, c, 0:T], cc32[:, c, 0:T], ALU.mult)
        cp(uh0, u32[:, :, T:T + 2])
        if kind == "sample":
            state_rows_out(u32, T + 2 - 32, cbs, 30, 2)
        if last:
            state_rows_out(u32, T + 2 - 32, cbp, 30, 2)
        stage(5)
        wo_block("w0o", 0, T)
        stage(6)
        ffn_block(0, T)
        stage(7)
        uc32, sg32, c32, q32, gate32 = G[0], G[1], G[2], G[3], G[4]
        if kind == "sample":
            dma("sp", uh1.rearrange("p a b -> p (a b)"), scc)
            dma("sp", S32.rearrange("p a b -> p (a b)"), shg)
        elif kind == "prefix":
            ms(uh1, 0.0)
            ms(S32, 0.0)
        cp(uc32[:, :, 0:30], uh1)
        order = [4, 5, 6, 7, 0, 1, 2, 3] + list(range(8, 20))
        wg1 = {}

        def w1(gi):
            if gi not in wg1:
                wg1.clear()
                wg1[gi] = wl("w1in", gi)[:, 0:4096].rearrange("p (k n) -> p k n", k=8)
            return wg1[gi]

        for m in order:
            wv = w1(m // 4); off = (m % 4) * 128; c = m % 4
            b = nb()
            for k in range(8):
                mm(b[:, 0:T], wv[:, k, off:off + 128], hbf[:, k, 0:T], k == 0, k == 7)
            bc = pc("b1", m)
            if m < 4:
                stt(uc32[:, c, 30:30 + T], b[:, 0:T], bc, c32[:, c, 0:T], ALU.add, ALU.mult)
            elif m < 8:
                act(c32[:, c, 0:T], b[:, 0:T], AF.Sigmoid, bias=bc)
            elif m < 12:
                act(q32[:, c, 0:T], b[:, 0:T], AF.Identity, bias=bc)
            elif m < 16:
                act(sg32[:, c, 0:T], b[:, 0:T], AF.Sigmoid, bias=bc)
            else:
                act(gate32[:, c, 0:T], b[:, 0:T], AF.Silu, bias=bc)
                ts(gate32[:, c, 0:T], gate32[:, c, 0:T], pc("normg", c), None, ALU.mult)
        wv = w1(5)
        for s in range(NS):
            b = nb()
            for k in range(8):
                mm(b[0:Pt, 0:512], hbf[:, k, s * 128:s * 128 + Pt], wv[:, k, 0:512], k == 0, k == 7)
            tt(vtok1[0:Pt, s, :], b[0:Pt, 0:512], bi[0:Pt, :], ALU.add)
        if kind == "prefix":
            ms(uc32[:, :, 0:78], 0.0)
        stage(8)
        cp(ucbf[:, :, 0:30 + T], uc32[:, :, 0:30 + T], "act")
        cp(uh1, uc32[:, :, T:T + 30])
        if kind == "sample":
            state_rows_out(uc32, T + 30 - 32, ccs, 2, 30)
        if last:
            state_rows_out(uc32, T + 30 - 32, ccp, 2, 30)
        di = [0]
        for c in range(4):
            b = nb()
            for j in range(31):
                d = dg[:, di[0] % 8, :]
                di[0] += 1
                if j % 2:
                    act(d, identbf, AF.Identity, scale=pc("wC", c * 31 + j))
                else:
                    ts(d, identbf, pc("wC", c * 31 + j), None, ALU.mult)
                mm(b[:, 0:T], d, ucbf[:, c, j:j + T], j == 0, j == 30, sig=True)
            act(c32[:, c, 0:T], b[:, 0:T], AF.Identity, bias=pc("ccb", c))
        ln_block(4, c32, 1, lambda k: pc("clng", k), lambda k: pc("clnb", k), None, None, mixbf[:, 4:8, :], T,
                 silu_out=mixbf)
        stage(9)
        lf32 = G[0]; cum32 = G[2]
        for c in range(4):
            ts(sg32[:, c, 0:T], sg32[:, c, 0:T], oml[:, c:c + 1], lb[:, c:c + 1], ALU.mult, ALU.add)
        act(lf32[:, :, 0:T], sg32[:, :, 0:T], AF.Ln)
        ts(sg32[:, :, 0:T], sg32[:, :, 0:T], -1.0, 1.0, ALU.mult, ALU.add)
        for c in range(4):
            S.add("dve", lambda e, o=cum32[:, c, 0:T], d0=cmask[:, 0:T], d1=lf32[:, c, 0:T]:
                  e.tensor_tensor_scan(out=o, data0=d0, data1=d1, initial=0.0, op0=ALU.mult, op1=ALU.add),
                  ins=[cmask[:, 0:T], lf32[:, c, 0:T]], outs=[cum32[:, c, 0:T]])
        NCH = T // 64
        etot = small[:, 32:32 + 4 * 8].rearrange("p (a b) -> p a b", a=4)
        act(etot[:, :, 0:NCH], cum32[:, :, 63:T:64], AF.Exp)
        act(lf32[:, :, 0:T], cum32[:, :, 0:T], AF.Exp)
        tt(qT[:, :, 0:T], q32[:, :, 0:T], lf32[:, :, 0:T], ALU.mult)
        act(lf32[:, :, 0:T], cum32[:, :, 0:T], AF.Exp, scale=-1.0)
        tt(sg32[:, :, 0:T], sg32[:, :, 0:T], lf32[:, :, 0:T], ALU.mult)
        kd32 = G[0]
        for c in range(4):
            tt(kd32[:, c, 0:T].rearrange("p (a b) -> p a b", b=64), sg32[:, c, 0:T].rearrange("p (a b) -> p a b", b=64),
               etot[:, c, 0:NCH].unsqueeze(2).to_broadcast([128, NCH, 64]), ALU.mult)
        cp(kebf[:, :, 0:T], sg32[:, :, 0:T], "act")
        if kind == "prefix":
            ms(kebf[:, :, 0:48], 0.0)
            ms(kd32[:, :, 0:48], 0.0)
        for s in range(NS):
            b = nb()
            for c in range(4):
                tr(b[0:Pt, c * 128:(c + 1) * 128], kd32[:, c, s * 128:s * 128 + Pt], ident32[:, :])
            cp(kdtok[0:Pt, s, :], b[0:Pt, :], "act")
        for c in range(4):
            b = nb()
            for ch in range(NCH):
                po = (ch % 2) * 64; s = ch // 2
                mm(b[po:po + 64, s * 64:(s + 1) * 64], kebf[:, c, ch * 64:(ch + 1) * 64], qT[:, c, ch * 64:(ch + 1) * 64])
            if NCH == 1:
                tt(atbf[0:64, c, 0, :], b[0:64, 0:64], matt[0:64, 0, :], ALU.mult)
            else:
                tt(atbf[:, c, :, :], b[:, 0:256].rearrange("p (a b) -> p a b", a=4), matt[:, :, :], ALU.mult)
        cp(Sbf[:, 0, :, :], S32, "act")
        for ch in range(NCH):
            po = (ch % 2) * 64; s = ch // 2
            b = nb()
            for c in range(4):
                mm(b[:, c * 128:(c + 1) * 128], kdtok[po:po + 64, s, c * 128:(c + 1) * 128], vtok1[po:po + 64, s, c * 128:(c + 1) * 128])
            for c in range(4):
                stt(S32[:, c, :], S32[:, c, :], etot[:, c, ch:ch + 1], b[:, c * 128:(c + 1) * 128], ALU.mult, ALU.add)
            cp(Sbf[:, ch + 1, :, :], S32, "act")
        if kind == "sample":
            dma("pool", hgs.rearrange("h k v -> k h v"), S32)
        if last:
            dma("pool", hgp.rearrange("h k v -> k h v"), S32)
        o32 = G[3]
        for c in range(4):
            b = nb()
            for ch in range(NCH):
                po = (ch % 2) * 64; s = ch // 2
                o = b[:, ch * 64:(ch + 1) * 64]
                mm(o, Sbf[:, ch, c, :], qT[:, c, ch * 64:(ch + 1) * 64], True, False)
                mm(o, vtok1[po:po + 64, s, c * 128:(c + 1) * 128], atbf[po:po + 64, c, s, :], False, True)
            cp(o32[:, c, 0:T], b[:, 0:T], "dve")
        act(tbf[:, 0:4, 0:T], o32[:, :, 0:T], AF.Square)
        for c in range(4):
            b = nb()
            mm(b[:, 0:T], ones[:, 2, :], tbf[:, c, 0:T])
            rs = stt_[c % 3][:, 0:T]
            act(rs, b[:, 0:T], AF.Ln, bias=small[:, 201:202])
            act(rs, rs, AF.Exp, scale=-0.5)
            tt(o32[:, c, 0:T], o32[:, c, 0:T], rs, ALU.mult)
            tt(mixbf[:, 4 + c, 0:T], o32[:, c, 0:T], gate32[:, c, 0:T], ALU.mult)
        stage(10)
        wo_block("w1o", 1, T)
        ffn_block(1, T)
        if nxt is not None:
            load_x(*nxt)
        stage(11)
        if kind != "prefix":
            for s in range(NS):
                for q in range(2):
                    b = nb()
                    for cc in range(4):
                        c = q * 4 + cc
                        tr(b[0:Pt, cc * 128:(cc + 1) * 128], h32[:, c, s * 128:s * 128 + Pt], ident32[:, :])
                    cp(xstage[0:Pt, s, q * 512:(q + 1) * 512], b[0:Pt, :], "act" if q else "dve")
            if kind == "sample":
                dma("pool", ys, xstage[0:64, 0, :])
            else:
                dma("pool", yp[ti * 512:(ti + 1) * 512, :].rearrange("(s p) d -> p s d", p=128), xstage[:, :, :])

    def stage(n):
        if n > stop_at:
            raise _Stop()

    try:
        stage(1)
        seq = [("sample", 0), ("prefix", 0)] + [("main", i) for i in range(n_main_tiles)]
        for i, (kd_, ti_) in enumerate(seq):
            tile_pass(kd_, ti_, pre_loaded=(i > 0), nxt=(seq[i + 1] if i + 1 < len(seq) else None))
    except _Stop:
        pass
    S.finish()
    S.emit(nc)
    st.close()
    return nc

from concourse.bass_utils import run_bass_kernel_spmd

_CACHE = {}


def _tile_w(W, gn):
    K, N = W.shape
    kc = K // 128
    g = N // gn
    return np.ascontiguousarray(W.reshape(kc, 128, g, gn).transpose(2, 1, 0, 3).reshape(g * 128, kc * gn))


def _cols(v, n):
    return np.ascontiguousarray(np.asarray(v, np.float32).reshape(n, 128).T)


def _consts():
    ident = np.eye(128, dtype=np.float32)
    slopes = np.exp2(-8.0 * np.arange(1, 9, dtype=np.float32) / 8).astype(np.float32)
    r = np.arange(128)[:, None]
    c = np.arange(256)[None, :]
    dist = np.abs(128 + r - c).astype(np.float32)
    base_mask = np.zeros((128, 256), bool)
    base_mask[:64, 192:] = True
    base_mask[64:, :64] = True
    m_first = base_mask.copy(); m_first[:, :176] = True
    m_t0 = base_mask.copy(); m_t0[:, :112] = True
    tabs = []
    for msk in (base_mask, m_first, m_t0):
        t = np.zeros((128, 8, 256), np.float32)
        for h in range(8):
            t[:, h, :] = np.where(msk, np.float32(NEG), -slopes[h] * dist)
        tabs.append(t.reshape(128, 2048))
    biasT = np.concatenate(tabs, 0)
    p = np.arange(128)[:, None] % 64
    t = np.arange(64)[None, :]
    matt = np.tile((p <= t).astype(np.float32)[:, None, :], (1, 4, 1)).reshape(128, 256)
    cm = np.ones((128, 512), np.float32)
    cm[:, ::64] = 0.0
    return ident, biasT, np.ascontiguousarray(matt), cm


def kernel(x_prompt, x_sample, cache_k_a, cache_v_a, state_conv_b, state_conv_c, state_hgrn,
           meta_tokens, ab_w_in, ab_b_in, a_sinks, b_conv_w, ab_w_o, cd_w_in, cd_b_in,
           c_conv_w, c_conv_b, c_ln_g, c_ln_b, d_lower_bounds, d_norm_g, cd_w_o,
           ln1_g, ln1_b, ln2_g, ln2_b, ffn_w_gu, ffn_w_down):
    f = lambda a: np.asarray(a, np.float32)
    x_prompt, x_sample = f(x_prompt), f(x_sample)
    ab_w_in, ab_b_in = f(ab_w_in), f(ab_b_in)
    cd_w_in, cd_b_in = f(cd_w_in), f(cd_b_in)
    q0, k0, v0, bg0, cg0, hb0 = 0, 512, 640, 768, 1280, 1792
    kd_idx = np.concatenate([np.arange(k0, k0 + 64), np.arange(k0, k0 + 64), np.arange(k0 + 64, k0 + 128), np.arange(k0 + 64, k0 + 128)])
    colsA = np.concatenate([np.arange(q0, q0 + 512), kd_idx, np.arange(bg0, bg0 + 512), np.arange(hb0, hb0 + 512),
                            np.arange(cg0, cg0 + 512)])
    colsB = np.concatenate([np.arange(k0, k0 + 128), np.arange(v0, v0 + 128)])
    w0in = _tile_w(ab_w_in[:, np.concatenate([colsA, colsB])], 512)
    b0 = _cols(ab_b_in[colsA], 18)
    bkv = np.ascontiguousarray(np.tile(ab_b_in[colsB][None, :], (128, 1)))
    c1 = np.concatenate([np.arange(0, 2048), np.arange(2560, 3072), np.arange(2048, 2560)])
    w1in = _tile_w(cd_w_in[:, c1], 512)
    b1 = _cols(cd_b_in[c1[:2560]], 20)
    bi = np.ascontiguousarray(np.tile(cd_b_in[2048:2560][None, :], (128, 1)))
    gu_idx = np.stack([np.arange(2816).reshape(22, 128), 2816 + np.arange(2816).reshape(22, 128)], 1).reshape(-1)
    wts = {"w0in": w0in, "w0o": _tile_w(f(ab_w_o), 512), "w1in": w1in, "w1o": _tile_w(f(cd_w_o), 512)}
    for l in range(2):
        wts[f"gu{l}"] = _tile_w(f(ffn_w_gu)[l][:, gu_idx], 512)
        wts[f"dn{l}"] = _tile_w(f(ffn_w_down)[l], 128)
    par = np.zeros((128, NPAR), np.float32)
    par[:, PO["b0"]:PO["b0"] + 18] = b0
    par[:, PO["wB"]:PO["wB"] + 12] = f(b_conv_w).reshape(3, 4, 128).transpose(2, 1, 0).reshape(128, 12)
    for l in range(2):
        lo = PO["ln"] + l * 32
        par[:, lo:lo + 8] = _cols(f(ln1_g)[l], 8); par[:, lo + 8:lo + 16] = _cols(f(ln1_b)[l], 8)
        par[:, lo + 16:lo + 24] = _cols(f(ln2_g)[l], 8); par[:, lo + 24:lo + 32] = _cols(f(ln2_b)[l], 8)
    par[:, PO["b1"]:PO["b1"] + 20] = b1
    par[:, PO["wC"]:PO["wC"] + 124] = f(c_conv_w).reshape(31, 4, 128).transpose(2, 1, 0).reshape(128, 124)
    par[:, PO["ccb"]:PO["ccb"] + 4] = _cols(c_conv_b, 4)
    par[:, PO["clng"]:PO["clng"] + 4] = _cols(c_ln_g, 4)
    par[:, PO["clnb"]:PO["clnb"] + 4] = _cols(c_ln_b, 4)
    par[:, PO["lbin"]:PO["lbin"] + 4] = _cols(f(d_lower_bounds)[0], 4)
    par[:, PO["lbin"] + 4:PO["lbin"] + 8] = _cols(f(d_lower_bounds)[1], 4)
    par[:, PO["normg"]:PO["normg"] + 4] = _cols(d_norm_g, 4)
    par[:, PO["sink"]:PO["sink"] + 8] = np.tile(f(a_sinks)[None, :], (128, 1))
    ident, biasT, matt, cm = _consts()
    common = {"meta": f(meta_tokens), "par": par, "bkv": bkv, "bi": bi, "ident": ident, "biasT": biasT,
              "matt": matt, "cmask": cm}
    common.update(wts)
    in_maps = []
    for c in range(8):
        ckc = f(cache_k_a)[c].reshape(128, 2, 64)
        m = dict(common)
        m["xp"] = np.ascontiguousarray(x_prompt[c % 4])
        m["xs"] = np.ascontiguousarray(x_sample[c])
        m["ckd"] = np.ascontiguousarray(np.concatenate([ckc[:, 0], ckc[:, 0], ckc[:, 1], ckc[:, 1]], 1))
        m["ck"] = np.ascontiguousarray(ckc.reshape(128, 128))
        m["cv"] = np.ascontiguousarray(f(cache_v_a)[c].reshape(128, 128))
        m["scb"] = np.ascontiguousarray(f(state_conv_b)[c].reshape(2, 4, 128).transpose(2, 1, 0).reshape(128, 8))
        m["scc"] = np.ascontiguousarray(f(state_conv_c)[c].reshape(30, 4, 128).transpose(2, 1, 0).reshape(128, 120))
        m["shg"] = np.ascontiguousarray(f(state_hgrn)[c].transpose(1, 0, 2).reshape(128, 512))
        in_maps.append(m)
    if "nc" not in _CACHE:
        nc = bass.Bass("TRN2", target_bir_lowering=False)
        build_program(nc)
        _CACHE["nc"] = nc
    res = run_bass_kernel_spmd(_CACHE["nc"], in_maps, core_ids=list(range(8)))
    R = res.results
    yp = np.stack([R[b]["yp"] for b in range(4)]).astype(np.float32)
    ys = np.stack([R[c]["ys"] for c in range(8)]).astype(np.float32)
    st4 = lambda k, shp: np.stack([R[b][k] for b in range(4)]).reshape(shp).astype(np.float32)
    st8 = lambda k, shp: np.stack([R[c][k] for c in range(8)]).reshape(shp).astype(np.float32)
    return (yp, ys, st4("kp", (4, 128, 2, 64)), st4("vp", (4, 128, 2, 64)), st4("cbp", (4, 2, 512)),
            st4("ccp", (4, 30, 512)), st4("hgp", (4, 4, 128, 128)),
            st8("ks", (8, 128, 2, 64)), st8("vs", (8, 128, 2, 64)), st8("cbs", (8, 2, 512)),
            st8("ccs", (8, 30, 512)), st8("hgs", (8, 4, 128, 128)))
```

```python
import numpy as np
import concourse.bass as bass
import concourse.mybir as mybir

F32 = mybir.dt.float32
BF16 = mybir.dt.bfloat16
AF = mybir.ActivationFunctionType
ALU = mybir.AluOpType
AX = mybir.AxisListType

_DS = {F32: 4, BF16: 2}


def _rng(ap):
    t = ap.tensor
    ds = _DS.get(ap.dtype, 4)
    a = ap.ap
    off = int(ap.offset)
    sp = str(ap.space)
    if "PSUM" in sp.upper():
        return (t.name, 0, 1 << 30, 0, 128)
    if "DRAM" in sp.upper() or "HBM" in sp.upper():
        span = 1
        for st, n in a:
            span += abs(st) * (n - 1)
        return (t.name, off * ds, (off + span) * ds, 0, 1)
    pst, pn = a[0]
    if pst == 0:
        pst = 1 << 40
    if pn > 1 or True:
        tp = 1
        for s in list(t.shape)[1:]:
            tp *= s
        tds = _DS.get(t.dtype, 4)
        tp = tp * tds // ds
    plo = off // tp
    fo = off % tp
    span = 1
    for st, n in a[1:]:
        span += abs(st) * (n - 1)
    return (t.name, fo * ds, (fo + span) * ds, plo, plo + pn)


class Sched:
    ENG = ["pe", "act", "dve", "pool", "sp"]
    R = 12
    RQ = {'sp': 12, 'pool': 2, 'act': 4}

    def __init__(self):
        self.ops = {e: [] for e in self.ENG}
        self.count = {e: 0 for e in self.ENG}
        self.seen = {e: {} for e in self.ENG}
        self.recs = {}
        self.dma_idx = {"sp": 0, "pool": 0, "act": 0}
        self.all_dma = {}

    def _deps(self, ins, outs, eng=None):
        deps = {}

        def add(tk):
            if tk is None:
                return
            s, v = tk
            if deps.get(s, 0) < v:
                deps[s] = v

        for ap in ins:
            n, lo, hi, pl, ph = _rng(ap)
            for r in self.recs.get(n, ()):
                if r[0] < hi and lo < r[1] and r[2] < ph and pl < r[3]:
                    add(r[4])
                    if hi == 1 << 30:
                        for s, v in r[5].items():
                            if s != eng:
                                add((s, v))
        for ap in outs:
            n, lo, hi, pl, ph = _rng(ap)
            for r in self.recs.get(n, ()):
                if r[0] < hi and lo < r[1] and r[2] < ph and pl < r[3]:
                    add(r[4])
                    for s, v in r[5].items():
                        add((s, v))
        return deps

    def _split(self, n, lo, hi, pl, ph):
        lst = self.recs.get(n)
        if not lst:
            return
        out = []
        for r in lst:
            if r[0] < hi and lo < r[1] and pl <= r[2] and r[3] <= ph and (r[0] < lo or hi < r[1]):
                if r[0] < lo:
                    out.append([r[0], lo, r[2], r[3], r[4], dict(r[5])])
                out.append([max(r[0], lo), min(r[1], hi), r[2], r[3], r[4], dict(r[5])])
                if hi < r[1]:
                    out.append([hi, r[1], r[2], r[3], r[4], dict(r[5])])
            else:
                out.append(r)
        self.recs[n] = out

    def _update(self, ins, outs, tk):
        for ap in ins:
            n, lo, hi, pl, ph = _rng(ap)
            self._split(n, lo, hi, pl, ph)
            hit = False
            for r in self.recs.get(n, ()):
                if r[0] < hi and lo < r[1] and r[2] < ph and pl < r[3]:
                    if r[5].get(tk[0], 0) < tk[1]:
                        r[5][tk[0]] = tk[1]
                    hit = True
            if not hit:
                self.recs.setdefault(n, []).append([lo, hi, pl, ph, None, {tk[0]: tk[1]}])
        for ap in outs:
            n, lo, hi, pl, ph = _rng(ap)
            self._split(n, lo, hi, pl, ph)
            lst = self.recs.setdefault(n, [])
            lst[:] = [r for r in lst if not (lo <= r[0] and r[1] <= hi and pl <= r[2] and r[3] <= ph)]
            lst.append([lo, hi, pl, ph, tk, {}])

    def add(self, eng, fn, ins=(), outs=(), dma=False, signal=True):
        deps = self._deps(ins, outs, eng)
        if dma:
            i = self.dma_idx[eng]
            self.dma_idx[eng] = i + 1
            R = self.RQ[eng]
            k = i % R
            sem = f"{eng}_d{k}"
            val = 16 * (i // R + 1)
            if val > 16:
                if deps.get(sem, 0) < val - 16:
                    deps[sem] = val - 16
            tk = (sem, val)
            inc = (sem, 16)
            self.all_dma[sem] = val
        elif not signal:
            tk = (eng, self.count[eng] + 1)
            inc = None
        else:
            self.count[eng] += 1
            tk = (eng, self.count[eng])
            inc = (eng, 1)
        waits = []
        seen = self.seen[eng]
        for s, v in deps.items():
            if s == eng and eng == "pe":
                continue
            if seen.get(s, 0) >= v:
                continue
            seen[s] = v
            waits.append((s, v))
        self.ops[eng].append((waits, fn, inc))
        self._update(ins, outs, tk)
        return tk

    def finish(self):
        waits = []
        for s, v in self.all_dma.items():
            waits.append((s, v))
        for e in ["pe", "act", "dve", "pool"]:
            if self.count[e]:
                waits.append((e, self.count[e]))
        self.ops["sp"].append((waits, None, None))

    def sem_names(self):
        names = ["pe", "act", "dve", "pool"]
        for q in ("sp", "pool", "act"):
            n = min(self.dma_idx[q], self.RQ[q])
            names += [f"{q}_d{k}" for k in range(n)]
        return names

    def emit(self, nc):
        import contextlib
        names = self.sem_names()
        with contextlib.ExitStack() as st:
            sems = {n: st.enter_context(nc.semaphore(n)) for n in names}
            block = st.enter_context(nc.Block())

            def run(e, key):
                for waits, fn, inc in self.ops[key]:
                    for s, v in waits:
                        e.wait_ge(sems[s], v)
                    if fn is None:
                        continue
                    ins = fn(e)
                    if inc is not None:
                        ins.then_inc(sems[inc[0]], inc[1])

            @block.tensor
            def _(e):
                run(e, "pe")

            @block.scalar
            def _(e):
                run(e, "act")

            @block.vector
            def _(e):
                run(e, "dve")

            @block.gpsimd
            def _(e):
                run(e, "pool")

            @block.sync
            def _(e):
                run(e, "sp")

import contextlib

D = 1024
NH = 8
ALPHA = 4 ** 0.25
LN_EPS = 1e-5
RMS_EPS = 1e-6
NEG = -1e30
SLOT = 4096
NSLOT = 4

PO = {}
_o = 0
for _n, _w in [("b0", 18), ("wB", 12), ("ln", 64), ("b1", 20), ("wC", 124), ("ccb", 4), ("clng", 4),
               ("clnb", 4), ("lbin", 8), ("normg", 4), ("sink", 8)]:
    PO[_n] = _o
    _o += _w
NPAR = _o

WSPEC = {
    "w0in": (5, 4096), "w0o": (2, 4096), "gu0": (11, 4096), "dn0": (8, 2816),
    "w1in": (6, 4096), "w1o": (2, 4096), "gu1": (11, 4096), "dn1": (8, 2816),
}


class _Stop(Exception):
    pass


def build_program(nc, n_main_tiles=8, debug=False, stop_at=99):
    S = Sched()
    dr = {}

    def din(name, shape):
        dr[name] = nc.dram_tensor(name, shape, F32, kind="ExternalInput").ap()
        return dr[name]

    def dout(name, shape):
        dr[name] = nc.dram_tensor(name, shape, F32, kind="ExternalOutput").ap()
        return dr[name]

    xp = din("xp", [4096, D]); xs = din("xs", [64, D]); meta = din("meta", [16, D])
    ckd = din("ckd", [128, 256]); ck = din("ck", [128, 128]); cv = din("cv", [128, 128])
    scb = din("scb", [128, 8]); scc = din("scc", [128, 120]); shg = din("shg", [128, 512])
    par_d = din("par", [128, NPAR]); bkv_d = din("bkv", [128, 256]); bi_d = din("bi", [128, 512])
    ident_d = din("ident", [128, 128]); biasT_d = din("biasT", [3 * 128, 2048])
    matt_d = din("matt", [128, 256]); cmask_d = din("cmask", [128, 512])
    wd = {}
    ws = {}
    for n, (g, c) in WSPEC.items():
        wd[n] = din(n, [g * 128, c])
        ws[n] = nc.dram_tensor(n + "_s", [g * 128, c], BF16).ap()
    yp = dout("yp", [4096, D]); ys = dout("ys", [64, D])
    kp = dout("kp", [128, 128]); vp = dout("vp", [128, 128]); cbp = dout("cbp", [2, 512]); ccp = dout("ccp", [30, 512])
    hgp = dout("hgp", [4, 128, 128])
    kso = dout("ks", [128, 128]); vso = dout("vs", [128, 128]); cbs = dout("cbs", [2, 512]); ccs = dout("ccs", [30, 512])
    hgs = dout("hgs", [4, 128, 128])

    st = contextlib.ExitStack()
    cur = [0]

    def alloc(nbytes):
        o = cur[0]
        cur[0] += (nbytes + 63) // 64 * 64
        return o

    GB = 4 * 544 * 4
    offs = {}
    offs["U"] = alloc(5 * GB)
    for n, nb_ in [("h32", 16384), ("hbf", 8192), ("r32", 16384), ("tbf", 8192), ("mixbf", 8192), ("ST", 6144),
                   ("qT", 4096), ("kdT", 2 * 640 * 2), ("vtok", 5 * 128 * 2), ("kebf", 4096), ("kdtok", 4096),
                   ("vtok1", 4096), ("atbf", 2048), ("ucbf", 4 * 544 * 2), ("PA", 10240), ("S32", 2048),
                   ("biasg", 4096), ("biasx", 4096), ("biasx2", 4096), ("ident32", 512), ("identbf", 256), ("ones", 768),
                   ("matt", 1024), ("cmask", 2048), ("par", NPAR * 4), ("dg", 8 * 256), ("stg", 2048),
                   ("kvout", 1024), ("bkv", 1024), ("bi", 2048), ("small", 1024), ("uh0", 32), ("uh1", 480),
                   ("lbw", 256), ("ring", NSLOT * SLOT * 2)]:
        offs[n] = alloc(nb_)
    total = cur[0]
    assert total <= 212000, total
    print('SBUF total', total)
    arena = st.enter_context(nc.sbuf_tensor("arena", [128, total // 4], F32))

    def V(off, shape, dt=F32):
        n = 1
        for s_ in shape:
            n *= s_
        nb_ = n * (4 if dt == F32 else 2)
        a = arena[:, off // 4: off // 4 + nb_ // 4]
        if dt != F32:
            a = a.bitcast(dt)
        if len(shape) == 2:
            return a.rearrange("p (a b) -> p a b", a=shape[0])
        if len(shape) == 3:
            return a.rearrange("p (a b c) -> p a b c", a=shape[0], b=shape[1])
        return a

    G = [V(offs["U"] + i * GB, [4, 544]) for i in range(5)]
    actb = V(offs["U"], [22, 512], BF16)
    h32 = V(offs["h32"], [8, 512]); hbf = V(offs["hbf"], [8, 512], BF16)
    r32 = V(offs["r32"], [8, 512]); xstage = V(offs["r32"], [4, 1024])
    xin = V(offs["U"], [4, 1024])
    tbf = V(offs["tbf"], [8, 512], BF16); mixbf = V(offs["mixbf"], [8, 512], BF16)
    stt_ = [V(offs["ST"] + i * 2048, [512]) for i in range(3)]
    qT = V(offs["qT"], [4, 512], BF16)
    kdT = V(offs["kdT"], [2, 640], BF16); vtok = V(offs["vtok"], [5, 128], BF16)
    kebf = V(offs["kebf"], [4, 512], BF16); kdtok = V(offs["kdtok"], [4, 512], BF16)
    vtok1 = V(offs["vtok1"], [4, 512], BF16); atbf = V(offs["atbf"], [4, 4, 64], BF16)
    ucbf = V(offs["ucbf"], [4, 544], BF16)
    P32 = V(offs["PA"], [4, 256]); Pn32 = V(offs["PA"] + 4096, [4, 256]); PTb = V(offs["PA"] + 8192, [8, 128], BF16)
    Sbf = V(offs["PA"], [9, 4, 128], BF16)
    S32 = V(offs["S32"], [4, 128])
    biasg = V(offs["biasg"], [8, 256], BF16); biasx = V(offs["biasx"], [8, 256], BF16); biasx2 = V(offs["biasx2"], [8, 256], BF16)
    ident32 = V(offs["ident32"], [128]); identbf = V(offs["identbf"], [128], BF16)
    ones = V(offs["ones"], [3, 128], BF16)
    matt = V(offs["matt"], [4, 64]); cmask = V(offs["cmask"], [512])
    par = V(offs["par"], [NPAR])
    dg = V(offs["dg"], [8, 128], BF16)
    stg = V(offs["stg"], [512]); kvout = V(offs["kvout"], [256]); bkv = V(offs["bkv"], [256]); bi = V(offs["bi"], [512])
    small = V(offs["small"], [256])
    uh0 = V(offs["uh0"], [4, 2]); uh1 = V(offs["uh1"], [4, 30]); lbw = V(offs["lbw"], [64])
    ring = [V(offs["ring"] + i * SLOT * 2, [SLOT], BF16) for i in range(NSLOT)]
    PS = [st.enter_context(nc.psum_tensor(f"ps{i}", [128, 512], F32)) for i in range(8)]
    bank_i = [0]

    def nb():
        b = PS[bank_i[0] % 8]
        bank_i[0] += 1
        return b

    def pc(name, i=0, n=1):
        return par[:, PO[name] + i: PO[name] + i + n]

    def aps(*xs_):
        return [x for x in xs_ if x is not None and not isinstance(x, (int, float))]

    def mm(out, lhsT, rhs, start=True, stop=True, sig=None):
        S.add("pe", lambda e, o=out, l=lhsT, r=rhs, s=start, t=stop: e.matmul(o, lhsT=l, rhs=r, start=s, stop=t),
              ins=[lhsT, rhs], outs=[out], signal=(stop if sig is None else sig))

    def tr(out, in_, idn):
        S.add("pe", lambda e, o=out, i=in_, d=idn: e.transpose(o, i, d), ins=[in_, idn], outs=[out])

    def act(out, in_, func, bias=None, scale=None, accum=None):
        kw = {}
        if bias is not None:
            kw["bias"] = bias
        if scale is not None:
            kw["scale"] = scale
        if accum is not None:
            kw["accum_out"] = accum
        S.add("act", lambda e, o=out, i=in_, f=func, k=kw: e.activation(out=o, in_=i, func=f, **k),
              ins=aps(in_, bias, scale), outs=aps(out, accum))

    def ts(out, in0, s1, s2, op0, op1=None, eng="dve"):
        if op1 is None:
            S.add(eng, lambda e, o=out, i=in0, a=s1, p=op0: e.tensor_scalar(out=o, in0=i, scalar1=a, scalar2=None, op0=p),
                  ins=aps(in0, s1), outs=[out])
        else:
            S.add(eng, lambda e, o=out, i=in0, a=s1, b=s2, p=op0, q=op1: e.tensor_scalar(out=o, in0=i, scalar1=a, scalar2=b, op0=p, op1=q),
                  ins=aps(in0, s1, s2), outs=[out])

    def tt(out, in0, in1, op, eng="dve"):
        S.add(eng, lambda e, o=out, a=in0, b=in1, p=op: e.tensor_tensor(out=o, in0=a, in1=b, op=p), ins=[in0, in1], outs=[out])

    def stt(out, in0, scalar, in1, op0, op1):
        S.add("dve", lambda e, o=out, a=in0, s=scalar, b=in1, p=op0, q=op1: e.scalar_tensor_tensor(out=o, in0=a, scalar=s, in1=b, op0=p, op1=q),
              ins=aps(in0, scalar, in1), outs=[out])

    def cp(out, in_, eng="dve"):
        if eng == "act":
            act(out, in_, AF.Identity)
        else:
            S.add(eng, lambda e, o=out, i=in_: e.tensor_copy(out=o, in_=i), ins=[in_], outs=[out])

    def ms(ap, val, eng="dve"):
        S.add(eng, lambda e, a=ap, v=val: e.memset(a, v), outs=[ap])

    def dma(eng, out, in_):
        S.add(eng, lambda e, o=out, i=in_: e.dma_start(out=o, in_=i), ins=[in_], outs=[out], dma=True)

    def red(out, in_, op):
        S.add("dve", lambda e, o=out, i=in_, p=op: e.tensor_reduce(out=o, in_=i, axis=AX.X, op=p), ins=[in_], outs=[out])

    def recip(out, in_):
        S.add("dve", lambda e, o=out, i=in_: e.reciprocal(out=o, in_=i), ins=[in_], outs=[out])

    ms(arena[:, 0:total // 8], 0.0, "dve")
    ms(arena[:, total // 8: total // 4], 0.0, "pool")
    dma("sp", par, par_d); dma("sp", ident32, ident_d); dma("sp", bkv, bkv_d); dma("sp", bi, bi_d)
    dma("sp", matt.rearrange("p a b -> p (a b)"), matt_d); dma("sp", cmask, cmask_d)
    dma("pool", biasg.rearrange("p a b -> p (a b)"), biasT_d[0:128, :])
    dma("pool", biasx.rearrange("p a b -> p (a b)"), biasT_d[128:256, :])
    dma("pool", biasx2.rearrange("p a b -> p (a b)"), biasT_d[256:384, :])
    dma("pool", vtok[:, 0, :], cv)
    cp(identbf, ident32)
    ms(ones[:, 0, :], 1.0 / 1024); ms(ones[:, 1, :], 1.0 / 512); ms(ones[:, 2, :], 1.0 / 128)
    l0 = par[:, PO["lbin"]: PO["lbin"] + 4]; l1 = par[:, PO["lbin"] + 4: PO["lbin"] + 8]
    e0 = lbw[:, 0:4]; e1 = lbw[:, 4:8]; sm = lbw[:, 8:12]; p0 = lbw[:, 12:16]; p1 = lbw[:, 16:20]
    lb = lbw[:, 20:24]; oml = lbw[:, 24:28]; mxl = lbw[:, 28:32]
    tt(mxl, l0, l1, ALU.max)
    tt(e0, l0, mxl, ALU.subtract); tt(e1, l1, mxl, ALU.subtract)
    act(e0, e0, AF.Exp); act(e1, e1, AF.Exp)
    tt(sm, e0, e1, ALU.add); recip(sm, sm)
    tt(p0, e0, sm, ALU.mult); tt(p1, e1, sm, ALU.mult)
    tt(lb, p0, p1, ALU.add); tt(lb, lb, p0, ALU.subtract)
    ts(oml, lb, -1.0, 1.0, ALU.mult, ALU.add)
    cast_seq = []
    for n in ["w0in", "w0o", "gu0", "dn0"]:
        cast_seq += [(n, gi) for gi in range(WSPEC[n][0])]
    cast_seq += [("w1in", gi) for gi in (1, 0, 2, 3, 4, 5)]
    for n in ["w1o", "gu1", "dn1"]:
        cast_seq += [(n, gi) for gi in range(WSPEC[n][0])]
    cast_pos = {c: i for i, c in enumerate(cast_seq)}
    cast_done = [0]

    def ensure_cast(upto):
        while cast_done[0] <= min(upto, len(cast_seq) - 1):
            n, gi = cast_seq[cast_done[0]]
            dma("pool", ws[n][gi * 128:(gi + 1) * 128, :], wd[n][gi * 128:(gi + 1) * 128, :])
            cast_done[0] += 1

    ensure_cast(len(cast_seq))
    ring_i = [0]

    def wl(name, gi):
        g, c = WSPEC[name]
        k = ring_i[0] % NSLOT
        ring_i[0] += 1
        dma("sp", ring[k][:, 0:c], ws[name][gi * 128:(gi + 1) * 128, :])
        return ring[k]

    def ln_block(nch, src, onesrow, gname, bname, dst32, dstbf, sqbuf, T, silu_out=None):
        h = nch // 2
        cp(tbf[:, 0:h, 0:T], src[:, 0:h, 0:T], "dve")
        cp(tbf[:, h:nch, 0:T], src[:, h:nch, 0:T], "dve")
        act(sqbuf[:, 0:h, 0:T], src[:, 0:h, 0:T], AF.Square)
        act(sqbuf[:, h:nch, 0:T], src[:, h:nch, 0:T], AF.Square)
        bS = nb(); bQ = nb()
        for k in range(nch):
            mm(bS[:, 0:T], ones[:, onesrow, :], tbf[:, k, 0:T], k == 0, k == nch - 1)
        for k in range(nch):
            mm(bQ[:, 0:T], ones[:, onesrow, :], sqbuf[:, k, 0:T], k == 0, k == nch - 1)
        mean = stt_[0][:, 0:T]; msq = stt_[1][:, 0:T]; rstd = stt_[2][:, 0:T]
        cp(mean, bS[:, 0:T], "dve")
        act(msq, bS[:, 0:T], AF.Square)
        tt(rstd, bQ[:, 0:T], msq, ALU.subtract)
        ts(rstd, rstd, 0.0, None, ALU.max)
        act(rstd, rstd, AF.Ln, bias=small[:, 200:201])
        act(rstd, rstd, AF.Exp, scale=-0.5)
        for k in range(nch):
            tt(src[:, k, 0:T], src[:, k, 0:T], mean, ALU.subtract)
            tt(src[:, k, 0:T], src[:, k, 0:T], rstd, ALU.mult)
            if silu_out is not None:
                act(silu_out[:, k, 0:T], src[:, k, 0:T], AF.Silu, bias=bname(k), scale=gname(k))
            else:
                act(dstbf[:, k, 0:T], src[:, k, 0:T], AF.Identity, bias=bname(k), scale=gname(k))
        if silu_out is None:
            for k in range(nch):
                act(dst32[:, k, 0:T], src[:, k, 0:T], AF.Identity, bias=bname(k), scale=gname(k))

    ms(small[:, 200:201], LN_EPS)
    ms(small[:, 201:202], RMS_EPS)

    def ffn_block(layer, T):
        gu = f"gu{layer}"; dn = f"dn{layer}"
        for j in range(22):
            if j % 2 == 0:
                w = wl(gu, j // 2)
                wv = w[:, 0:4096].rearrange("p (k n) -> p k n", k=8)
            off = (j % 2) * 256
            bG = nb(); bU = nb()
            for k in range(8):
                mm(bG[:, 0:T], wv[:, k, off:off + 128], hbf[:, k, 0:T], k == 0, k == 7)
            for k in range(8):
                mm(bU[:, 0:T], wv[:, k, off + 128:off + 256], hbf[:, k, 0:T], k == 0, k == 7)
            sil = stt_[j % 2][:, 0:T]
            act(sil, bG[:, 0:T], AF.Silu)
            tt(actb[:, j, 0:T], sil, bU[:, 0:T], ALU.mult)
        for m in range(8):
            w = wl(dn, m)
            wv = w[:, 0:2816].rearrange("p (k n) -> p k n", k=22)
            b = nb()
            for k in range(22):
                mm(b[:, 0:T], wv[:, k, :], actb[:, k, 0:T], k == 0, k == 21)
            stt(r32[:, m, 0:T], h32[:, m, 0:T], ALPHA, b[:, 0:T], ALU.mult, ALU.add)
        lo = PO["ln"] + layer * 32
        ln_block(8, r32, 0, lambda k: par[:, lo + 16 + k: lo + 17 + k], lambda k: par[:, lo + 24 + k: lo + 25 + k],
                 h32, hbf, mixbf, T)

    def wo_block(name, layer, T):
        for m in range(8):
            if m % 4 == 0:
                w = wl(name, m // 4)
                wv = w[:, 0:4096].rearrange("p (k n) -> p k n", k=8)
            off = (m % 4) * 128
            b = nb()
            for k in range(8):
                mm(b[:, 0:T], wv[:, k, off:off + 128], mixbf[:, k, 0:T], k == 0, k == 7)
            stt(r32[:, m, 0:T], h32[:, m, 0:T], ALPHA, b[:, 0:T], ALU.mult, ALU.add)
        lo = PO["ln"] + layer * 32
        ln_block(8, r32, 0, lambda k: par[:, lo + k: lo + 1 + k], lambda k: par[:, lo + 8 + k: lo + 9 + k],
                 h32, hbf, mixbf, T)

    def state_rows_out(src, c0, dst, r0, nrows):
        b = nb()
        for c in range(4):
            tr(b[0:32, c * 128:(c + 1) * 128], src[:, c, c0:c0 + 32], ident32[:, :])
        cp(stg[0:32, :], b[0:32, :], "dve")
        dma("pool", dst, stg[r0:r0 + nrows, :])

    def load_x(kind, ti):
        if kind == "sp":
            dma("pool", xin[0:64, 0, :], xs)
            ms(xin[64:128, 0, :], 0.0)
            dma("pool", xin[112:128, 0, :], meta)
        elif kind == "sample":
            dma("pool", xin[0:64, 0, :], xs)
        elif kind == "prefix":
            ms(xin[0:64, 0, :], 0.0)
            dma("pool", xin[48:64, 0, :], meta)
        else:
            dma("pool", xin[:, :, :], xp[ti * 512:(ti + 1) * 512, :].rearrange("(s p) d -> p s d", p=128))

    def tile_pass(kind, ti, pre_loaded=False, nxt=None):
        T = 512 if kind == "main" else (128 if kind == "sp" else 64)
        SEG = 64 if kind == "sp" else T
        is_s = kind in ("sample", "sp")
        NS = max(1, T // 128)
        Pt = min(T, 128)
        last = (kind == "main" and ti == n_main_tiles - 1)
        if not pre_loaded:
            load_x(kind, ti)
        for c in range(8):
            b = nb()
            for s in range(NS):
                tr(b[:, s * 128:s * 128 + Pt], xin[0:Pt, s, c * 128:(c + 1) * 128], ident32[0:Pt, 0:Pt])
            cp(h32[:, c, 0:T], b[:, 0:T], "dve")
            cp(hbf[:, c, 0:T], b[:, 0:T], "act")
        stage(2)
        bg, u32, cc32 = G[0], G[1], G[2]
        if is_s:
            dma("sp", uh0.rearrange("p a b -> p (a b)"), scb)
            dma("sp", r32[:, 0, 0:256], ckd)
            for g in range(2):
                b = nb()
                tr(b[:, 0:128], r32[:, 0, g * 128:(g + 1) * 128], ident32[:, :])
                cp(kdT[:, g, 0:128], b[:, 0:128], "act")
        elif kind == "prefix":
            ms(uh0, 0.0)
        cp(u32[:, :, 0:2], uh0)
        stage(2.1)
        wg = {}

        def w0(gi):
            if gi not in wg:
                wg.clear()
                wg[gi] = wl("w0in", gi)[:, 0:4096].rearrange("p (k n) -> p k n", k=8)
            return wg[gi]

        for m in range(18):
            wv = w0(m // 4); off = (m % 4) * 128
            b = nb()
            for k in range(8):
                mm(b[:, 0:T], wv[:, k, off:off + 128], hbf[:, k, 0:T], k == 0, k == 7)
            bc = pc("b0", m)
            if m < 4:
                ts(qT[:, m, 0:T], b[:, 0:T], bc, 0.125, ALU.add, ALU.mult)
            elif m < 6:
                act(kdT[:, m - 4, 128:128 + T], b[:, 0:T], AF.Identity, bias=bc)
            elif m < 10:
                act(bg[:, m - 6, 0:T], b[:, 0:T], AF.Identity, bias=bc)
            elif m < 14:
                act(u32[:, m - 10, 2:2 + T], b[:, 0:T], AF.Identity, bias=bc)
            else:
                stt(u32[:, m - 14, 2:2 + T], b[:, 0:T], bc, u32[:, m - 14, 2:2 + T], ALU.add, ALU.mult)
        stage(2.3)
        wv = w0(4)
        for s in range(NS):
            b = nb()
            for k in range(8):
                mm(b[0:Pt, 0:256], hbf[:, k, s * 128:s * 128 + Pt], wv[:, k, 256:512], k == 0, k == 7)
            tt(vtok[0:Pt, 1 + s, :], b[0:Pt, 128:256], bkv[0:Pt, 128:256], ALU.add)
            if last and s == NS - 1 or is_s:
                tt(kvout[0:Pt, :], b[0:Pt, 0:256], bkv[0:Pt, :], ALU.add)
        if kind == "prefix":
            b = nb()
            for k in range(8):
                mm(b[64:128, 0:256], hbf[:, k, 0:64], wv[:, k, 256:512], k == 0, k == 7)
            tt(vtok[64:128, 0, :], b[64:128, 128:256], bkv[64:128, 128:256], ALU.add)
            ms(u32[:, :, 0:50], 0.0)
        if kind == "sp":
            ms(u32[:, :, 2 + 64:2 + 112], 0.0)
        stage(2.5)
        if is_s:
            dma("sp", stg[0:64, 0:128], ck[64:128, :]); dma("sp", stg[0:64, 128:256], cv[64:128, :])
            dma("pool", kso[0:64, :], stg[0:64, 0:128]); dma("pool", vso[0:64, :], stg[0:64, 128:256])
            dma("pool", kso[64:128, :], kvout[0:64, 0:128]); dma("pool", vso[64:128, :], kvout[0:64, 128:256])
        stage(2.7)
        if last:
            dma("pool", kp, kvout[:, 0:128]); dma("pool", vp, kvout[:, 128:256])
        stage(3)
        nq = NS

        def s_phase(j, g):
            bt = biasx if kind in ("prefix", "sp") else (biasx2 if (kind == "main" and ti == 0 and j == 0) else biasg)
            banks = [nb(), nb()]
            for hh in range(4):
                h = 4 * g + hh; c = h // 2; hf = h % 2
                o = banks[hh // 2][0:Pt, (hh % 2) * 256:(hh % 2) * 256 + 256]
                mm(o, qT[64 * hf:64 * hf + 64, c, j * 128:j * 128 + Pt], kdT[64 * hf:64 * hf + 64, g, 128 * j:128 * j + 256], True, False)
                mm(o, identbf[:, 0:Pt], bt[:, h, :], False, True)
            return banks

        def rest_phase(j, g, banks):
            mx = small[0:Pt, 0:4]; mneg = small[0:Pt, 4:8]; ssum = small[0:Pt, 8:12]; esk = small[0:Pt, 12:16]
            for q in range(2):
                red(mx[:, 2 * q:2 * q + 2], banks[q][0:Pt, :].rearrange("p (h k) -> p h k", h=2), ALU.max)
            sk = par[0:Pt, PO["sink"] + 4 * g: PO["sink"] + 4 * g + 4]
            tt(mx, mx, sk, ALU.max)
            ts(mneg, mx, -1.0, None, ALU.mult)
            for hh in range(4):
                act(P32[0:Pt, hh, :], banks[hh // 2][0:Pt, (hh % 2) * 256:(hh % 2) * 256 + 256], AF.Exp,
                    bias=mneg[:, hh:hh + 1], accum=ssum[:, hh:hh + 1])
            tt(esk, sk, mx, ALU.subtract)
            act(esk, esk, AF.Exp)
            tt(ssum, ssum, esk, ALU.add)
            recip(ssum, ssum)
            for hh in range(4):
                ts(Pn32[0:Pt, hh, :], P32[0:Pt, hh, :], ssum[:, hh:hh + 1], None, ALU.mult)
            tb = [nb(), nb()]
            for hh in range(4):
                for kb in range(2):
                    idx = hh * 2 + kb
                    tr(tb[idx // 4][:, (idx % 4) * 128:(idx % 4) * 128 + Pt], Pn32[0:Pt, hh, kb * 128:(kb + 1) * 128], ident32[0:Pt, 0:Pt])
            for q in range(2):
                src = tb[q][:, :].rearrange("p (a b) -> p a b", a=4)[:, :, 0:Pt]
                cp(PTb[:, 4 * q:4 * q + 4, 0:Pt], src, "act" if q else "dve")
            for pr in range(2):
                ob = nb()
                for hf in range(2):
                    hh = pr * 2 + hf
                    for kb in range(2):
                        mm(ob[64 * hf:64 * hf + 64, 0:Pt], vtok[:, j + kb, g * 64:(g + 1) * 64], PTb[:, hh * 2 + kb, 0:Pt], kb == 0, kb == 1)
                cp(mixbf[:, 2 * g + pr, j * 128:j * 128 + Pt], ob[:, 0:Pt], "act")

        units = [(j, g) for j in range(nq) for g in range(2)]
        prev = None
        for u in units:
            bk = s_phase(*u)
            if prev is not None:
                rest_phase(*prev)
            prev = (u[0], u[1], bk)
        rest_phase(*prev)
        if kind == "prefix":
            cp(kdT[:, :, 64:128], kdT[:, :, 128:192])
        elif kind == "sp":
            cp(kdT[:, :, 64:128], kdT[:, :, 192:256])
            cp(vtok[64:128, 0, :], vtok[64:128, 1, :], "dve")
        elif kind == "main":
            cp(kdT[:, :, 0:128], kdT[:, :, 512:640])
            cp(vtok[:, 0, :], vtok[:, 4, :], "dve")
        stage(4)
        for c in range(4):
            ts(cc32[:, c, 0:T], u32[:, c, 0:T], pc("wB", c * 3), None, ALU.mult)
            stt(cc32[:, c, 0:T], u32[:, c, 1:1 + T], pc("wB", c * 3 + 1), cc32[:, c, 0:T], ALU.mult, ALU.add)
            stt(cc32[:, c, 0:T], u32[:, c, 2:2 + T], pc("wB", c * 3 + 2), cc32[:, c, 0:T], ALU.mult, ALU.add)
            tt(mixbf[:, 4 + c, 0:T], bg[:, c, 0:T], cc32[:, c, 0:T], ALU.mult)
        cp(uh0, u32[:, :, T:T + 2])
        if is_s:
            state_rows_out(u32, SEG + 2 - 32, cbs, 30, 2)
        if last:
            state_rows_out(u32, T + 2 - 32, cbp, 30, 2)
        stage(5)
        wo_block("w0o", 0, T)
        stage(6)
        ffn_block(0, T)
        stage(7)
        uc32, sg32, c32, q32, gate32 = G[0], G[1], G[2], G[3], G[4]
        if is_s:
            dma("sp", uh1.rearrange("p a b -> p (a b)"), scc)
            dma("sp", S32.rearrange("p a b -> p (a b)"), shg)
        elif kind == "prefix":
            ms(uh1, 0.0)
            ms(S32, 0.0)
        cp(uc32[:, :, 0:30], uh1)
        order = [4, 5, 6, 7, 0, 1, 2, 3] + list(range(8, 20))
        wg1 = {}

        def w1(gi):
            if gi not in wg1:
                wg1.clear()
                wg1[gi] = wl("w1in", gi)[:, 0:4096].rearrange("p (k n) -> p k n", k=8)
            return wg1[gi]

        for m in order:
            wv = w1(m // 4); off = (m % 4) * 128; c = m % 4
            b = nb()
            for k in range(8):
                mm(b[:, 0:T], wv[:, k, off:off + 128], hbf[:, k, 0:T], k == 0, k == 7)
            bc = pc("b1", m)
            if m < 4:
                stt(uc32[:, c, 30:30 + T], b[:, 0:T], bc, c32[:, c, 0:T], ALU.add, ALU.mult)
            elif m < 8:
                act(c32[:, c, 0:T], b[:, 0:T], AF.Sigmoid, bias=bc)
            elif m < 12:
                act(q32[:, c, 0:T], b[:, 0:T], AF.Identity, bias=bc)
            elif m < 16:
                act(sg32[:, c, 0:T], b[:, 0:T], AF.Sigmoid, bias=bc)
            else:
                act(gate32[:, c, 0:T], b[:, 0:T], AF.Silu, bias=bc)
                ts(gate32[:, c, 0:T], gate32[:, c, 0:T], pc("normg", c), None, ALU.mult)
        wv = w1(5)
        for s in range(NS):
            b = nb()
            for k in range(8):
                mm(b[0:Pt, 0:512], hbf[:, k, s * 128:s * 128 + Pt], wv[:, k, 0:512], k == 0, k == 7)
            tt(vtok1[0:Pt, s, :], b[0:Pt, 0:512], bi[0:Pt, :], ALU.add)
        if kind == "prefix":
            ms(uc32[:, :, 0:78], 0.0)
        if kind == "sp":
            ms(uc32[:, :, 30 + 64:30 + 112], 0.0)
        stage(8)
        cp(ucbf[:, :, 0:30 + T], uc32[:, :, 0:30 + T], "act")
        cp(uh1, uc32[:, :, T:T + 30])
        if is_s:
            state_rows_out(uc32, SEG + 30 - 32, ccs, 2, 30)
        if last:
            state_rows_out(uc32, T + 30 - 32, ccp, 2, 30)
        di = [0]
        for c in range(4):
            b = nb()
            for j in range(31):
                d = dg[:, di[0] % 8, :]
                di[0] += 1
                if j % 2:
                    act(d, identbf, AF.Identity, scale=pc("wC", c * 31 + j))
                else:
                    ts(d, identbf, pc("wC", c * 31 + j), None, ALU.mult)
                mm(b[:, 0:T], d, ucbf[:, c, j:j + T], j == 0, j == 30, sig=True)
            act(c32[:, c, 0:T], b[:, 0:T], AF.Identity, bias=pc("ccb", c))
        ln_block(4, c32, 1, lambda k: pc("clng", k), lambda k: pc("clnb", k), None, None, mixbf[:, 4:8, :], T,
                 silu_out=mixbf)
        stage(9)
        lf32 = G[0]; cum32 = G[2]
        for c in range(4):
            ts(sg32[:, c, 0:T], sg32[:, c, 0:T], oml[:, c:c + 1], lb[:, c:c + 1], ALU.mult, ALU.add)
        act(lf32[:, :, 0:T], sg32[:, :, 0:T], AF.Ln)
        ts(sg32[:, :, 0:T], sg32[:, :, 0:T], -1.0, 1.0, ALU.mult, ALU.add)
        for c in range(4):
            S.add("dve", lambda e, o=cum32[:, c, 0:T], d0=cmask[:, 0:T], d1=lf32[:, c, 0:T]:
                  e.tensor_tensor_scan(out=o, data0=d0, data1=d1, initial=0.0, op0=ALU.mult, op1=ALU.add),
                  ins=[cmask[:, 0:T], lf32[:, c, 0:T]], outs=[cum32[:, c, 0:T]])
        NCH = T // 64
        etot = small[:, 32:32 + 4 * 8].rearrange("p (a b) -> p a b", a=4)
        act(etot[:, :, 0:NCH], cum32[:, :, 63:T:64], AF.Exp)
        act(lf32[:, :, 0:T], cum32[:, :, 0:T], AF.Exp)
        tt(qT[:, :, 0:T], q32[:, :, 0:T], lf32[:, :, 0:T], ALU.mult)
        act(lf32[:, :, 0:T], cum32[:, :, 0:T], AF.Exp, scale=-1.0)
        tt(sg32[:, :, 0:T], sg32[:, :, 0:T], lf32[:, :, 0:T], ALU.mult)
        kd32 = G[0]
        for c in range(4):
            tt(kd32[:, c, 0:T].rearrange("p (a b) -> p a b", b=64), sg32[:, c, 0:T].rearrange("p (a b) -> p a b", b=64),
               etot[:, c, 0:NCH].unsqueeze(2).to_broadcast([128, NCH, 64]), ALU.mult)
        cp(kebf[:, :, 0:T], sg32[:, :, 0:T], "act")
        if kind == "prefix":
            ms(kebf[:, :, 0:48], 0.0)
            ms(kd32[:, :, 0:48], 0.0)
        if kind == "sp":
            ms(kebf[:, :, 64:112], 0.0)
            ms(kd32[:, :, 64:112], 0.0)
        for s in range(NS):
            b = nb()
            for c in range(4):
                tr(b[0:Pt, c * 128:(c + 1) * 128], kd32[:, c, s * 128:s * 128 + Pt], ident32[:, :])
            cp(kdtok[0:Pt, s, :], b[0:Pt, :], "act")
        for c in range(4):
            b = nb()
            for ch in range(NCH):
                po = (ch % 2) * 64; s = ch // 2
                mm(b[po:po + 64, s * 64:(s + 1) * 64], kebf[:, c, ch * 64:(ch + 1) * 64], qT[:, c, ch * 64:(ch + 1) * 64])
            if NCH == 1:
                tt(atbf[0:64, c, 0, :], b[0:64, 0:64], matt[0:64, 0, :], ALU.mult)
            elif NCH == 2:
                tt(atbf[:, c, 0, :], b[:, 0:64], matt[:, 0, :], ALU.mult)
            else:
                tt(atbf[:, c, :, :], b[:, 0:256].rearrange("p (a b) -> p a b", a=4), matt[:, :, :], ALU.mult)
        cp(Sbf[:, 0, :, :], S32, "act")
        for ch in range(NCH):
            po = (ch % 2) * 64; s = ch // 2
            b = nb()
            for c in range(4):
                mm(b[:, c * 128:(c + 1) * 128], kdtok[po:po + 64, s, c * 128:(c + 1) * 128], vtok1[po:po + 64, s, c * 128:(c + 1) * 128])
            for c in range(4):
                stt(S32[:, c, :], S32[:, c, :], etot[:, c, ch:ch + 1], b[:, c * 128:(c + 1) * 128], ALU.mult, ALU.add)
            if kind == "sp" and ch == 0:
                dma("pool", hgs.rearrange("h k v -> k h v"), S32)
                ms(S32, 0.0)
            cp(Sbf[:, ch + 1, :, :], S32, "act")
        if kind == "sample":
            dma("pool", hgs.rearrange("h k v -> k h v"), S32)
        if last:
            dma("pool", hgp.rearrange("h k v -> k h v"), S32)
        o32 = G[3]
        for c in range(4):
            b = nb()
            for ch in range(NCH):
                po = (ch % 2) * 64; s = ch // 2
                o = b[:, ch * 64:(ch + 1) * 64]
                mm(o, Sbf[:, ch, c, :], qT[:, c, ch * 64:(ch + 1) * 64], True, False)
                mm(o, vtok1[po:po + 64, s, c * 128:(c + 1) * 128], atbf[po:po + 64, c, s, :], False, True)
            cp(o32[:, c, 0:T], b[:, 0:T], "dve")
        act(tbf[:, 0:4, 0:T], o32[:, :, 0:T], AF.Square)
        for c in range(4):
            b = nb()
            mm(b[:, 0:T], ones[:, 2, :], tbf[:, c, 0:T])
            rs = stt_[c % 3][:, 0:T]
            act(rs, b[:, 0:T], AF.Ln, bias=small[:, 201:202])
            act(rs, rs, AF.Exp, scale=-0.5)
            tt(o32[:, c, 0:T], o32[:, c, 0:T], rs, ALU.mult)
            tt(mixbf[:, 4 + c, 0:T], o32[:, c, 0:T], gate32[:, c, 0:T], ALU.mult)
        stage(10)
        wo_block("w1o", 1, T)
        ffn_block(1, T)
        if nxt is not None:
            load_x(*nxt)
        stage(11)
        if kind != "prefix":
            pass
        if kind != "prefix":
            for s in range(NS):
                for q in range(2):
                    b = nb()
                    for cc in range(4):
                        c = q * 4 + cc
                        tr(b[0:Pt, cc * 128:(cc + 1) * 128], h32[:, c, s * 128:s * 128 + Pt], ident32[:, :])
                    cp(xstage[0:Pt, s, q * 512:(q + 1) * 512], b[0:Pt, :], "act" if q else "dve")
            if is_s:
                dma("pool", ys, xstage[0:64, 0, :])
            else:
                dma("pool", yp[ti * 512:(ti + 1) * 512, :].rearrange("(s p) d -> p s d", p=128), xstage[:, :, :])

    def stage(n):
        if n > stop_at:
            raise _Stop()

    try:
        stage(1)
        seq = [("sp", 0)] + [("main", i) for i in range(n_main_tiles)]
        for i, (kd_, ti_) in enumerate(seq):
            tile_pass(kd_, ti_, pre_loaded=(i > 0), nxt=(seq[i + 1] if i + 1 < len(seq) else None))
    except _Stop:
        pass
    S.finish()
    S.emit(nc)
    st.close()
    return nc

from concourse.bass_utils import run_bass_kernel_spmd

_CACHE = {}


def _tile_w(W, gn):
    K, N = W.shape
    kc = K // 128
    g = N // gn
    return np.ascontiguousarray(W.reshape(kc, 128, g, gn).transpose(2, 1, 0, 3).reshape(g * 128, kc * gn))


def _cols(v, n):
    return np.ascontiguousarray(np.asarray(v, np.float32).reshape(n, 128).T)


def _consts():
    ident = np.eye(128, dtype=np.float32)
    slopes = np.exp2(-8.0 * np.arange(1, 9, dtype=np.float32) / 8).astype(np.float32)
    r = np.arange(128)[:, None]
    c = np.arange(256)[None, :]
    dist = np.abs(128 + r - c).astype(np.float32)
    base_mask = np.zeros((128, 256), bool)
    base_mask[:64, 192:] = True
    base_mask[64:, :64] = True
    m_first = base_mask.copy(); m_first[64:, :240] = True
    m_t0 = base_mask.copy(); m_t0[:, :112] = True
    tabs = []
    for msk in (base_mask, m_first, m_t0):
        t = np.zeros((128, 8, 256), np.float32)
        for h in range(8):
            t[:, h, :] = np.where(msk, np.float32(NEG), -slopes[h] * dist)
        tabs.append(t.reshape(128, 2048))
    biasT = np.concatenate(tabs, 0)
    p = np.arange(128)[:, None] % 64
    t = np.arange(64)[None, :]
    matt = np.tile((p <= t).astype(np.float32)[:, None, :], (1, 4, 1)).reshape(128, 256)
    cm = np.ones((128, 512), np.float32)
    cm[:, ::64] = 0.0
    return ident, biasT, np.ascontiguousarray(matt), cm


def kernel(x_prompt, x_sample, cache_k_a, cache_v_a, state_conv_b, state_conv_c, state_hgrn,
           meta_tokens, ab_w_in, ab_b_in, a_sinks, b_conv_w, ab_w_o, cd_w_in, cd_b_in,
           c_conv_w, c_conv_b, c_ln_g, c_ln_b, d_lower_bounds, d_norm_g, cd_w_o,
           ln1_g, ln1_b, ln2_g, ln2_b, ffn_w_gu, ffn_w_down):
    f = lambda a: np.asarray(a, np.float32)
    x_prompt, x_sample = f(x_prompt), f(x_sample)
    ab_w_in, ab_b_in = f(ab_w_in), f(ab_b_in)
    cd_w_in, cd_b_in = f(cd_w_in), f(cd_b_in)
    q0, k0, v0, bg0, cg0, hb0 = 0, 512, 640, 768, 1280, 1792
    kd_idx = np.concatenate([np.arange(k0, k0 + 64), np.arange(k0, k0 + 64), np.arange(k0 + 64, k0 + 128), np.arange(k0 + 64, k0 + 128)])
    colsA = np.concatenate([np.arange(q0, q0 + 512), kd_idx, np.arange(bg0, bg0 + 512), np.arange(hb0, hb0 + 512),
                            np.arange(cg0, cg0 + 512)])
    colsB = np.concatenate([np.arange(k0, k0 + 128), np.arange(v0, v0 + 128)])
    w0in = _tile_w(ab_w_in[:, np.concatenate([colsA, colsB])], 512)
    b0 = _cols(ab_b_in[colsA], 18)
    bkv = np.ascontiguousarray(np.tile(ab_b_in[colsB][None, :], (128, 1)))
    c1 = np.concatenate([np.arange(0, 2048), np.arange(2560, 3072), np.arange(2048, 2560)])
    w1in = _tile_w(cd_w_in[:, c1], 512)
    b1 = _cols(cd_b_in[c1[:2560]], 20)
    bi = np.ascontiguousarray(np.tile(cd_b_in[2048:2560][None, :], (128, 1)))
    gu_idx = np.stack([np.arange(2816).reshape(22, 128), 2816 + np.arange(2816).reshape(22, 128)], 1).reshape(-1)
    wts = {"w0in": w0in, "w0o": _tile_w(f(ab_w_o), 512), "w1in": w1in, "w1o": _tile_w(f(cd_w_o), 512)}
    for l in range(2):
        wts[f"gu{l}"] = _tile_w(f(ffn_w_gu)[l][:, gu_idx], 512)
        wts[f"dn{l}"] = _tile_w(f(ffn_w_down)[l], 128)
    par = np.zeros((128, NPAR), np.float32)
    par[:, PO["b0"]:PO["b0"] + 18] = b0
    par[:, PO["wB"]:PO["wB"] + 12] = f(b_conv_w).reshape(3, 4, 128).transpose(2, 1, 0).reshape(128, 12)
    for l in range(2):
        lo = PO["ln"] + l * 32
        par[:, lo:lo + 8] = _cols(f(ln1_g)[l], 8); par[:, lo + 8:lo + 16] = _cols(f(ln1_b)[l], 8)
        par[:, lo + 16:lo + 24] = _cols(f(ln2_g)[l], 8); par[:, lo + 24:lo + 32] = _cols(f(ln2_b)[l], 8)
    par[:, PO["b1"]:PO["b1"] + 20] = b1
    par[:, PO["wC"]:PO["wC"] + 124] = f(c_conv_w).reshape(31, 4, 128).transpose(2, 1, 0).reshape(128, 124)
    par[:, PO["ccb"]:PO["ccb"] + 4] = _cols(c_conv_b, 4)
    par[:, PO["clng"]:PO["clng"] + 4] = _cols(c_ln_g, 4)
    par[:, PO["clnb"]:PO["clnb"] + 4] = _cols(c_ln_b, 4)
    par[:, PO["lbin"]:PO["lbin"] + 4] = _cols(f(d_lower_bounds)[0], 4)
    par[:, PO["lbin"] + 4:PO["lbin"] + 8] = _cols(f(d_lower_bounds)[1], 4)
    par[:, PO["normg"]:PO["normg"] + 4] = _cols(d_norm_g, 4)
    par[:, PO["sink"]:PO["sink"] + 8] = np.tile(f(a_sinks)[None, :], (128, 1))
    ident, biasT, matt, cm = _consts()
    common = {"meta": f(meta_tokens), "par": par, "bkv": bkv, "bi": bi, "ident": ident, "biasT": biasT,
              "matt": matt, "cmask": cm}
    common.update(wts)
    in_maps = []
    for c in range(8):
        ckc = f(cache_k_a)[c].reshape(128, 2, 64)
        m = dict(common)
        m["xp"] = np.ascontiguousarray(x_prompt[c % 4])
        m["xs"] = np.ascontiguousarray(x_sample[c])
        m["ckd"] = np.ascontiguousarray(np.concatenate([ckc[:, 0], ckc[:, 0], ckc[:, 1], ckc[:, 1]], 1))
        m["ck"] = np.ascontiguousarray(ckc.reshape(128, 128))
        m["cv"] = np.ascontiguousarray(f(cache_v_a)[c].reshape(128, 128))
        m["scb"] = np.ascontiguousarray(f(state_conv_b)[c].reshape(2, 4, 128).transpose(2, 1, 0).reshape(128, 8))
        m["scc"] = np.ascontiguousarray(f(state_conv_c)[c].reshape(30, 4, 128).transpose(2, 1, 0).reshape(128, 120))
        m["shg"] = np.ascontiguousarray(f(state_hgrn)[c].transpose(1, 0, 2).reshape(128, 512))
        in_maps.append(m)
    if "nc" not in _CACHE:
        nc = bass.Bass("TRN2", target_bir_lowering=False)
        build_program(nc)
        _CACHE["nc"] = nc
    res = run_bass_kernel_spmd(_CACHE["nc"], in_maps, core_ids=list(range(8)))
    R = res.results
    yp = np.stack([R[b]["yp"] for b in range(4)]).astype(np.float32)
    ys = np.stack([R[c]["ys"] for c in range(8)]).astype(np.float32)
    st4 = lambda k, shp: np.stack([R[b][k] for b in range(4)]).reshape(shp).astype(np.float32)
    st8 = lambda k, shp: np.stack([R[c][k] for c in range(8)]).reshape(shp).astype(np.float32)
    return (yp, ys, st4("kp", (4, 128, 2, 64)), st4("vp", (4, 128, 2, 64)), st4("cbp", (4, 2, 512)),
            st4("ccp", (4, 30, 512)), st4("hgp", (4, 4, 128, 128)),
            st8("ks", (8, 128, 2, 64)), st8("vs", (8, 128, 2, 64)), st8("cbs", (8, 2, 512)),
            st8("ccs", (8, 30, 512)), st8("hgs", (8, 4, 128, 128)))
```

```python
import numpy as np
import concourse.bass as bass
import concourse.mybir as mybir

F32 = mybir.dt.float32
BF16 = mybir.dt.bfloat16
AF = mybir.ActivationFunctionType
ALU = mybir.AluOpType
AX = mybir.AxisListType

_DS = {F32: 4, BF16: 2}


def _rng(ap):
    t = ap.tensor
    ds = _DS.get(ap.dtype, 4)
    a = ap.ap
    off = int(ap.offset)
    sp = str(ap.space)
    if "PSUM" in sp.upper():
        return (t.name, 0, 1 << 30, 0, 128)
    if "DRAM" in sp.upper() or "HBM" in sp.upper():
        span = 1
        for st, n in a:
            span += abs(st) * (n - 1)
        return (t.name, off * ds, (off + span) * ds, 0, 1)
    pst, pn = a[0]
    if pst == 0:
        pst = 1 << 40
    if pn > 1 or True:
        tp = 1
        for s in list(t.shape)[1:]:
            tp *= s
        tds = _DS.get(t.dtype, 4)
        tp = tp * tds // ds
    plo = off // tp
    fo = off % tp
    span = 1
    for st, n in a[1:]:
        span += abs(st) * (n - 1)
    return (t.name, fo * ds, (fo + span) * ds, plo, plo + pn)


class Sched:
    ENG = ["pe", "act", "dve", "pool", "sp"]
    R = 12
    RQ = {'sp': 12, 'pool': 2, 'act': 4}

    def __init__(self):
        self.ops = {e: [] for e in self.ENG}
        self.count = {e: 0 for e in self.ENG}
        self.seen = {e: {} for e in self.ENG}
        self.recs = {}
        self.dma_idx = {"sp": 0, "pool": 0, "act": 0}
        self.all_dma = {}

    def _deps(self, ins, outs, eng=None):
        deps = {}

        def add(tk):
            if tk is None:
                return
            s, v = tk
            if deps.get(s, 0) < v:
                deps[s] = v

        for ap in ins:
            n, lo, hi, pl, ph = _rng(ap)
            for r in self.recs.get(n, ()):
                if r[0] < hi and lo < r[1] and r[2] < ph and pl < r[3]:
                    add(r[4])
                    if hi == 1 << 30:
                        for s, v in r[5].items():
                            if s != eng:
                                add((s, v))
        for ap in outs:
            n, lo, hi, pl, ph = _rng(ap)
            for r in self.recs.get(n, ()):
                if r[0] < hi and lo < r[1] and r[2] < ph and pl < r[3]:
                    add(r[4])
                    for s, v in r[5].items():
                        add((s, v))
        return deps

    def _split(self, n, lo, hi, pl, ph):
        lst = self.recs.get(n)
        if not lst:
            return
        out = []
        for r in lst:
            if r[0] < hi and lo < r[1] and pl <= r[2] and r[3] <= ph and (r[0] < lo or hi < r[1]):
                if r[0] < lo:
                    out.append([r[0], lo, r[2], r[3], r[4], dict(r[5])])
                out.append([max(r[0], lo), min(r[1], hi), r[2], r[3], r[4], dict(r[5])])
                if hi < r[1]:
                    out.append([hi, r[1], r[2], r[3], r[4], dict(r[5])])
            else:
                out.append(r)
        self.recs[n] = out

    def _update(self, ins, outs, tk):
        for ap in ins:
            n, lo, hi, pl, ph = _rng(ap)
            self._split(n, lo, hi, pl, ph)
            hit = False
            for r in self.recs.get(n, ()):
                if r[0] < hi and lo < r[1] and r[2] < ph and pl < r[3]:
                    if r[5].get(tk[0], 0) < tk[1]:
                        r[5][tk[0]] = tk[1]
                    hit = True
            if not hit:
                self.recs.setdefault(n, []).append([lo, hi, pl, ph, None, {tk[0]: tk[1]}])
        for ap in outs:
            n, lo, hi, pl, ph = _rng(ap)
            self._split(n, lo, hi, pl, ph)
            lst = self.recs.setdefault(n, [])
            lst[:] = [r for r in lst if not (lo <= r[0] and r[1] <= hi and pl <= r[2] and r[3] <= ph)]
            lst.append([lo, hi, pl, ph, tk, {}])

    def add(self, eng, fn, ins=(), outs=(), dma=False, signal=True):
        deps = self._deps(ins, outs, eng)
        if dma:
            i = self.dma_idx[eng]
            self.dma_idx[eng] = i + 1
            R = self.RQ[eng]
            k = i % R
            sem = f"{eng}_d{k}"
            val = 16 * (i // R + 1)
            if val > 16:
                if deps.get(sem, 0) < val - 16:
                    deps[sem] = val - 16
            tk = (sem, val)
            inc = (sem, 16)
            self.all_dma[sem] = val
        elif not signal:
            tk = (eng, self.count[eng] + 1)
            inc = None
        else:
            self.count[eng] += 1
            tk = (eng, self.count[eng])
            inc = (eng, 1)
        waits = []
        seen = self.seen[eng]
        for s, v in deps.items():
            if s == eng and eng == "pe":
                continue
            if seen.get(s, 0) >= v:
                continue
            seen[s] = v
            waits.append((s, v))
        self.ops[eng].append((waits, fn, inc))
        self._update(ins, outs, tk)
        return tk

    def finish(self):
        waits = []
        for s, v in self.all_dma.items():
            waits.append((s, v))
        for e in ["pe", "act", "dve", "pool"]:
            if self.count[e]:
                waits.append((e, self.count[e]))
        self.ops["sp"].append((waits, None, None))

    def sem_names(self):
        names = ["pe", "act", "dve", "pool"]
        for q in ("sp", "pool", "act"):
            n = min(self.dma_idx[q], self.RQ[q])
            names += [f"{q}_d{k}" for k in range(n)]
        return names

    def emit(self, nc):
        import contextlib
        names = self.sem_names()
        with contextlib.ExitStack() as st:
            sems = {n: st.enter_context(nc.semaphore(n)) for n in names}
            block = st.enter_context(nc.Block())

            def run(e, key):
                for waits, fn, inc in self.ops[key]:
                    for s, v in waits:
                        e.wait_ge(sems[s], v)
                    if fn is None:
                        continue
                    ins = fn(e)
                    if inc is not None:
                        ins.then_inc(sems[inc[0]], inc[1])

            @block.tensor
            def _(e):
                run(e, "pe")

            @block.scalar
            def _(e):
                run(e, "act")

            @block.vector
            def _(e):
                run(e, "dve")

            @block.gpsimd
            def _(e):
                run(e, "pool")

            @block.sync
            def _(e):
                run(e, "sp")

import contextlib

D = 1024
NH = 8
ALPHA = 4 ** 0.25
LN_EPS = 1e-5
RMS_EPS = 1e-6
NEG = -1e30
SLOT = 4096
NSLOT = 4

PO = {}
_o = 0
for _n, _w in [("b0", 18), ("wB", 12), ("ln", 64), ("b1", 20), ("wC", 124), ("ccb", 4), ("clng", 4),
               ("clnb", 4), ("lbin", 8), ("normg", 4), ("sink", 8)]:
    PO[_n] = _o
    _o += _w
NPAR = _o

WSPEC = {
    "w0in": (5, 4096), "w0o": (2, 4096), "gu0": (11, 4096), "dn0": (8, 2816),
    "w1in": (6, 4096), "w1o": (2, 4096), "gu1": (11, 4096), "dn1": (8, 2816),
}


class _Stop(Exception):
    pass


def build_program(nc, n_main_tiles=8, debug=False, stop_at=99):
    S = Sched()
    dr = {}

    def din(name, shape):
        dr[name] = nc.dram_tensor(name, shape, F32, kind="ExternalInput").ap()
        return dr[name]

    def dout(name, shape):
        dr[name] = nc.dram_tensor(name, shape, F32, kind="ExternalOutput").ap()
        return dr[name]

    xp = din("xp", [4096, D]); xs = din("xs", [64, D]); meta = din("meta", [16, D])
    ckd = din("ckd", [128, 256]); ck = din("ck", [128, 128]); cv = din("cv", [128, 128])
    scb = din("scb", [128, 8]); scc = din("scc", [128, 120]); shg = din("shg", [128, 512])
    par_d = din("par", [128, NPAR]); bkv_d = din("bkv", [128, 256]); bi_d = din("bi", [128, 512])
    ident_d = din("ident", [128, 128]); biasT_d = din("biasT", [3 * 128, 2048])
    matt_d = din("matt", [128, 256]); cmask_d = din("cmask", [128, 512])
    wd = {}
    ws = {}
    for n, (g, c) in WSPEC.items():
        wd[n] = din(n, [g * 128, c])
        ws[n] = nc.dram_tensor(n + "_s", [g * 128, c], BF16).ap()
    dgs = nc.dram_tensor("dg_s", [4 * 128, 31 * 128], BF16).ap()
    yp = dout("yp", [4096, D]); ys = dout("ys", [64, D])
    kp = dout("kp", [128, 128]); vp = dout("vp", [128, 128]); cbp = dout("cbp", [2, 512]); ccp = dout("ccp", [30, 512])
    hgp = dout("hgp", [4, 128, 128])
    kso = dout("ks", [128, 128]); vso = dout("vs", [128, 128]); cbs = dout("cbs", [2, 512]); ccs = dout("ccs", [30, 512])
    hgs = dout("hgs", [4, 128, 128])

    st = contextlib.ExitStack()
    cur = [0]

    def alloc(nbytes):
        o = cur[0]
        cur[0] += (nbytes + 63) // 64 * 64
        return o

    GB = 4 * 544 * 4
    offs = {}
    offs["U"] = alloc(5 * GB)
    for n, nb_ in [("h32", 16384), ("hbf", 8192), ("r32", 16384), ("tbf", 8192), ("mixbf", 8192), ("ST", 6144),
                   ("qT", 4096), ("kdT", 2 * 640 * 2), ("vtok", 5 * 128 * 2), ("kebf", 4096), ("kdtok", 4096),
                   ("vtok1", 4096), ("atbf", 2048), ("ucbf", 4 * 544 * 2), ("PA", 10240), ("S32", 2048),
                   ("biasg", 4096), ("biasx", 4096), ("biasx2", 4096), ("ident32", 512), ("identbf", 256), ("ones", 768),
                   ("matt", 1024), ("cmask", 2048), ("par", NPAR * 4), ("dg", 8 * 256), ("stg", 2048),
                   ("kvout", 1024), ("bkv", 1024), ("bi", 2048), ("small", 1024), ("uh0", 32), ("uh1", 480),
                   ("lbw", 256), ("ring", NSLOT * SLOT * 2)]:
        offs[n] = alloc(nb_)
    total = cur[0]
    assert total <= 212000, total
    print('SBUF total', total)
    arena = st.enter_context(nc.sbuf_tensor("arena", [128, total // 4], F32))

    def V(off, shape, dt=F32):
        n = 1
        for s_ in shape:
            n *= s_
        nb_ = n * (4 if dt == F32 else 2)
        a = arena[:, off // 4: off // 4 + nb_ // 4]
        if dt != F32:
            a = a.bitcast(dt)
        if len(shape) == 2:
            return a.rearrange("p (a b) -> p a b", a=shape[0])
        if len(shape) == 3:
            return a.rearrange("p (a b c) -> p a b c", a=shape[0], b=shape[1])
        return a

    G = [V(offs["U"] + i * GB, [4, 544]) for i in range(5)]
    actb = V(offs["U"], [22, 512], BF16)
    h32 = V(offs["h32"], [8, 512]); hbf = V(offs["hbf"], [8, 512], BF16)
    r32 = V(offs["r32"], [8, 512]); xstage = V(offs["r32"], [4, 1024])
    xin = V(offs["U"], [4, 1024])
    tbf = V(offs["tbf"], [8, 512], BF16); mixbf = V(offs["mixbf"], [8, 512], BF16)
    stt_ = [V(offs["ST"] + i * 2048, [512]) for i in range(3)]
    qT = V(offs["qT"], [4, 512], BF16)
    kdT = V(offs["kdT"], [2, 640], BF16); vtok = V(offs["vtok"], [5, 128], BF16)
    kebf = V(offs["kebf"], [4, 512], BF16); kdtok = V(offs["kdtok"], [4, 512], BF16)
    vtok1 = V(offs["vtok1"], [4, 512], BF16); atbf = V(offs["atbf"], [4, 4, 64], BF16)
    ucbf = V(offs["ucbf"], [4, 544], BF16)
    P32 = V(offs["PA"], [4, 256]); Pn32 = V(offs["PA"] + 4096, [4, 256]); PTb = V(offs["PA"] + 8192, [8, 128], BF16)
    Sbf = V(offs["PA"], [9, 4, 128], BF16)
    S32 = V(offs["S32"], [4, 128])
    biasg = V(offs["biasg"], [8, 256], BF16); biasx = V(offs["biasx"], [8, 256], BF16); biasx2 = V(offs["biasx2"], [8, 256], BF16)
    ident32 = V(offs["ident32"], [128]); identbf = V(offs["identbf"], [128], BF16)
    ones = V(offs["ones"], [3, 128], BF16)
    matt = V(offs["matt"], [4, 64]); cmask = V(offs["cmask"], [512])
    par = V(offs["par"], [NPAR])
    dg = V(offs["dg"], [8, 128], BF16)
    stg = V(offs["stg"], [512]); kvout = V(offs["kvout"], [256]); bkv = V(offs["bkv"], [256]); bi = V(offs["bi"], [512])
    small = V(offs["small"], [256])
    uh0 = V(offs["uh0"], [4, 2]); uh1 = V(offs["uh1"], [4, 30]); lbw = V(offs["lbw"], [64])
    ring = [V(offs["ring"] + i * SLOT * 2, [SLOT], BF16) for i in range(NSLOT)]
    PS = [st.enter_context(nc.psum_tensor(f"ps{i}", [128, 512], F32)) for i in range(8)]
    bank_i = [0]

    def nb():
        b = PS[bank_i[0] % 8]
        bank_i[0] += 1
        return b

    def pc(name, i=0, n=1):
        return par[:, PO[name] + i: PO[name] + i + n]

    def aps(*xs_):
        return [x for x in xs_ if x is not None and not isinstance(x, (int, float))]

    def mm(out, lhsT, rhs, start=True, stop=True, sig=None):
        S.add("pe", lambda e, o=out, l=lhsT, r=rhs, s=start, t=stop: e.matmul(o, lhsT=l, rhs=r, start=s, stop=t),
              ins=[lhsT, rhs], outs=[out], signal=(stop if sig is None else sig))

    def tr(out, in_, idn):
        S.add("pe", lambda e, o=out, i=in_, d=idn: e.transpose(o, i, d), ins=[in_, idn], outs=[out])

    def act(out, in_, func, bias=None, scale=None, accum=None):
        kw = {}
        if bias is not None:
            kw["bias"] = bias
        if scale is not None:
            kw["scale"] = scale
        if accum is not None:
            kw["accum_out"] = accum
        S.add("act", lambda e, o=out, i=in_, f=func, k=kw: e.activation(out=o, in_=i, func=f, **k),
              ins=aps(in_, bias, scale), outs=aps(out, accum))

    def ts(out, in0, s1, s2, op0, op1=None, eng="dve"):
        if op1 is None:
            S.add(eng, lambda e, o=out, i=in0, a=s1, p=op0: e.tensor_scalar(out=o, in0=i, scalar1=a, scalar2=None, op0=p),
                  ins=aps(in0, s1), outs=[out])
        else:
            S.add(eng, lambda e, o=out, i=in0, a=s1, b=s2, p=op0, q=op1: e.tensor_scalar(out=o, in0=i, scalar1=a, scalar2=b, op0=p, op1=q),
                  ins=aps(in0, s1, s2), outs=[out])

    def tt(out, in0, in1, op, eng="dve"):
        S.add(eng, lambda e, o=out, a=in0, b=in1, p=op: e.tensor_tensor(out=o, in0=a, in1=b, op=p), ins=[in0, in1], outs=[out])

    def stt(out, in0, scalar, in1, op0, op1):
        S.add("dve", lambda e, o=out, a=in0, s=scalar, b=in1, p=op0, q=op1: e.scalar_tensor_tensor(out=o, in0=a, scalar=s, in1=b, op0=p, op1=q),
              ins=aps(in0, scalar, in1), outs=[out])

    def cp(out, in_, eng="dve"):
        if eng == "act":
            act(out, in_, AF.Identity)
        else:
            S.add(eng, lambda e, o=out, i=in_: e.tensor_copy(out=o, in_=i), ins=[in_], outs=[out])

    def ms(ap, val, eng="dve"):
        S.add(eng, lambda e, a=ap, v=val: e.memset(a, v), outs=[ap])

    def dma(eng, out, in_):
        S.add(eng, lambda e, o=out, i=in_: e.dma_start(out=o, in_=i), ins=[in_], outs=[out], dma=True)

    def red(out, in_, op):
        S.add("dve", lambda e, o=out, i=in_, p=op: e.tensor_reduce(out=o, in_=i, axis=AX.X, op=p), ins=[in_], outs=[out])

    def recip(out, in_):
        S.add("dve", lambda e, o=out, i=in_: e.reciprocal(out=o, in_=i), ins=[in_], outs=[out])

    ms(arena[:, 0:total // 8], 0.0, "dve")
    ms(arena[:, total // 8: total // 4], 0.0, "pool")
    dma("sp", par, par_d); dma("sp", ident32, ident_d); dma("sp", bkv, bkv_d); dma("sp", bi, bi_d)
    dma("sp", matt.rearrange("p a b -> p (a b)"), matt_d); dma("sp", cmask, cmask_d)
    dma("pool", biasg.rearrange("p a b -> p (a b)"), biasT_d[0:128, :])
    dma("pool", biasx.rearrange("p a b -> p (a b)"), biasT_d[128:256, :])
    dma("pool", biasx2.rearrange("p a b -> p (a b)"), biasT_d[256:384, :])
    dma("pool", vtok[:, 0, :], cv)
    cp(identbf, ident32)
    ms(ones[:, 0, :], 1.0 / 1024); ms(ones[:, 1, :], 1.0 / 512); ms(ones[:, 2, :], 1.0 / 128)
    l0 = par[:, PO["lbin"]: PO["lbin"] + 4]; l1 = par[:, PO["lbin"] + 4: PO["lbin"] + 8]
    e0 = lbw[:, 0:4]; e1 = lbw[:, 4:8]; sm = lbw[:, 8:12]; p0 = lbw[:, 12:16]; p1 = lbw[:, 16:20]
    lb = lbw[:, 20:24]; oml = lbw[:, 24:28]; mxl = lbw[:, 28:32]
    tt(mxl, l0, l1, ALU.max)
    tt(e0, l0, mxl, ALU.subtract); tt(e1, l1, mxl, ALU.subtract)
    act(e0, e0, AF.Exp); act(e1, e1, AF.Exp)
    tt(sm, e0, e1, ALU.add); recip(sm, sm)
    tt(p0, e0, sm, ALU.mult); tt(p1, e1, sm, ALU.mult)
    tt(lb, p0, p1, ALU.add); tt(lb, lb, p0, ALU.subtract)
    ts(oml, lb, -1.0, 1.0, ALU.mult, ALU.add)
    cast_seq = []
    for n in ["w0in", "w0o", "gu0", "dn0"]:
        cast_seq += [(n, gi) for gi in range(WSPEC[n][0])]
    cast_seq += [("w1in", gi) for gi in (1, 0, 2, 3, 4, 5)]
    for n in ["w1o", "gu1", "dn1"]:
        cast_seq += [(n, gi) for gi in range(WSPEC[n][0])]
    cast_pos = {c: i for i, c in enumerate(cast_seq)}
    cast_done = [0]

    def ensure_cast(upto):
        while cast_done[0] <= min(upto, len(cast_seq) - 1):
            n, gi = cast_seq[cast_done[0]]
            dma("pool", ws[n][gi * 128:(gi + 1) * 128, :], wd[n][gi * 128:(gi + 1) * 128, :])
            cast_done[0] += 1

    ensure_cast(len(cast_seq))
    for c_ in range(4):
        slot_ = ring[c_ % NSLOT]
        for j_ in range(31):
            ts(slot_[:, j_ * 128:(j_ + 1) * 128], identbf, pc("wC", c_ * 31 + j_), None, ALU.mult)
        dma("pool", dgs[c_ * 128:(c_ + 1) * 128, :], slot_[:, 0:31 * 128])
    ring_i = [0]

    def wl(name, gi):
        g, c = WSPEC[name]
        k = ring_i[0] % NSLOT
        ring_i[0] += 1
        dma("sp", ring[k][:, 0:c], ws[name][gi * 128:(gi + 1) * 128, :])
        return ring[k]

    def ln_block(nch, src, onesrow, gname, bname, dst32, dstbf, sqbuf, T, silu_out=None):
        h = nch // 2
        cp(tbf[:, 0:h, 0:T], src[:, 0:h, 0:T], "dve")
        cp(tbf[:, h:nch, 0:T], src[:, h:nch, 0:T], "dve")
        act(sqbuf[:, 0:h, 0:T], src[:, 0:h, 0:T], AF.Square)
        act(sqbuf[:, h:nch, 0:T], src[:, h:nch, 0:T], AF.Square)
        bS = nb(); bQ = nb()
        for k in range(nch):
            mm(bS[:, 0:T], ones[:, onesrow, :], tbf[:, k, 0:T], k == 0, k == nch - 1)
        for k in range(nch):
            mm(bQ[:, 0:T], ones[:, onesrow, :], sqbuf[:, k, 0:T], k == 0, k == nch - 1)
        mean = stt_[0][:, 0:T]; msq = stt_[1][:, 0:T]; rstd = stt_[2][:, 0:T]
        cp(mean, bS[:, 0:T], "dve")
        act(msq, bS[:, 0:T], AF.Square)
        tt(rstd, bQ[:, 0:T], msq, ALU.subtract)
        ts(rstd, rstd, 0.0, None, ALU.max)
        act(rstd, rstd, AF.Ln, bias=small[:, 200:201])
        act(rstd, rstd, AF.Exp, scale=-0.5)
        for k in range(nch):
            tt(src[:, k, 0:T], src[:, k, 0:T], mean, ALU.subtract)
            tt(src[:, k, 0:T], src[:, k, 0:T], rstd, ALU.mult)
            if silu_out is not None:
                act(silu_out[:, k, 0:T], src[:, k, 0:T], AF.Silu, bias=bname(k), scale=gname(k))
            else:
                act(dstbf[:, k, 0:T], src[:, k, 0:T], AF.Identity, bias=bname(k), scale=gname(k))
        if silu_out is None:
            for k in range(nch):
                act(dst32[:, k, 0:T], src[:, k, 0:T], AF.Identity, bias=bname(k), scale=gname(k))

    ms(small[:, 200:201], LN_EPS)
    ms(small[:, 201:202], RMS_EPS)

    def ffn_block(layer, T):
        gu = f"gu{layer}"; dn = f"dn{layer}"
        for j in range(22):
            if j % 2 == 0:
                w = wl(gu, j // 2)
                wv = w[:, 0:4096].rearrange("p (k n) -> p k n", k=8)
            off = (j % 2) * 256
            bG = nb(); bU = nb()
            for k in range(8):
                mm(bG[:, 0:T], wv[:, k, off:off + 128], hbf[:, k, 0:T], k == 0, k == 7)
            for k in range(8):
                mm(bU[:, 0:T], wv[:, k, off + 128:off + 256], hbf[:, k, 0:T], k == 0, k == 7)
            sil = stt_[j % 2][:, 0:T]
            act(sil, bG[:, 0:T], AF.Silu)
            tt(actb[:, j, 0:T], sil, bU[:, 0:T], ALU.mult)
        for m in range(8):
            w = wl(dn, m)
            wv = w[:, 0:2816].rearrange("p (k n) -> p k n", k=22)
            b = nb()
            for k in range(22):
                mm(b[:, 0:T], wv[:, k, :], actb[:, k, 0:T], k == 0, k == 21)
            stt(r32[:, m, 0:T], h32[:, m, 0:T], ALPHA, b[:, 0:T], ALU.mult, ALU.add)
        lo = PO["ln"] + layer * 32
        ln_block(8, r32, 0, lambda k: par[:, lo + 16 + k: lo + 17 + k], lambda k: par[:, lo + 24 + k: lo + 25 + k],
                 h32, hbf, mixbf, T)

    def wo_block(name, layer, T):
        for m in range(8):
            if m % 4 == 0:
                w = wl(name, m // 4)
                wv = w[:, 0:4096].rearrange("p (k n) -> p k n", k=8)
            off = (m % 4) * 128
            b = nb()
            for k in range(8):
                mm(b[:, 0:T], wv[:, k, off:off + 128], mixbf[:, k, 0:T], k == 0, k == 7)
            stt(r32[:, m, 0:T], h32[:, m, 0:T], ALPHA, b[:, 0:T], ALU.mult, ALU.add)
        lo = PO["ln"] + layer * 32
        ln_block(8, r32, 0, lambda k: par[:, lo + k: lo + 1 + k], lambda k: par[:, lo + 8 + k: lo + 9 + k],
                 h32, hbf, mixbf, T)

    def state_rows_out(src, c0, dst, r0, nrows):
        b = nb()
        for c in range(4):
            tr(b[0:32, c * 128:(c + 1) * 128], src[:, c, c0:c0 + 32], ident32[:, :])
        cp(stg[0:32, :], b[0:32, :], "dve")
        dma("pool", dst, stg[r0:r0 + nrows, :])

    def load_x(kind, ti):
        if kind == "sp":
            dma("pool", xin[0:64, 0, :], xs)
            ms(xin[64:128, 0, :], 0.0)
            dma("pool", xin[112:128, 0, :], meta)
        elif kind == "sample":
            dma("pool", xin[0:64, 0, :], xs)
        elif kind == "prefix":
            ms(xin[0:64, 0, :], 0.0)
            dma("pool", xin[48:64, 0, :], meta)
        else:
            dma("pool", xin[:, :, :], xp[ti * 512:(ti + 1) * 512, :].rearrange("(s p) d -> p s d", p=128))

    def tile_pass(kind, ti, pre_loaded=False, nxt=None):
        T = 512 if kind == "main" else (128 if kind == "sp" else 64)
        SEG = 64 if kind == "sp" else T
        is_s = kind in ("sample", "sp")
        NS = max(1, T // 128)
        Pt = min(T, 128)
        last = (kind == "main" and ti == n_main_tiles - 1)
        if not pre_loaded:
            load_x(kind, ti)
        for c in range(8):
            b = nb()
            for s in range(NS):
                tr(b[:, s * 128:s * 128 + Pt], xin[0:Pt, s, c * 128:(c + 1) * 128], ident32[0:Pt, 0:Pt])
            cp(h32[:, c, 0:T], b[:, 0:T], "dve")
            cp(hbf[:, c, 0:T], b[:, 0:T], "act")
        stage(2)
        bg, u32, cc32 = G[0], G[1], G[2]
        if is_s:
            dma("sp", uh0.rearrange("p a b -> p (a b)"), scb)
            dma("sp", r32[:, 0, 0:256], ckd)
            for g in range(2):
                b = nb()
                tr(b[:, 0:128], r32[:, 0, g * 128:(g + 1) * 128], ident32[:, :])
                cp(kdT[:, g, 0:128], b[:, 0:128], "act")
        elif kind == "prefix":
            ms(uh0, 0.0)
        cp(u32[:, :, 0:2], uh0)
        stage(2.1)
        wg = {}

        def w0(gi):
            if gi not in wg:
                wg.clear()
                wg[gi] = wl("w0in", gi)[:, 0:4096].rearrange("p (k n) -> p k n", k=8)
            return wg[gi]

        for m in range(18):
            wv = w0(m // 4); off = (m % 4) * 128
            b = nb()
            for k in range(8):
                mm(b[:, 0:T], wv[:, k, off:off + 128], hbf[:, k, 0:T], k == 0, k == 7)
            bc = pc("b0", m)
            if m < 4:
                ts(qT[:, m, 0:T], b[:, 0:T], bc, 0.125, ALU.add, ALU.mult)
            elif m < 6:
                act(kdT[:, m - 4, 128:128 + T], b[:, 0:T], AF.Identity, bias=bc)
            elif m < 10:
                act(bg[:, m - 6, 0:T], b[:, 0:T], AF.Identity, bias=bc)
            elif m < 14:
                act(u32[:, m - 10, 2:2 + T], b[:, 0:T], AF.Identity, bias=bc)
            else:
                stt(u32[:, m - 14, 2:2 + T], b[:, 0:T], bc, u32[:, m - 14, 2:2 + T], ALU.add, ALU.mult)
        stage(2.3)
        wv = w0(4)
        for s in range(NS):
            b = nb()
            for k in range(8):
                mm(b[0:Pt, 0:256], hbf[:, k, s * 128:s * 128 + Pt], wv[:, k, 256:512], k == 0, k == 7)
            tt(vtok[0:Pt, 1 + s, :], b[0:Pt, 128:256], bkv[0:Pt, 128:256], ALU.add)
            if last and s == NS - 1 or is_s:
                tt(kvout[0:Pt, :], b[0:Pt, 0:256], bkv[0:Pt, :], ALU.add)
        if kind == "prefix":
            b = nb()
            for k in range(8):
                mm(b[64:128, 0:256], hbf[:, k, 0:64], wv[:, k, 256:512], k == 0, k == 7)
            tt(vtok[64:128, 0, :], b[64:128, 128:256], bkv[64:128, 128:256], ALU.add)
            ms(u32[:, :, 0:50], 0.0)
        if kind == "sp":
            ms(u32[:, :, 2 + 64:2 + 112], 0.0)
        stage(2.5)
        if is_s:
            dma("sp", stg[0:64, 0:128], ck[64:128, :]); dma("sp", stg[0:64, 128:256], cv[64:128, :])
            dma("pool", kso[0:64, :], stg[0:64, 0:128]); dma("pool", vso[0:64, :], stg[0:64, 128:256])
            dma("pool", kso[64:128, :], kvout[0:64, 0:128]); dma("pool", vso[64:128, :], kvout[0:64, 128:256])
        stage(2.7)
        if last:
            dma("pool", kp, kvout[:, 0:128]); dma("pool", vp, kvout[:, 128:256])
        stage(3)
        nq = NS

        def s_phase(j, g):
            bt = biasx if kind in ("prefix", "sp") else (biasx2 if (kind == "main" and ti == 0 and j == 0) else biasg)
            banks = [nb(), nb()]
            for hh in range(4):
                h = 4 * g + hh; c = h // 2; hf = h % 2
                o = banks[hh // 2][0:Pt, (hh % 2) * 256:(hh % 2) * 256 + 256]
                mm(o, qT[64 * hf:64 * hf + 64, c, j * 128:j * 128 + Pt], kdT[64 * hf:64 * hf + 64, g, 128 * j:128 * j + 256], True, False)
                mm(o, identbf[:, 0:Pt], bt[:, h, :], False, True)
            return banks

        def rest_phase(j, g, banks):
            mx = small[0:Pt, 0:4]; mneg = small[0:Pt, 4:8]; ssum = small[0:Pt, 8:12]; esk = small[0:Pt, 12:16]
            for q in range(2):
                red(mx[:, 2 * q:2 * q + 2], banks[q][0:Pt, :].rearrange("p (h k) -> p h k", h=2), ALU.max)
            sk = par[0:Pt, PO["sink"] + 4 * g: PO["sink"] + 4 * g + 4]
            tt(mx, mx, sk, ALU.max)
            ts(mneg, mx, -1.0, None, ALU.mult)
            for hh in range(4):
                act(P32[0:Pt, hh, :], banks[hh // 2][0:Pt, (hh % 2) * 256:(hh % 2) * 256 + 256], AF.Exp,
                    bias=mneg[:, hh:hh + 1], accum=ssum[:, hh:hh + 1])
            tt(esk, sk, mx, ALU.subtract)
            act(esk, esk, AF.Exp)
            tt(ssum, ssum, esk, ALU.add)
            recip(ssum, ssum)
            for hh in range(4):
                ts(Pn32[0:Pt, hh, :], P32[0:Pt, hh, :], ssum[:, hh:hh + 1], None, ALU.mult)
            tb = [nb(), nb()]
            for hh in range(4):
                for kb in range(2):
                    idx = hh * 2 + kb
                    tr(tb[idx // 4][:, (idx % 4) * 128:(idx % 4) * 128 + Pt], Pn32[0:Pt, hh, kb * 128:(kb + 1) * 128], ident32[0:Pt, 0:Pt])
            for q in range(2):
                src = tb[q][:, :].rearrange("p (a b) -> p a b", a=4)[:, :, 0:Pt]
                cp(PTb[:, 4 * q:4 * q + 4, 0:Pt], src, "act" if q else "dve")
            for pr in range(2):
                ob = nb()
                for hf in range(2):
                    hh = pr * 2 + hf
                    for kb in range(2):
                        mm(ob[64 * hf:64 * hf + 64, 0:Pt], vtok[:, j + kb, g * 64:(g + 1) * 64], PTb[:, hh * 2 + kb, 0:Pt], kb == 0, kb == 1)
                cp(mixbf[:, 2 * g + pr, j * 128:j * 128 + Pt], ob[:, 0:Pt], "act")

        units = [(j, g) for j in range(nq) for g in range(2)]
        prev = None
        for u in units:
            bk = s_phase(*u)
            if prev is not None:
                rest_phase(*prev)
            prev = (u[0], u[1], bk)
        rest_phase(*prev)
        if kind == "prefix":
            cp(kdT[:, :, 64:128], kdT[:, :, 128:192])
        elif kind == "sp":
            cp(kdT[:, :, 64:128], kdT[:, :, 192:256])
            cp(vtok[64:128, 0, :], vtok[64:128, 1, :], "dve")
        elif kind == "main":
            cp(kdT[:, :, 0:128], kdT[:, :, 512:640])
            cp(vtok[:, 0, :], vtok[:, 4, :], "dve")
        stage(4)
        for c in range(4):
            ts(cc32[:, c, 0:T], u32[:, c, 0:T], pc("wB", c * 3), None, ALU.mult)
            stt(cc32[:, c, 0:T], u32[:, c, 1:1 + T], pc("wB", c * 3 + 1), cc32[:, c, 0:T], ALU.mult, ALU.add)
            stt(cc32[:, c, 0:T], u32[:, c, 2:2 + T], pc("wB", c * 3 + 2), cc32[:, c, 0:T], ALU.mult, ALU.add)
            tt(mixbf[:, 4 + c, 0:T], bg[:, c, 0:T], cc32[:, c, 0:T], ALU.mult)
        cp(uh0, u32[:, :, T:T + 2])
        if is_s:
            state_rows_out(u32, SEG + 2 - 32, cbs, 30, 2)
        if last:
            state_rows_out(u32, T + 2 - 32, cbp, 30, 2)
        stage(5)
        wo_block("w0o", 0, T)
        stage(6)
        ffn_block(0, T)
        stage(7)
        uc32, sg32, c32, q32, gate32 = G[0], G[1], G[2], G[3], G[4]
        if is_s:
            dma("sp", uh1.rearrange("p a b -> p (a b)"), scc)
            dma("sp", S32.rearrange("p a b -> p (a b)"), shg)
        elif kind == "prefix":
            ms(uh1, 0.0)
            ms(S32, 0.0)
        cp(uc32[:, :, 0:30], uh1)
        order = [4, 5, 6, 7, 0, 1, 2, 3] + list(range(8, 20))
        wg1 = {}

        def w1(gi):
            if gi not in wg1:
                wg1.clear()
                wg1[gi] = wl("w1in", gi)[:, 0:4096].rearrange("p (k n) -> p k n", k=8)
            return wg1[gi]

        for m in order:
            wv = w1(m // 4); off = (m % 4) * 128; c = m % 4
            b = nb()
            for k in range(8):
                mm(b[:, 0:T], wv[:, k, off:off + 128], hbf[:, k, 0:T], k == 0, k == 7)
            bc = pc("b1", m)
            if m < 4:
                stt(uc32[:, c, 30:30 + T], b[:, 0:T], bc, c32[:, c, 0:T], ALU.add, ALU.mult)
            elif m < 8:
                act(c32[:, c, 0:T], b[:, 0:T], AF.Sigmoid, bias=bc)
            elif m < 12:
                act(q32[:, c, 0:T], b[:, 0:T], AF.Identity, bias=bc)
            elif m < 16:
                act(sg32[:, c, 0:T], b[:, 0:T], AF.Sigmoid, bias=bc)
            else:
                act(gate32[:, c, 0:T], b[:, 0:T], AF.Silu, bias=bc)
                ts(gate32[:, c, 0:T], gate32[:, c, 0:T], pc("normg", c), None, ALU.mult)
        wv = w1(5)
        for s in range(NS):
            b = nb()
            for k in range(8):
                mm(b[0:Pt, 0:512], hbf[:, k, s * 128:s * 128 + Pt], wv[:, k, 0:512], k == 0, k == 7)
            tt(vtok1[0:Pt, s, :], b[0:Pt, 0:512], bi[0:Pt, :], ALU.add)
        if kind == "prefix":
            ms(uc32[:, :, 0:78], 0.0)
        if kind == "sp":
            ms(uc32[:, :, 30 + 64:30 + 112], 0.0)
        stage(8)
        cp(ucbf[:, :, 0:30 + T], uc32[:, :, 0:30 + T], "act")
        cp(uh1, uc32[:, :, T:T + 30])
        if is_s:
            state_rows_out(uc32, SEG + 30 - 32, ccs, 2, 30)
        if last:
            state_rows_out(uc32, T + 30 - 32, ccp, 2, 30)
        for c in range(4):
            kslot = ring_i[0] % NSLOT
            ring_i[0] += 1
            dma("sp", ring[kslot][:, 0:31 * 128], dgs[c * 128:(c + 1) * 128, :])
            b = nb()
            for j in range(31):
                mm(b[:, 0:T], ring[kslot][:, j * 128:(j + 1) * 128], ucbf[:, c, j:j + T], j == 0, j == 30)
            act(c32[:, c, 0:T], b[:, 0:T], AF.Identity, bias=pc("ccb", c))
        ln_block(4, c32, 1, lambda k: pc("clng", k), lambda k: pc("clnb", k), None, None, mixbf[:, 4:8, :], T,
                 silu_out=mixbf)
        stage(9)
        lf32 = G[0]; cum32 = G[2]
        for c in range(4):
            ts(sg32[:, c, 0:T], sg32[:, c, 0:T], oml[:, c:c + 1], lb[:, c:c + 1], ALU.mult, ALU.add)
        act(lf32[:, :, 0:T], sg32[:, :, 0:T], AF.Ln)
        ts(sg32[:, :, 0:T], sg32[:, :, 0:T], -1.0, 1.0, ALU.mult, ALU.add)
        for c in range(4):
            S.add("dve", lambda e, o=cum32[:, c, 0:T], d0=cmask[:, 0:T], d1=lf32[:, c, 0:T]:
                  e.tensor_tensor_scan(out=o, data0=d0, data1=d1, initial=0.0, op0=ALU.mult, op1=ALU.add),
                  ins=[cmask[:, 0:T], lf32[:, c, 0:T]], outs=[cum32[:, c, 0:T]])
        NCH = T // 64
        etot = small[:, 32:32 + 4 * 8].rearrange("p (a b) -> p a b", a=4)
        act(etot[:, :, 0:NCH], cum32[:, :, 63:T:64], AF.Exp)
        act(lf32[:, :, 0:T], cum32[:, :, 0:T], AF.Exp)
        tt(qT[:, :, 0:T], q32[:, :, 0:T], lf32[:, :, 0:T], ALU.mult)
        act(lf32[:, :, 0:T], cum32[:, :, 0:T], AF.Exp, scale=-1.0)
        tt(sg32[:, :, 0:T], sg32[:, :, 0:T], lf32[:, :, 0:T], ALU.mult)
        kd32 = G[0]
        for c in range(4):
            tt(kd32[:, c, 0:T].rearrange("p (a b) -> p a b", b=64), sg32[:, c, 0:T].rearrange("p (a b) -> p a b", b=64),
               etot[:, c, 0:NCH].unsqueeze(2).to_broadcast([128, NCH, 64]), ALU.mult)
        cp(kebf[:, :, 0:T], sg32[:, :, 0:T], "act")
        if kind == "prefix":
            ms(kebf[:, :, 0:48], 0.0)
            ms(kd32[:, :, 0:48], 0.0)
        if kind == "sp":
            ms(kebf[:, :, 64:112], 0.0)
            ms(kd32[:, :, 64:112], 0.0)
        for s in range(NS):
            b = nb()
            for c in range(4):
                tr(b[0:Pt, c * 128:(c + 1) * 128], kd32[:, c, s * 128:s * 128 + Pt], ident32[:, :])
            cp(kdtok[0:Pt, s, :], b[0:Pt, :], "act")
        for c in range(4):
            b = nb()
            for ch in range(NCH):
                po = (ch % 2) * 64; s = ch // 2
                mm(b[po:po + 64, s * 64:(s + 1) * 64], kebf[:, c, ch * 64:(ch + 1) * 64], qT[:, c, ch * 64:(ch + 1) * 64])
            if NCH == 1:
                tt(atbf[0:64, c, 0, :], b[0:64, 0:64], matt[0:64, 0, :], ALU.mult)
            elif NCH == 2:
                tt(atbf[:, c, 0, :], b[:, 0:64], matt[:, 0, :], ALU.mult)
            else:
                tt(atbf[:, c, :, :], b[:, 0:256].rearrange("p (a b) -> p a b", a=4), matt[:, :, :], ALU.mult)
        cp(Sbf[:, 0, :, :], S32, "act")
        for ch in range(NCH):
            po = (ch % 2) * 64; s = ch // 2
            b = nb()
            for c in range(4):
                mm(b[:, c * 128:(c + 1) * 128], kdtok[po:po + 64, s, c * 128:(c + 1) * 128], vtok1[po:po + 64, s, c * 128:(c + 1) * 128])
            for c in range(4):
                stt(S32[:, c, :], S32[:, c, :], etot[:, c, ch:ch + 1], b[:, c * 128:(c + 1) * 128], ALU.mult, ALU.add)
            if kind == "sp" and ch == 0:
                dma("pool", hgs.rearrange("h k v -> k h v"), S32)
                ms(S32, 0.0)
            cp(Sbf[:, ch + 1, :, :], S32, "act")
        if kind == "sample":
            dma("pool", hgs.rearrange("h k v -> k h v"), S32)
        if last:
            dma("pool", hgp.rearrange("h k v -> k h v"), S32)
        o32 = G[3]
        for c in range(4):
            b = nb()
            for ch in range(NCH):
                po = (ch % 2) * 64; s = ch // 2
                o = b[:, ch * 64:(ch + 1) * 64]
                mm(o, Sbf[:, ch, c, :], qT[:, c, ch * 64:(ch + 1) * 64], True, False)
                mm(o, vtok1[po:po + 64, s, c * 128:(c + 1) * 128], atbf[po:po + 64, c, s, :], False, True)
            cp(o32[:, c, 0:T], b[:, 0:T], "dve")
        act(tbf[:, 0:4, 0:T], o32[:, :, 0:T], AF.Square)
        for c in range(4):
            b = nb()
            mm(b[:, 0:T], ones[:, 2, :], tbf[:, c, 0:T])
            rs = stt_[c % 3][:, 0:T]
            act(rs, b[:, 0:T], AF.Ln, bias=small[:, 201:202])
            act(rs, rs, AF.Exp, scale=-0.5)
            tt(o32[:, c, 0:T], o32[:, c, 0:T], rs, ALU.mult)
            tt(mixbf[:, 4 + c, 0:T], o32[:, c, 0:T], gate32[:, c, 0:T], ALU.mult)
        stage(10)
        wo_block("w1o", 1, T)
        ffn_block(1, T)
        if nxt is not None:
            load_x(*nxt)
        stage(11)
        if kind != "prefix":
            pass
        if kind != "prefix":
            for s in range(NS):
                for q in range(2):
                    b = nb()
                    for cc in range(4):
                        c = q * 4 + cc
                        tr(b[0:Pt, cc * 128:(cc + 1) * 128], h32[:, c, s * 128:s * 128 + Pt], ident32[:, :])
                    cp(xstage[0:Pt, s, q * 512:(q + 1) * 512], b[0:Pt, :], "act" if q else "dve")
            if is_s:
                dma("pool", ys, xstage[0:64, 0, :])
            else:
                dma("pool", yp[ti * 512:(ti + 1) * 512, :].rearrange("(s p) d -> p s d", p=128), xstage[:, :, :])

    def stage(n):
        if n > stop_at:
            raise _Stop()

    try:
        stage(1)
        seq = [("sp", 0)] + [("main", i) for i in range(n_main_tiles)]
        for i, (kd_, ti_) in enumerate(seq):
            tile_pass(kd_, ti_, pre_loaded=(i > 0), nxt=(seq[i + 1] if i + 1 < len(seq) else None))
    except _Stop:
        pass
    S.finish()
    S.emit(nc)
    st.close()
    return nc

from concourse.bass_utils import run_bass_kernel_spmd

_CACHE = {}


def _tile_w(W, gn):
    K, N = W.shape
    kc = K // 128
    g = N // gn
    return np.ascontiguousarray(W.reshape(kc, 128, g, gn).transpose(2, 1, 0, 3).reshape(g * 128, kc * gn))


def _cols(v, n):
    return np.ascontiguousarray(np.asarray(v, np.float32).reshape(n, 128).T)


def _consts():
    ident = np.eye(128, dtype=np.float32)
    slopes = np.exp2(-8.0 * np.arange(1, 9, dtype=np.float32) / 8).astype(np.float32)
    r = np.arange(128)[:, None]
    c = np.arange(256)[None, :]
    dist = np.abs(128 + r - c).astype(np.float32)
    base_mask = np.zeros((128, 256), bool)
    base_mask[:64, 192:] = True
    base_mask[64:, :64] = True
    m_first = base_mask.copy(); m_first[64:, :240] = True
    m_t0 = base_mask.copy(); m_t0[:, :112] = True
    tabs = []
    for msk in (base_mask, m_first, m_t0):
        t = np.zeros((128, 8, 256), np.float32)
        for h in range(8):
            t[:, h, :] = np.where(msk, np.float32(NEG), -slopes[h] * dist)
        tabs.append(t.reshape(128, 2048))
    biasT = np.concatenate(tabs, 0)
    p = np.arange(128)[:, None] % 64
    t = np.arange(64)[None, :]
    matt = np.tile((p <= t).astype(np.float32)[:, None, :], (1, 4, 1)).reshape(128, 256)
    cm = np.ones((128, 512), np.float32)
    cm[:, ::64] = 0.0
    return ident, biasT, np.ascontiguousarray(matt), cm


def kernel(x_prompt, x_sample, cache_k_a, cache_v_a, state_conv_b, state_conv_c, state_hgrn,
           meta_tokens, ab_w_in, ab_b_in, a_sinks, b_conv_w, ab_w_o, cd_w_in, cd_b_in,
           c_conv_w, c_conv_b, c_ln_g, c_ln_b, d_lower_bounds, d_norm_g, cd_w_o,
           ln1_g, ln1_b, ln2_g, ln2_b, ffn_w_gu, ffn_w_down):
    f = lambda a: np.asarray(a, np.float32)
    x_prompt, x_sample = f(x_prompt), f(x_sample)
    ab_w_in, ab_b_in = f(ab_w_in), f(ab_b_in)
    cd_w_in, cd_b_in = f(cd_w_in), f(cd_b_in)
    q0, k0, v0, bg0, cg0, hb0 = 0, 512, 640, 768, 1280, 1792
    kd_idx = np.concatenate([np.arange(k0, k0 + 64), np.arange(k0, k0 + 64), np.arange(k0 + 64, k0 + 128), np.arange(k0 + 64, k0 + 128)])
    colsA = np.concatenate([np.arange(q0, q0 + 512), kd_idx, np.arange(bg0, bg0 + 512), np.arange(hb0, hb0 + 512),
                            np.arange(cg0, cg0 + 512)])
    colsB = np.concatenate([np.arange(k0, k0 + 128), np.arange(v0, v0 + 128)])
    w0in = _tile_w(ab_w_in[:, np.concatenate([colsA, colsB])], 512)
    b0 = _cols(ab_b_in[colsA], 18)
    bkv = np.ascontiguousarray(np.tile(ab_b_in[colsB][None, :], (128, 1)))
    c1 = np.concatenate([np.arange(0, 2048), np.arange(2560, 3072), np.arange(2048, 2560)])
    w1in = _tile_w(cd_w_in[:, c1], 512)
    b1 = _cols(cd_b_in[c1[:2560]], 20)
    bi = np.ascontiguousarray(np.tile(cd_b_in[2048:2560][None, :], (128, 1)))
    gu_idx = np.stack([np.arange(2816).reshape(22, 128), 2816 + np.arange(2816).reshape(22, 128)], 1).reshape(-1)
    wts = {"w0in": w0in, "w0o": _tile_w(f(ab_w_o), 512), "w1in": w1in, "w1o": _tile_w(f(cd_w_o), 512)}
    for l in range(2):
        wts[f"gu{l}"] = _tile_w(f(ffn_w_gu)[l][:, gu_idx], 512)
        wts[f"dn{l}"] = _tile_w(f(ffn_w_down)[l], 128)
    par = np.zeros((128, NPAR), np.float32)
    par[:, PO["b0"]:PO["b0"] + 18] = b0
    par[:, PO["wB"]:PO["wB"] + 12] = f(b_conv_w).reshape(3, 4, 128).transpose(2, 1, 0).reshape(128, 12)
    for l in range(2):
        lo = PO["ln"] + l * 32
        par[:, lo:lo + 8] = _cols(f(ln1_g)[l], 8); par[:, lo + 8:lo + 16] = _cols(f(ln1_b)[l], 8)
        par[:, lo + 16:lo + 24] = _cols(f(ln2_g)[l], 8); par[:, lo + 24:lo + 32] = _cols(f(ln2_b)[l], 8)
    par[:, PO["b1"]:PO["b1"] + 20] = b1
    par[:, PO["wC"]:PO["wC"] + 124] = f(c_conv_w).reshape(31, 4, 128).transpose(2, 1, 0).reshape(128, 124)
    par[:, PO["ccb"]:PO["ccb"] + 4] = _cols(c_conv_b, 4)
    par[:, PO["clng"]:PO["clng"] + 4] = _cols(c_ln_g, 4)
    par[:, PO["clnb"]:PO["clnb"] + 4] = _cols(c_ln_b, 4)
    par[:, PO["lbin"]:PO["lbin"] + 4] = _cols(f(d_lower_bounds)[0], 4)
    par[:, PO["lbin"] + 4:PO["lbin"] + 8] = _cols(f(d_lower_bounds)[1], 4)
    par[:, PO["normg"]:PO["normg"] + 4] = _cols(d_norm_g, 4)
    par[:, PO["sink"]:PO["sink"] + 8] = np.tile(f(a_sinks)[None, :], (128, 1))
    ident, biasT, matt, cm = _consts()
    common = {"meta": f(meta_tokens), "par": par, "bkv": bkv, "bi": bi, "ident": ident, "biasT": biasT,
              "matt": matt, "cmask": cm}
    common.update(wts)
    in_maps = []
    for c in range(8):
        ckc = f(cache_k_a)[c].reshape(128, 2, 64)
        m = dict(common)
        m["xp"] = np.ascontiguousarray(x_prompt[c % 4])
        m["xs"] = np.ascontiguousarray(x_sample[c])
        m["ckd"] = np.ascontiguousarray(np.concatenate([ckc[:, 0], ckc[:, 0], ckc[:, 1], ckc[:, 1]], 1))
        m["ck"] = np.ascontiguousarray(ckc.reshape(128, 128))
        m["cv"] = np.ascontiguousarray(f(cache_v_a)[c].reshape(128, 128))
        m["scb"] = np.ascontiguousarray(f(state_conv_b)[c].reshape(2, 4, 128).transpose(2, 1, 0).reshape(128, 8))
        m["scc"] = np.ascontiguousarray(f(state_conv_c)[c].reshape(30, 4, 128).transpose(2, 1, 0).reshape(128, 120))
        m["shg"] = np.ascontiguousarray(f(state_hgrn)[c].transpose(1, 0, 2).reshape(128, 512))
        in_maps.append(m)
    if "nc" not in _CACHE:
        nc = bass.Bass("TRN2", target_bir_lowering=False)
        build_program(nc)
        _CACHE["nc"] = nc
    res = run_bass_kernel_spmd(_CACHE["nc"], in_maps, core_ids=list(range(8)))
    R = res.results
    yp = np.stack([R[b]["yp"] for b in range(4)]).astype(np.float32)
    ys = np.stack([R[c]["ys"] for c in range(8)]).astype(np.float32)
    st4 = lambda k, shp: np.stack([R[b][k] for b in range(4)]).reshape(shp).astype(np.float32)
    st8 = lambda k, shp: np.stack([R[c][k] for c in range(8)]).reshape(shp).astype(np.float32)
    return (yp, ys, st4("kp", (4, 128, 2, 64)), st4("vp", (4, 128, 2, 64)), st4("cbp", (4, 2, 512)),
            st4("ccp", (4, 30, 512)), st4("hgp", (4, 4, 128, 128)),
            st8("ks", (8, 128, 2, 64)), st8("vs", (8, 128, 2, 64)), st8("cbs", (8, 2, 512)),
            st8("ccs", (8, 30, 512)), st8("hgs", (8, 4, 128, 128)))
```

```python
import numpy as np
import concourse.bass as bass
import concourse.mybir as mybir

F32 = mybir.dt.float32
BF16 = mybir.dt.bfloat16
AF = mybir.ActivationFunctionType
ALU = mybir.AluOpType
AX = mybir.AxisListType

_DS = {F32: 4, BF16: 2}


def _rng(ap):
    t = ap.tensor
    ds = _DS.get(ap.dtype, 4)
    a = ap.ap
    off = int(ap.offset)
    sp = str(ap.space)
    if "PSUM" in sp.upper():
        return (t.name, 0, 1 << 30, 0, 128)
    if "DRAM" in sp.upper() or "HBM" in sp.upper():
        span = 1
        for st, n in a:
            span += abs(st) * (n - 1)
        return (t.name, off * ds, (off + span) * ds, 0, 1)
    pst, pn = a[0]
    if pst == 0:
        pst = 1 << 40
    if pn > 1 or True:
        tp = 1
        for s in list(t.shape)[1:]:
            tp *= s
        tds = _DS.get(t.dtype, 4)
        tp = tp * tds // ds
    plo = off // tp
    fo = off % tp
    span = 1
    for st, n in a[1:]:
        span += abs(st) * (n - 1)
    return (t.name, fo * ds, (fo + span) * ds, plo, plo + pn)


class Sched:
    ENG = ["pe", "act", "dve", "pool", "sp"]
    R = 12
    RQ = {'sp': 12, 'pool': 2, 'act': 4}

    def __init__(self):
        self.ops = {e: [] for e in self.ENG}
        self.count = {e: 0 for e in self.ENG}
        self.seen = {e: {} for e in self.ENG}
        self.recs = {}
        self.dma_idx = {"sp": 0, "pool": 0, "act": 0}
        self.all_dma = {}

    def _deps(self, ins, outs, eng=None):
        deps = {}

        def add(tk):
            if tk is None:
                return
            s, v = tk
            if deps.get(s, 0) < v:
                deps[s] = v

        for ap in ins:
            n, lo, hi, pl, ph = _rng(ap)
            for r in self.recs.get(n, ()):
                if r[0] < hi and lo < r[1] and r[2] < ph and pl < r[3]:
                    add(r[4])
                    if hi == 1 << 30:
                        for s, v in r[5].items():
                            if s != eng:
                                add((s, v))
        for ap in outs:
            n, lo, hi, pl, ph = _rng(ap)
            for r in self.recs.get(n, ()):
                if r[0] < hi and lo < r[1] and r[2] < ph and pl < r[3]:
                    add(r[4])
                    for s, v in r[5].items():
                        add((s, v))
        return deps

    def _split(self, n, lo, hi, pl, ph):
        lst = self.recs.get(n)
        if not lst:
            return
        out = []
        for r in lst:
            if r[0] < hi and lo < r[1] and pl <= r[2] and r[3] <= ph and (r[0] < lo or hi < r[1]):
                if r[0] < lo:
                    out.append([r[0], lo, r[2], r[3], r[4], dict(r[5])])
                out.append([max(r[0], lo), min(r[1], hi), r[2], r[3], r[4], dict(r[5])])
                if hi < r[1]:
                    out.append([hi, r[1], r[2], r[3], r[4], dict(r[5])])
            else:
                out.append(r)
        self.recs[n] = out

    def _update(self, ins, outs, tk):
        for ap in ins:
            n, lo, hi, pl, ph = _rng(ap)
            self._split(n, lo, hi, pl, ph)
            hit = False
            for r in self.recs.get(n, ()):
                if r[0] < hi and lo < r[1] and r[2] < ph and pl < r[3]:
                    if r[5].get(tk[0], 0) < tk[1]:
                        r[5][tk[0]] = tk[1]
                    hit = True
            if not hit:
                self.recs.setdefault(n, []).append([lo, hi, pl, ph, None, {tk[0]: tk[1]}])
        for ap in outs:
            n, lo, hi, pl, ph = _rng(ap)
            self._split(n, lo, hi, pl, ph)
            lst = self.recs.setdefault(n, [])
            lst[:] = [r for r in lst if not (lo <= r[0] and r[1] <= hi and pl <= r[2] and r[3] <= ph)]
            lst.append([lo, hi, pl, ph, tk, {}])

    def add(self, eng, fn, ins=(), outs=(), dma=False, signal=True):
        deps = self._deps(ins, outs, eng)
        if dma:
            i = self.dma_idx[eng]
            self.dma_idx[eng] = i + 1
            R = self.RQ[eng]
            k = i % R
            sem = f"{eng}_d{k}"
            val = 16 * (i // R + 1)
            if val > 16:
                if deps.get(sem, 0) < val - 16:
                    deps[sem] = val - 16
            tk = (sem, val)
            inc = (sem, 16)
            self.all_dma[sem] = val
        elif not signal:
            tk = (eng, self.count[eng] + 1)
            inc = None
        else:
            self.count[eng] += 1
            tk = (eng, self.count[eng])
            inc = (eng, 1)
        waits = []
        seen = self.seen[eng]
        for s, v in deps.items():
            if s == eng and eng == "pe":
                continue
            if seen.get(s, 0) >= v:
                continue
            seen[s] = v
            waits.append((s, v))
        self.ops[eng].append((waits, fn, inc))
        self._update(ins, outs, tk)
        return tk

    def finish(self):
        waits = []
        for s, v in self.all_dma.items():
            waits.append((s, v))
        for e in ["pe", "act", "dve", "pool"]:
            if self.count[e]:
                waits.append((e, self.count[e]))
        self.ops["sp"].append((waits, None, None))

    def sem_names(self):
        names = ["pe", "act", "dve", "pool"]
        for q in ("sp", "pool", "act"):
            n = min(self.dma_idx[q], self.RQ[q])
            names += [f"{q}_d{k}" for k in range(n)]
        return names

    def emit(self, nc):
        import contextlib
        names = self.sem_names()
        with contextlib.ExitStack() as st:
            sems = {n: st.enter_context(nc.semaphore(n)) for n in names}
            block = st.enter_context(nc.Block())

            def run(e, key):
                for waits, fn, inc in self.ops[key]:
                    for s, v in waits:
                        e.wait_ge(sems[s], v)
                    if fn is None:
                        continue
                    ins = fn(e)
                    if inc is not None:
                        ins.then_inc(sems[inc[0]], inc[1])

            @block.tensor
            def _(e):
                run(e, "pe")

            @block.scalar
            def _(e):
                run(e, "act")

            @block.vector
            def _(e):
                run(e, "dve")

            @block.gpsimd
            def _(e):
                run(e, "pool")

            @block.sync
            def _(e):
                run(e, "sp")

import contextlib

D = 1024
NH = 8
ALPHA = 4 ** 0.25
LN_EPS = 1e-5
RMS_EPS = 1e-6
NEG = -1e30
SLOT = 4096
NSLOT = 4

PO = {}
_o = 0
for _n, _w in [("b0", 18), ("wB", 12), ("ln", 64), ("b1", 20), ("wC", 124), ("ccb", 4), ("clng", 4),
               ("clnb", 4), ("lbin", 8), ("normg", 4), ("sink", 8)]:
    PO[_n] = _o
    _o += _w
NPAR = _o

WSPEC = {
    "w0in": (5, 4096), "w0o": (2, 4096), "gu0": (11, 4096), "dn0": (8, 2816),
    "w1in": (6, 4096), "w1o": (2, 4096), "gu1": (11, 4096), "dn1": (8, 2816),
}


class _Stop(Exception):
    pass


def build_program(nc, n_main_tiles=8, debug=False, stop_at=99):
    S = Sched()
    dr = {}

    def din(name, shape):
        dr[name] = nc.dram_tensor(name, shape, F32, kind="ExternalInput").ap()
        return dr[name]

    def dout(name, shape):
        dr[name] = nc.dram_tensor(name, shape, F32, kind="ExternalOutput").ap()
        return dr[name]

    xp = din("xp", [4096, D]); xs = din("xs", [64, D]); meta = din("meta", [16, D])
    ckd = din("ckd", [128, 256]); ck = din("ck", [128, 128]); cv = din("cv", [128, 128])
    scb = din("scb", [128, 8]); scc = din("scc", [128, 120]); shg = din("shg", [128, 512])
    par_d = din("par", [128, NPAR]); bkv_d = din("bkv", [128, 256]); bi_d = din("bi", [128, 512])
    ident_d = din("ident", [128, 128]); biasT_d = din("biasT", [3 * 128, 2048])
    matt_d = din("matt", [128, 256]); cmask_d = din("cmask", [128, 512])
    wd = {}
    ws = {}
    for n, (g, c) in WSPEC.items():
        wd[n] = din(n, [g * 128, c])
        ws[n] = nc.dram_tensor(n + "_s", [g * 128, c], BF16).ap()
    dgs = nc.dram_tensor("dg_s", [4 * 128, 31 * 128], BF16).ap()
    yp = dout("yp", [4096, D]); ys = dout("ys", [64, D])
    kp = dout("kp", [128, 128]); vp = dout("vp", [128, 128]); cbp = dout("cbp", [2, 512]); ccp = dout("ccp", [30, 512])
    hgp = dout("hgp", [4, 128, 128])
    kso = dout("ks", [128, 128]); vso = dout("vs", [128, 128]); cbs = dout("cbs", [2, 512]); ccs = dout("ccs", [30, 512])
    hgs = dout("hgs", [4, 128, 128])

    st = contextlib.ExitStack()
    cur = [0]

    def alloc(nbytes):
        o = cur[0]
        cur[0] += (nbytes + 63) // 64 * 64
        return o

    GB = 4 * 544 * 4
    offs = {}
    offs["U"] = alloc(5 * GB)
    for n, nb_ in [("h32", 16384), ("hbf", 8192), ("r32", 16384), ("tbf", 8192), ("mixbf", 8192), ("ST", 6144),
                   ("qT", 4096), ("kdT", 2 * 640 * 2), ("vtok", 5 * 128 * 2), ("kebf", 4096), ("kdtok", 4096),
                   ("vtok1", 4096), ("atbf", 2048), ("ucbf", 4 * 544 * 2), ("PA", 10240), ("S32", 2048),
                   ("biasg", 4096), ("biasx", 4096), ("biasx2", 4096), ("ident32", 512), ("identbf", 256), ("ones", 768),
                   ("matt", 1024), ("cmask", 2048), ("par", NPAR * 4), ("dg", 8 * 256), ("stg", 2048),
                   ("kvout", 1024), ("bkv", 1024), ("bi", 2048), ("small", 1024), ("uh0", 32), ("uh1", 480),
                   ("lbw", 256), ("ring", NSLOT * SLOT * 2)]:
        offs[n] = alloc(nb_)
    total = cur[0]
    assert total <= 212000, total
    print('SBUF total', total)
    arena = st.enter_context(nc.sbuf_tensor("arena", [128, total // 4], F32))

    def V(off, shape, dt=F32):
        n = 1
        for s_ in shape:
            n *= s_
        nb_ = n * (4 if dt == F32 else 2)
        a = arena[:, off // 4: off // 4 + nb_ // 4]
        if dt != F32:
            a = a.bitcast(dt)
        if len(shape) == 2:
            return a.rearrange("p (a b) -> p a b", a=shape[0])
        if len(shape) == 3:
            return a.rearrange("p (a b c) -> p a b c", a=shape[0], b=shape[1])
        return a

    G = [V(offs["U"] + i * GB, [4, 544]) for i in range(5)]
    actb = V(offs["U"], [22, 512], BF16)
    h32 = V(offs["h32"], [8, 512]); hbf = V(offs["hbf"], [8, 512], BF16)
    r32 = V(offs["r32"], [8, 512]); xstage = V(offs["r32"], [4, 1024])
    xin = V(offs["U"], [4, 1024])
    tbf = V(offs["tbf"], [8, 512], BF16); mixbf = V(offs["mixbf"], [8, 512], BF16)
    stt_ = [V(offs["ST"] + i * 2048, [512]) for i in range(3)]
    qT = V(offs["qT"], [4, 512], BF16)
    kdT = V(offs["kdT"], [2, 640], BF16); vtok = V(offs["vtok"], [5, 128], BF16)
    kebf = V(offs["kebf"], [4, 512], BF16); kdtok = V(offs["kdtok"], [4, 512], BF16)
    vtok1 = V(offs["vtok1"], [4, 512], BF16); atbf = V(offs["atbf"], [4, 4, 64], BF16)
    ucbf = V(offs["ucbf"], [4, 544], BF16)
    P32 = V(offs["PA"], [4, 256]); Pn32 = V(offs["PA"] + 4096, [4, 256]); PTb = V(offs["PA"] + 8192, [8, 128], BF16)
    Sbf = V(offs["PA"], [9, 4, 128], BF16)
    S32 = V(offs["S32"], [4, 128])
    biasg = V(offs["biasg"], [8, 256], BF16); biasx = V(offs["biasx"], [8, 256], BF16); biasx2 = V(offs["biasx2"], [8, 256], BF16)
    ident32 = V(offs["ident32"], [128]); identbf = V(offs["identbf"], [128], BF16)
    ones = V(offs["ones"], [3, 128], BF16)
    matt = V(offs["matt"], [4, 64]); cmask = V(offs["cmask"], [512])
    par = V(offs["par"], [NPAR])
    dg = V(offs["dg"], [8, 128], BF16)
    stg = V(offs["stg"], [512]); kvout = V(offs["kvout"], [256]); bkv = V(offs["bkv"], [256]); bi = V(offs["bi"], [512])
    small = V(offs["small"], [256])
    uh0 = V(offs["uh0"], [4, 2]); uh1 = V(offs["uh1"], [4, 30]); lbw = V(offs["lbw"], [64])
    ring = [V(offs["ring"] + i * SLOT * 2, [SLOT], BF16) for i in range(NSLOT)]
    PS = [st.enter_context(nc.psum_tensor(f"ps{i}", [128, 512], F32)) for i in range(8)]
    bank_i = [0]

    def nb():
        b = PS[bank_i[0] % 8]
        bank_i[0] += 1
        return b

    def pc(name, i=0, n=1):
        return par[:, PO[name] + i: PO[name] + i + n]

    def aps(*xs_):
        return [x for x in xs_ if x is not None and not isinstance(x, (int, float))]

    def mm(out, lhsT, rhs, start=True, stop=True, sig=None):
        S.add("pe", lambda e, o=out, l=lhsT, r=rhs, s=start, t=stop: e.matmul(o, lhsT=l, rhs=r, start=s, stop=t),
              ins=[lhsT, rhs], outs=[out], signal=(stop if sig is None else sig))

    def tr(out, in_, idn):
        S.add("pe", lambda e, o=out, i=in_, d=idn: e.transpose(o, i, d), ins=[in_, idn], outs=[out])

    def act(out, in_, func, bias=None, scale=None, accum=None):
        kw = {}
        if bias is not None:
            kw["bias"] = bias
        if scale is not None:
            kw["scale"] = scale
        if accum is not None:
            kw["accum_out"] = accum
        S.add("act", lambda e, o=out, i=in_, f=func, k=kw: e.activation(out=o, in_=i, func=f, **k),
              ins=aps(in_, bias, scale), outs=aps(out, accum))

    def ts(out, in0, s1, s2, op0, op1=None, eng="dve"):
        if op1 is None:
            S.add(eng, lambda e, o=out, i=in0, a=s1, p=op0: e.tensor_scalar(out=o, in0=i, scalar1=a, scalar2=None, op0=p),
                  ins=aps(in0, s1), outs=[out])
        else:
            S.add(eng, lambda e, o=out, i=in0, a=s1, b=s2, p=op0, q=op1: e.tensor_scalar(out=o, in0=i, scalar1=a, scalar2=b, op0=p, op1=q),
                  ins=aps(in0, s1, s2), outs=[out])

    def tt(out, in0, in1, op, eng="dve"):
        S.add(eng, lambda e, o=out, a=in0, b=in1, p=op: e.tensor_tensor(out=o, in0=a, in1=b, op=p), ins=[in0, in1], outs=[out])

    def stt(out, in0, scalar, in1, op0, op1):
        S.add("dve", lambda e, o=out, a=in0, s=scalar, b=in1, p=op0, q=op1: e.scalar_tensor_tensor(out=o, in0=a, scalar=s, in1=b, op0=p, op1=q),
              ins=aps(in0, scalar, in1), outs=[out])

    def cp(out, in_, eng="dve"):
        if eng == "act":
            act(out, in_, AF.Identity)
        else:
            S.add(eng, lambda e, o=out, i=in_: e.tensor_copy(out=o, in_=i), ins=[in_], outs=[out])

    def ms(ap, val, eng="dve"):
        S.add(eng, lambda e, a=ap, v=val: e.memset(a, v), outs=[ap])

    def dma(eng, out, in_):
        S.add(eng, lambda e, o=out, i=in_: e.dma_start(out=o, in_=i), ins=[in_], outs=[out], dma=True)

    def red(out, in_, op):
        S.add("dve", lambda e, o=out, i=in_, p=op: e.tensor_reduce(out=o, in_=i, axis=AX.X, op=p), ins=[in_], outs=[out])

    def recip(out, in_):
        S.add("dve", lambda e, o=out, i=in_: e.reciprocal(out=o, in_=i), ins=[in_], outs=[out])

    ms(arena[:, 0:total // 8], 0.0, "dve")
    ms(arena[:, total // 8: total // 4], 0.0, "pool")
    dma("sp", par, par_d); dma("sp", ident32, ident_d); dma("sp", bkv, bkv_d); dma("sp", bi, bi_d)
    dma("sp", matt.rearrange("p a b -> p (a b)"), matt_d); dma("sp", cmask, cmask_d)
    dma("pool", biasg.rearrange("p a b -> p (a b)"), biasT_d[0:128, :])
    dma("pool", biasx.rearrange("p a b -> p (a b)"), biasT_d[128:256, :])
    dma("pool", biasx2.rearrange("p a b -> p (a b)"), biasT_d[256:384, :])
    dma("pool", vtok[:, 0, :], cv)
    cp(identbf, ident32)
    ms(ones[:, 0, :], 1.0 / 1024); ms(ones[:, 1, :], 1.0 / 512); ms(ones[:, 2, :], 1.0 / 128)
    l0 = par[:, PO["lbin"]: PO["lbin"] + 4]; l1 = par[:, PO["lbin"] + 4: PO["lbin"] + 8]
    e0 = lbw[:, 0:4]; e1 = lbw[:, 4:8]; sm = lbw[:, 8:12]; p0 = lbw[:, 12:16]; p1 = lbw[:, 16:20]
    lb = lbw[:, 20:24]; oml = lbw[:, 24:28]; mxl = lbw[:, 28:32]
    tt(mxl, l0, l1, ALU.max)
    tt(e0, l0, mxl, ALU.subtract); tt(e1, l1, mxl, ALU.subtract)
    act(e0, e0, AF.Exp); act(e1, e1, AF.Exp)
    tt(sm, e0, e1, ALU.add); recip(sm, sm)
    tt(p0, e0, sm, ALU.mult); tt(p1, e1, sm, ALU.mult)
    tt(lb, p0, p1, ALU.add); tt(lb, lb, p0, ALU.subtract)
    ts(oml, lb, -1.0, 1.0, ALU.mult, ALU.add)
    cast_seq = []
    for n in ["w0in", "w0o", "gu0", "dn0"]:
        cast_seq += [(n, gi) for gi in range(WSPEC[n][0])]
    cast_seq += [("w1in", gi) for gi in (1, 0, 2, 3, 4, 5)]
    for n in ["w1o", "gu1", "dn1"]:
        cast_seq += [(n, gi) for gi in range(WSPEC[n][0])]
    cast_pos = {c: i for i, c in enumerate(cast_seq)}
    cast_done = [0]

    def ensure_cast(upto):
        while cast_done[0] <= min(upto, len(cast_seq) - 1):
            n, gi = cast_seq[cast_done[0]]
            dma("pool", ws[n][gi * 128:(gi + 1) * 128, :], wd[n][gi * 128:(gi + 1) * 128, :])
            cast_done[0] += 1

    ensure_cast(len(cast_seq))
    for c_ in range(4):
        slot_ = ring[c_ % NSLOT]
        for j_ in range(31):
            act(slot_[:, j_ * 128:(j_ + 1) * 128], identbf, AF.Identity, scale=pc("wC", c_ * 31 + j_))
        dma("pool", dgs[c_ * 128:(c_ + 1) * 128, :], slot_[:, 0:31 * 128])
    ring_i = [0]

    def wl(name, gi):
        g, c = WSPEC[name]
        k = ring_i[0] % NSLOT
        ring_i[0] += 1
        dma("sp", ring[k][:, 0:c], ws[name][gi * 128:(gi + 1) * 128, :])
        return ring[k]

    def ln_block(nch, src, onesrow, gname, bname, dst32, dstbf, sqbuf, T, silu_out=None):
        h = nch // 2
        cp(tbf[:, 0:h, 0:T], src[:, 0:h, 0:T], "dve")
        cp(tbf[:, h:nch, 0:T], src[:, h:nch, 0:T], "dve")
        act(sqbuf[:, 0:h, 0:T], src[:, 0:h, 0:T], AF.Square)
        act(sqbuf[:, h:nch, 0:T], src[:, h:nch, 0:T], AF.Square)
        bS = nb(); bQ = nb()
        for k in range(nch):
            mm(bS[:, 0:T], ones[:, onesrow, :], tbf[:, k, 0:T], k == 0, k == nch - 1)
        for k in range(nch):
            mm(bQ[:, 0:T], ones[:, onesrow, :], sqbuf[:, k, 0:T], k == 0, k == nch - 1)
        mean = stt_[0][:, 0:T]; msq = stt_[1][:, 0:T]; rstd = stt_[2][:, 0:T]
        cp(mean, bS[:, 0:T], "dve")
        act(msq, bS[:, 0:T], AF.Square)
        tt(rstd, bQ[:, 0:T], msq, ALU.subtract)
        ts(rstd, rstd, 0.0, None, ALU.max)
        act(rstd, rstd, AF.Ln, bias=small[:, 200:201])
        act(rstd, rstd, AF.Exp, scale=-0.5)
        for k in range(nch):
            tt(src[:, k, 0:T], src[:, k, 0:T], mean, ALU.subtract)
            tt(src[:, k, 0:T], src[:, k, 0:T], rstd, ALU.mult)
            if silu_out is not None:
                act(silu_out[:, k, 0:T], src[:, k, 0:T], AF.Silu, bias=bname(k), scale=gname(k))
            else:
                act(dstbf[:, k, 0:T], src[:, k, 0:T], AF.Identity, bias=bname(k), scale=gname(k))
        if silu_out is None:
            for k in range(nch):
                act(dst32[:, k, 0:T], src[:, k, 0:T], AF.Identity, bias=bname(k), scale=gname(k))

    ms(small[:, 200:201], LN_EPS)
    ms(small[:, 201:202], RMS_EPS)

    def ffn_block(layer, T):
        gu = f"gu{layer}"; dn = f"dn{layer}"
        for j in range(22):
            if j % 2 == 0:
                w = wl(gu, j // 2)
                wv = w[:, 0:4096].rearrange("p (k n) -> p k n", k=8)
            off = (j % 2) * 256
            bG = nb(); bU = nb()
            for k in range(8):
                mm(bG[:, 0:T], wv[:, k, off:off + 128], hbf[:, k, 0:T], k == 0, k == 7)
            for k in range(8):
                mm(bU[:, 0:T], wv[:, k, off + 128:off + 256], hbf[:, k, 0:T], k == 0, k == 7)
            sil = stt_[j % 2][:, 0:T]
            act(sil, bG[:, 0:T], AF.Silu)
            tt(actb[:, j, 0:T], sil, bU[:, 0:T], ALU.mult)
        for m in range(8):
            w = wl(dn, m)
            wv = w[:, 0:2816].rearrange("p (k n) -> p k n", k=22)
            b = nb()
            for k in range(22):
                mm(b[:, 0:T], wv[:, k, :], actb[:, k, 0:T], k == 0, k == 21)
            stt(r32[:, m, 0:T], h32[:, m, 0:T], ALPHA, b[:, 0:T], ALU.mult, ALU.add)
        lo = PO["ln"] + layer * 32
        ln_block(8, r32, 0, lambda k: par[:, lo + 16 + k: lo + 17 + k], lambda k: par[:, lo + 24 + k: lo + 25 + k],
                 h32, hbf, mixbf, T)

    def wo_block(name, layer, T):
        for m in range(8):
            if m % 4 == 0:
                w = wl(name, m // 4)
                wv = w[:, 0:4096].rearrange("p (k n) -> p k n", k=8)
            off = (m % 4) * 128
            b = nb()
            for k in range(8):
                mm(b[:, 0:T], wv[:, k, off:off + 128], mixbf[:, k, 0:T], k == 0, k == 7)
            stt(r32[:, m, 0:T], h32[:, m, 0:T], ALPHA, b[:, 0:T], ALU.mult, ALU.add)
        lo = PO["ln"] + layer * 32
        ln_block(8, r32, 0, lambda k: par[:, lo + k: lo + 1 + k], lambda k: par[:, lo + 8 + k: lo + 9 + k],
                 h32, hbf, mixbf, T)

    def state_rows_out(src, c0, dst, r0, nrows):
        b = nb()
        for c in range(4):
            tr(b[0:32, c * 128:(c + 1) * 128], src[:, c, c0:c0 + 32], ident32[:, :])
        cp(stg[0:32, :], b[0:32, :], "dve")
        dma("pool", dst, stg[r0:r0 + nrows, :])

    def load_x(kind, ti):
        if kind == "sp":
            dma("pool", xin[0:64, 0, :], xs)
            ms(xin[64:128, 0, :], 0.0)
            dma("pool", xin[112:128, 0, :], meta)
        elif kind == "sample":
            dma("pool", xin[0:64, 0, :], xs)
        elif kind == "prefix":
            ms(xin[0:64, 0, :], 0.0)
            dma("pool", xin[48:64, 0, :], meta)
        else:
            dma("pool", xin[:, :, :], xp[ti * 512:(ti + 1) * 512, :].rearrange("(s p) d -> p s d", p=128))

    def tile_pass(kind, ti, pre_loaded=False, nxt=None):
        T = 512 if kind == "main" else (128 if kind == "sp" else 64)
        SEG = 64 if kind == "sp" else T
        is_s = kind in ("sample", "sp")
        NS = max(1, T // 128)
        Pt = min(T, 128)
        last = (kind == "main" and ti == n_main_tiles - 1)
        if not pre_loaded:
            load_x(kind, ti)
        for c in range(8):
            b = nb()
            for s in range(NS):
                tr(b[:, s * 128:s * 128 + Pt], xin[0:Pt, s, c * 128:(c + 1) * 128], ident32[0:Pt, 0:Pt])
            cp(h32[:, c, 0:T], b[:, 0:T], "dve")
            cp(hbf[:, c, 0:T], b[:, 0:T], "act")
        stage(2)
        bg, u32, cc32 = G[0], G[1], G[2]
        if is_s:
            dma("sp", uh0.rearrange("p a b -> p (a b)"), scb)
            dma("sp", r32[:, 0, 0:256], ckd)
            for g in range(2):
                b = nb()
                tr(b[:, 0:128], r32[:, 0, g * 128:(g + 1) * 128], ident32[:, :])
                cp(kdT[:, g, 0:128], b[:, 0:128], "act")
        elif kind == "prefix":
            ms(uh0, 0.0)
        cp(u32[:, :, 0:2], uh0)
        stage(2.1)
        wg = {}

        def w0(gi):
            if gi not in wg:
                wg.clear()
                wg[gi] = wl("w0in", gi)[:, 0:4096].rearrange("p (k n) -> p k n", k=8)
            return wg[gi]

        for m in range(18):
            wv = w0(m // 4); off = (m % 4) * 128
            b = nb()
            for k in range(8):
                mm(b[:, 0:T], wv[:, k, off:off + 128], hbf[:, k, 0:T], k == 0, k == 7)
            bc = pc("b0", m)
            if m < 4:
                ts(qT[:, m, 0:T], b[:, 0:T], bc, 0.125, ALU.add, ALU.mult)
            elif m < 6:
                act(kdT[:, m - 4, 128:128 + T], b[:, 0:T], AF.Identity, bias=bc)
            elif m < 10:
                act(bg[:, m - 6, 0:T], b[:, 0:T], AF.Identity, bias=bc)
            elif m < 14:
                act(u32[:, m - 10, 2:2 + T], b[:, 0:T], AF.Identity, bias=bc)
            else:
                stt(u32[:, m - 14, 2:2 + T], b[:, 0:T], bc, u32[:, m - 14, 2:2 + T], ALU.add, ALU.mult)
        stage(2.3)
        wv = w0(4)
        for s in range(NS):
            b = nb()
            for k in range(8):
                mm(b[0:Pt, 0:256], hbf[:, k, s * 128:s * 128 + Pt], wv[:, k, 256:512], k == 0, k == 7)
            tt(vtok[0:Pt, 1 + s, :], b[0:Pt, 128:256], bkv[0:Pt, 128:256], ALU.add)
            if last and s == NS - 1 or is_s:
                tt(kvout[0:Pt, :], b[0:Pt, 0:256], bkv[0:Pt, :], ALU.add)
        if kind == "prefix":
            b = nb()
            for k in range(8):
                mm(b[64:128, 0:256], hbf[:, k, 0:64], wv[:, k, 256:512], k == 0, k == 7)
            tt(vtok[64:128, 0, :], b[64:128, 128:256], bkv[64:128, 128:256], ALU.add)
            ms(u32[:, :, 0:50], 0.0)
        if kind == "sp":
            ms(u32[:, :, 2 + 64:2 + 112], 0.0)
        stage(2.5)
        if is_s:
            dma("sp", stg[0:64, 0:128], ck[64:128, :]); dma("sp", stg[0:64, 128:256], cv[64:128, :])
            dma("pool", kso[0:64, :], stg[0:64, 0:128]); dma("pool", vso[0:64, :], stg[0:64, 128:256])
            dma("pool", kso[64:128, :], kvout[0:64, 0:128]); dma("pool", vso[64:128, :], kvout[0:64, 128:256])
        stage(2.7)
        if last:
            dma("pool", kp, kvout[:, 0:128]); dma("pool", vp, kvout[:, 128:256])
        stage(3)
        nq = NS

        def s_phase(j, g):
            bt = biasx if kind in ("prefix", "sp") else (biasx2 if (kind == "main" and ti == 0 and j == 0) else biasg)
            banks = [nb(), nb()]
            for hh in range(4):
                h = 4 * g + hh; c = h // 2; hf = h % 2
                o = banks[hh // 2][0:Pt, (hh % 2) * 256:(hh % 2) * 256 + 256]
                mm(o, qT[64 * hf:64 * hf + 64, c, j * 128:j * 128 + Pt], kdT[64 * hf:64 * hf + 64, g, 128 * j:128 * j + 256], True, False)
                mm(o, identbf[:, 0:Pt], bt[:, h, :], False, True)
            return banks

        def rest_phase(j, g, banks):
            mx = small[0:Pt, 0:4]; mneg = small[0:Pt, 4:8]; ssum = small[0:Pt, 8:12]; esk = small[0:Pt, 12:16]
            for q in range(2):
                red(mx[:, 2 * q:2 * q + 2], banks[q][0:Pt, :].rearrange("p (h k) -> p h k", h=2), ALU.max)
            sk = par[0:Pt, PO["sink"] + 4 * g: PO["sink"] + 4 * g + 4]
            tt(mx, mx, sk, ALU.max)
            ts(mneg, mx, -1.0, None, ALU.mult)
            for hh in range(4):
                act(P32[0:Pt, hh, :], banks[hh // 2][0:Pt, (hh % 2) * 256:(hh % 2) * 256 + 256], AF.Exp,
                    bias=mneg[:, hh:hh + 1], accum=ssum[:, hh:hh + 1])
            tt(esk, sk, mx, ALU.subtract)
            act(esk, esk, AF.Exp)
            tt(ssum, ssum, esk, ALU.add)
            recip(ssum, ssum)
            for hh in range(4):
                ts(Pn32[0:Pt, hh, :], P32[0:Pt, hh, :], ssum[:, hh:hh + 1], None, ALU.mult)
            tb = [nb(), nb()]
            for hh in range(4):
                for kb in range(2):
                    idx = hh * 2 + kb
                    tr(tb[idx // 4][:, (idx % 4) * 128:(idx % 4) * 128 + Pt], Pn32[0:Pt, hh, kb * 128:(kb + 1) * 128], ident32[0:Pt, 0:Pt])
            for q in range(2):
                src = tb[q][:, :].rearrange("p (a b) -> p a b", a=4)[:, :, 0:Pt]
                cp(PTb[:, 4 * q:4 * q + 4, 0:Pt], src, "act" if q else "dve")
            for pr in range(2):
                ob = nb()
                for hf in range(2):
                    hh = pr * 2 + hf
                    for kb in range(2):
                        mm(ob[64 * hf:64 * hf + 64, 0:Pt], vtok[:, j + kb, g * 64:(g + 1) * 64], PTb[:, hh * 2 + kb, 0:Pt], kb == 0, kb == 1)
                cp(mixbf[:, 2 * g + pr, j * 128:j * 128 + Pt], ob[:, 0:Pt], "act")

        units = [(j, g) for j in range(nq) for g in range(2)]
        prev = None
        for u in units:
            bk = s_phase(*u)
            if prev is not None:
                rest_phase(*prev)
            prev = (u[0], u[1], bk)
        rest_phase(*prev)
        if kind == "prefix":
            cp(kdT[:, :, 64:128], kdT[:, :, 128:192])
        elif kind == "sp":
            cp(kdT[:, :, 64:128], kdT[:, :, 192:256])
            cp(vtok[64:128, 0, :], vtok[64:128, 1, :], "dve")
        elif kind == "main":
            cp(kdT[:, :, 0:128], kdT[:, :, 512:640])
            cp(vtok[:, 0, :], vtok[:, 4, :], "dve")
        stage(4)
        for c in range(4):
            ts(cc32[:, c, 0:T], u32[:, c, 0:T], pc("wB", c * 3), None, ALU.mult)
            stt(cc32[:, c, 0:T], u32[:, c, 1:1 + T], pc("wB", c * 3 + 1), cc32[:, c, 0:T], ALU.mult, ALU.add)
            stt(cc32[:, c, 0:T], u32[:, c, 2:2 + T], pc("wB", c * 3 + 2), cc32[:, c, 0:T], ALU.mult, ALU.add)
            tt(mixbf[:, 4 + c, 0:T], bg[:, c, 0:T], cc32[:, c, 0:T], ALU.mult)
        cp(uh0, u32[:, :, T:T + 2])
        if is_s:
            state_rows_out(u32, SEG + 2 - 32, cbs, 30, 2)
        if last:
            state_rows_out(u32, T + 2 - 32, cbp, 30, 2)
        stage(5)
        wo_block("w0o", 0, T)
        stage(6)
        ffn_block(0, T)
        stage(7)
        uc32, sg32, c32, q32, gate32 = G[0], G[1], G[2], G[3], G[4]
        if is_s:
            dma("sp", uh1.rearrange("p a b -> p (a b)"), scc)
            dma("sp", S32.rearrange("p a b -> p (a b)"), shg)
        elif kind == "prefix":
            ms(uh1, 0.0)
            ms(S32, 0.0)
        cp(uc32[:, :, 0:30], uh1)
        order = [4, 5, 6, 7, 0, 1, 2, 3] + list(range(8, 20))
        wg1 = {}

        def w1(gi):
            if gi not in wg1:
                wg1.clear()
                wg1[gi] = wl("w1in", gi)[:, 0:4096].rearrange("p (k n) -> p k n", k=8)
            return wg1[gi]

        for m in order:
            wv = w1(m // 4); off = (m % 4) * 128; c = m % 4
            b = nb()
            for k in range(8):
                mm(b[:, 0:T], wv[:, k, off:off + 128], hbf[:, k, 0:T], k == 0, k == 7)
            bc = pc("b1", m)
            if m < 4:
                stt(uc32[:, c, 30:30 + T], b[:, 0:T], bc, c32[:, c, 0:T], ALU.add, ALU.mult)
            elif m < 8:
                act(c32[:, c, 0:T], b[:, 0:T], AF.Sigmoid, bias=bc)
            elif m < 12:
                act(q32[:, c, 0:T], b[:, 0:T], AF.Identity, bias=bc)
            elif m < 16:
                act(sg32[:, c, 0:T], b[:, 0:T], AF.Sigmoid, bias=bc)
            else:
                act(gate32[:, c, 0:T], b[:, 0:T], AF.Silu, bias=bc)
                ts(gate32[:, c, 0:T], gate32[:, c, 0:T], pc("normg", c), None, ALU.mult)
        wv = w1(5)
        for s in range(NS):
            b = nb()
            for k in range(8):
                mm(b[0:Pt, 0:512], hbf[:, k, s * 128:s * 128 + Pt], wv[:, k, 0:512], k == 0, k == 7)
            tt(vtok1[0:Pt, s, :], b[0:Pt, 0:512], bi[0:Pt, :], ALU.add)
        if kind == "prefix":
            ms(uc32[:, :, 0:78], 0.0)
        if kind == "sp":
            ms(uc32[:, :, 30 + 64:30 + 112], 0.0)
        stage(8)
        cp(ucbf[:, :, 0:30 + T], uc32[:, :, 0:30 + T], "act")
        cp(uh1, uc32[:, :, T:T + 30])
        if is_s:
            state_rows_out(uc32, SEG + 30 - 32, ccs, 2, 30)
        if last:
            state_rows_out(uc32, T + 30 - 32, ccp, 2, 30)
        lf32 = G[0]; cum32 = r32[:, 0:4, :]
        for c in range(4):
            ts(sg32[:, c, 0:T], sg32[:, c, 0:T], oml[:, c:c + 1], lb[:, c:c + 1], ALU.mult, ALU.add)
        act(lf32[:, :, 0:T], sg32[:, :, 0:T], AF.Ln)
        ts(sg32[:, :, 0:T], sg32[:, :, 0:T], -1.0, 1.0, ALU.mult, ALU.add)
        for c in range(4):
            S.add("dve", lambda e, o=cum32[:, c, 0:T], d0=cmask[:, 0:T], d1=lf32[:, c, 0:T]:
                  e.tensor_tensor_scan(out=o, data0=d0, data1=d1, initial=0.0, op0=ALU.mult, op1=ALU.add),
                  ins=[cmask[:, 0:T], lf32[:, c, 0:T]], outs=[cum32[:, c, 0:T]])
        NCH = T // 64
        etot = small[:, 32:32 + 4 * 8].rearrange("p (a b) -> p a b", a=4)
        act(etot[:, :, 0:NCH], cum32[:, :, 63:T:64], AF.Exp)
        act(lf32[:, :, 0:T], cum32[:, :, 0:T], AF.Exp)
        tt(qT[:, :, 0:T], q32[:, :, 0:T], lf32[:, :, 0:T], ALU.mult)
        act(lf32[:, :, 0:T], cum32[:, :, 0:T], AF.Exp, scale=-1.0)
        tt(sg32[:, :, 0:T], sg32[:, :, 0:T], lf32[:, :, 0:T], ALU.mult)
        kd32 = G[0]
        for c in range(4):
            tt(kd32[:, c, 0:T].rearrange("p (a b) -> p a b", b=64), sg32[:, c, 0:T].rearrange("p (a b) -> p a b", b=64),
               etot[:, c, 0:NCH].unsqueeze(2).to_broadcast([128, NCH, 64]), ALU.mult)
        cp(kebf[:, :, 0:T], sg32[:, :, 0:T], "act")
        if kind == "prefix":
            ms(kebf[:, :, 0:48], 0.0)
            ms(kd32[:, :, 0:48], 0.0)
        if kind == "sp":
            ms(kebf[:, :, 64:112], 0.0)
            ms(kd32[:, :, 64:112], 0.0)
        for c in range(4):
            kslot = ring_i[0] % NSLOT
            ring_i[0] += 1
            dma("sp", ring[kslot][:, 0:31 * 128], dgs[c * 128:(c + 1) * 128, :])
            b = nb()
            for j in range(31):
                mm(b[:, 0:T], ring[kslot][:, j * 128:(j + 1) * 128], ucbf[:, c, j:j + T], j == 0, j == 30)
            act(c32[:, c, 0:T], b[:, 0:T], AF.Identity, bias=pc("ccb", c))
        ln_block(4, c32, 1, lambda k: pc("clng", k), lambda k: pc("clnb", k), None, None, mixbf[:, 4:8, :], T,
                 silu_out=mixbf)
        stage(9)
        for s in range(NS):
            b = nb()
            for c in range(4):
                tr(b[0:Pt, c * 128:(c + 1) * 128], kd32[:, c, s * 128:s * 128 + Pt], ident32[:, :])
            cp(kdtok[0:Pt, s, :], b[0:Pt, :], "act")
        for c in range(4):
            b = nb()
            for ch in range(NCH):
                po = (ch % 2) * 64; s = ch // 2
                mm(b[po:po + 64, s * 64:(s + 1) * 64], kebf[:, c, ch * 64:(ch + 1) * 64], qT[:, c, ch * 64:(ch + 1) * 64])
            if NCH == 1:
                tt(atbf[0:64, c, 0, :], b[0:64, 0:64], matt[0:64, 0, :], ALU.mult)
            elif NCH == 2:
                tt(atbf[:, c, 0, :], b[:, 0:64], matt[:, 0, :], ALU.mult)
            else:
                tt(atbf[:, c, :, :], b[:, 0:256].rearrange("p (a b) -> p a b", a=4), matt[:, :, :], ALU.mult)
        cp(Sbf[:, 0, :, :], S32, "act")
        for ch in range(NCH):
            po = (ch % 2) * 64; s = ch // 2
            b = nb()
            for c in range(4):
                mm(b[:, c * 128:(c + 1) * 128], kdtok[po:po + 64, s, c * 128:(c + 1) * 128], vtok1[po:po + 64, s, c * 128:(c + 1) * 128])
            for c in range(4):
                stt(S32[:, c, :], S32[:, c, :], etot[:, c, ch:ch + 1], b[:, c * 128:(c + 1) * 128], ALU.mult, ALU.add)
            if kind == "sp" and ch == 0:
                dma("pool", hgs.rearrange("h k v -> k h v"), S32)
                ms(S32, 0.0)
            cp(Sbf[:, ch + 1, :, :], S32, "act")
        if kind == "sample":
            dma("pool", hgs.rearrange("h k v -> k h v"), S32)
        if last:
            dma("pool", hgp.rearrange("h k v -> k h v"), S32)
        o32 = G[3]
        for c in range(4):
            b = nb()
            for ch in range(NCH):
                po = (ch % 2) * 64; s = ch // 2
                o = b[:, ch * 64:(ch + 1) * 64]
                mm(o, Sbf[:, ch, c, :], qT[:, c, ch * 64:(ch + 1) * 64], True, False)
                mm(o, vtok1[po:po + 64, s, c * 128:(c + 1) * 128], atbf[po:po + 64, c, s, :], False, True)
            cp(o32[:, c, 0:T], b[:, 0:T], "dve")
        act(tbf[:, 0:4, 0:T], o32[:, :, 0:T], AF.Square)
        for c in range(4):
            b = nb()
            mm(b[:, 0:T], ones[:, 2, :], tbf[:, c, 0:T])
            rs = stt_[c % 3][:, 0:T]
            act(rs, b[:, 0:T], AF.Ln, bias=small[:, 201:202])
            act(rs, rs, AF.Exp, scale=-0.5)
            tt(o32[:, c, 0:T], o32[:, c, 0:T], rs, ALU.mult)
            tt(mixbf[:, 4 + c, 0:T], o32[:, c, 0:T], gate32[:, c, 0:T], ALU.mult)
        stage(10)
        wo_block("w1o", 1, T)
        ffn_block(1, T)
        if nxt is not None:
            load_x(*nxt)
        stage(11)
        if kind != "prefix":
            pass
        if kind != "prefix":
            for s in range(NS):
                for q in range(2):
                    b = nb()
                    for cc in range(4):
                        c = q * 4 + cc
                        tr(b[0:Pt, cc * 128:(cc + 1) * 128], h32[:, c, s * 128:s * 128 + Pt], ident32[:, :])
                    cp(xstage[0:Pt, s, q * 512:(q + 1) * 512], b[0:Pt, :], "act" if q else "dve")
            if is_s:
                dma("pool", ys, xstage[0:64, 0, :])
            else:
                dma("pool", yp[ti * 512:(ti + 1) * 512, :].rearrange("(s p) d -> p s d", p=128), xstage[:, :, :])

    def stage(n):
        if n > stop_at:
            raise _Stop()

    try:
        stage(1)
        seq = [("sp", 0)] + [("main", i) for i in range(n_main_tiles)]
        for i, (kd_, ti_) in enumerate(seq):
            tile_pass(kd_, ti_, pre_loaded=(i > 0), nxt=(seq[i + 1] if i + 1 < len(seq) else None))
    except _Stop:
        pass
    S.finish()
    S.emit(nc)
    st.close()
    return nc

from concourse.bass_utils import run_bass_kernel_spmd

_CACHE = {}


def _tile_w(W, gn):
    K, N = W.shape
    kc = K // 128
    g = N // gn
    return np.ascontiguousarray(W.reshape(kc, 128, g, gn).transpose(2, 1, 0, 3).reshape(g * 128, kc * gn))


def _cols(v, n):
    return np.ascontiguousarray(np.asarray(v, np.float32).reshape(n, 128).T)


def _consts():
    ident = np.eye(128, dtype=np.float32)
    slopes = np.exp2(-8.0 * np.arange(1, 9, dtype=np.float32) / 8).astype(np.float32)
    r = np.arange(128)[:, None]
    c = np.arange(256)[None, :]
    dist = np.abs(128 + r - c).astype(np.float32)
    base_mask = np.zeros((128, 256), bool)
    base_mask[:64, 192:] = True
    base_mask[64:, :64] = True
    m_first = base_mask.copy(); m_first[64:, :240] = True
    m_t0 = base_mask.copy(); m_t0[:, :112] = True
    tabs = []
    for msk in (base_mask, m_first, m_t0):
        t = np.zeros((128, 8, 256), np.float32)
        for h in range(8):
            t[:, h, :] = np.where(msk, np.float32(NEG), -slopes[h] * dist)
        tabs.append(t.reshape(128, 2048))
    biasT = np.concatenate(tabs, 0)
    p = np.arange(128)[:, None] % 64
    t = np.arange(64)[None, :]
    matt = np.tile((p <= t).astype(np.float32)[:, None, :], (1, 4, 1)).reshape(128, 256)
    cm = np.ones((128, 512), np.float32)
    cm[:, ::64] = 0.0
    return ident, biasT, np.ascontiguousarray(matt), cm


def kernel(x_prompt, x_sample, cache_k_a, cache_v_a, state_conv_b, state_conv_c, state_hgrn,
           meta_tokens, ab_w_in, ab_b_in, a_sinks, b_conv_w, ab_w_o, cd_w_in, cd_b_in,
           c_conv_w, c_conv_b, c_ln_g, c_ln_b, d_lower_bounds, d_norm_g, cd_w_o,
           ln1_g, ln1_b, ln2_g, ln2_b, ffn_w_gu, ffn_w_down):
    f = lambda a: np.asarray(a, np.float32)
    x_prompt, x_sample = f(x_prompt), f(x_sample)
    ab_w_in, ab_b_in = f(ab_w_in), f(ab_b_in)
    cd_w_in, cd_b_in = f(cd_w_in), f(cd_b_in)
    q0, k0, v0, bg0, cg0, hb0 = 0, 512, 640, 768, 1280, 1792
    kd_idx = np.concatenate([np.arange(k0, k0 + 64), np.arange(k0, k0 + 64), np.arange(k0 + 64, k0 + 128), np.arange(k0 + 64, k0 + 128)])
    colsA = np.concatenate([np.arange(q0, q0 + 512), kd_idx, np.arange(bg0, bg0 + 512), np.arange(hb0, hb0 + 512),
                            np.arange(cg0, cg0 + 512)])
    colsB = np.concatenate([np.arange(k0, k0 + 128), np.arange(v0, v0 + 128)])
    w0in = _tile_w(ab_w_in[:, np.concatenate([colsA, colsB])], 512)
    b0 = _cols(ab_b_in[colsA], 18)
    bkv = np.ascontiguousarray(np.tile(ab_b_in[colsB][None, :], (128, 1)))
    c1 = np.concatenate([np.arange(0, 2048), np.arange(2560, 3072), np.arange(2048, 2560)])
    w1in = _tile_w(cd_w_in[:, c1], 512)
    b1 = _cols(cd_b_in[c1[:2560]], 20)
    bi = np.ascontiguousarray(np.tile(cd_b_in[2048:2560][None, :], (128, 1)))
    gu_idx = np.stack([np.arange(2816).reshape(22, 128), 2816 + np.arange(2816).reshape(22, 128)], 1).reshape(-1)
    wts = {"w0in": w0in, "w0o": _tile_w(f(ab_w_o), 512), "w1in": w1in, "w1o": _tile_w(f(cd_w_o), 512)}
    for l in range(2):
        wts[f"gu{l}"] = _tile_w(f(ffn_w_gu)[l][:, gu_idx], 512)
        wts[f"dn{l}"] = _tile_w(f(ffn_w_down)[l], 128)
    par = np.zeros((128, NPAR), np.float32)
    par[:, PO["b0"]:PO["b0"] + 18] = b0
    par[:, PO["wB"]:PO["wB"] + 12] = f(b_conv_w).reshape(3, 4, 128).transpose(2, 1, 0).reshape(128, 12)
    for l in range(2):
        lo = PO["ln"] + l * 32
        par[:, lo:lo + 8] = _cols(f(ln1_g)[l], 8); par[:, lo + 8:lo + 16] = _cols(f(ln1_b)[l], 8)
        par[:, lo + 16:lo + 24] = _cols(f(ln2_g)[l], 8); par[:, lo + 24:lo + 32] = _cols(f(ln2_b)[l], 8)
    par[:, PO["b1"]:PO["b1"] + 20] = b1
    par[:, PO["wC"]:PO["wC"] + 124] = f(c_conv_w).reshape(31, 4, 128).transpose(2, 1, 0).reshape(128, 124)
    par[:, PO["ccb"]:PO["ccb"] + 4] = _cols(c_conv_b, 4)
    par[:, PO["clng"]:PO["clng"] + 4] = _cols(c_ln_g, 4)
    par[:, PO["clnb"]:PO["clnb"] + 4] = _cols(c_ln_b, 4)
    par[:, PO["lbin"]:PO["lbin"] + 4] = _cols(f(d_lower_bounds)[0], 4)
    par[:, PO["lbin"] + 4:PO["lbin"] + 8] = _cols(f(d_lower_bounds)[1], 4)
    par[:, PO["normg"]:PO["normg"] + 4] = _cols(d_norm_g, 4)
    par[:, PO["sink"]:PO["sink"] + 8] = np.tile(f(a_sinks)[None, :], (128, 1))
    ident, biasT, matt, cm = _consts()
    common = {"meta": f(meta_tokens), "par": par, "bkv": bkv, "bi": bi, "ident": ident, "biasT": biasT,
              "matt": matt, "cmask": cm}
    common.update(wts)
    in_maps = []
    for c in range(8):
        ckc = f(cache_k_a)[c].reshape(128, 2, 64)
        m = dict(common)
        m["xp"] = np.ascontiguousarray(x_prompt[c % 4])
        m["xs"] = np.ascontiguousarray(x_sample[c])
        m["ckd"] = np.ascontiguousarray(np.concatenate([ckc[:, 0], ckc[:, 0], ckc[:, 1], ckc[:, 1]], 1))
        m["ck"] = np.ascontiguousarray(ckc.reshape(128, 128))
        m["cv"] = np.ascontiguousarray(f(cache_v_a)[c].reshape(128, 128))
        m["scb"] = np.ascontiguousarray(f(state_conv_b)[c].reshape(2, 4, 128).transpose(2, 1, 0).reshape(128, 8))
        m["scc"] = np.ascontiguousarray(f(state_conv_c)[c].reshape(30, 4, 128).transpose(2, 1, 0).reshape(128, 120))
        m["shg"] = np.ascontiguousarray(f(state_hgrn)[c].transpose(1, 0, 2).reshape(128, 512))
        in_maps.append(m)
    if "nc" not in _CACHE:
        nc = bass.Bass("TRN2", target_bir_lowering=False)
        build_program(nc)
        _CACHE["nc"] = nc
    res = run_bass_kernel_spmd(_CACHE["nc"], in_maps, core_ids=list(range(8)))
    R = res.results
    yp = np.stack([R[b]["yp"] for b in range(4)]).astype(np.float32)
    ys = np.stack([R[c]["ys"] for c in range(8)]).astype(np.float32)
    st4 = lambda k, shp: np.stack([R[b][k] for b in range(4)]).reshape(shp).astype(np.float32)
    st8 = lambda k, shp: np.stack([R[c][k] for c in range(8)]).reshape(shp).astype(np.float32)
    return (yp, ys, st4("kp", (4, 128, 2, 64)), st4("vp", (4, 128, 2, 64)), st4("cbp", (4, 2, 512)),
            st4("ccp", (4, 30, 512)), st4("hgp", (4, 4, 128, 128)),
            st8("ks", (8, 128, 2, 64)), st8("vs", (8, 128, 2, 64)), st8("cbs", (8, 2, 512)),
            st8("ccs", (8, 30, 512)), st8("hgs", (8, 4, 128, 128)))
```
